# Optimizing a Trainium2 kernel written in Bass

```python
import jax, jax.numpy as jnp
from jax import lax
import numpy as np

D_MODEL = 4096
BATCH = 2
SEQ = 8192
DEPTH = 1

MEM_LEN = 256
MIX_WIDTH = D_MODEL
RET_WIDTH = MIX_WIDTH // 2
GDN_WIDTH = MIX_WIDTH - RET_WIDTH
RET_HEAD_DIM = 256
RET_HEADS = RET_WIDTH // RET_HEAD_DIM
RET_CHUNK = 128
GDN_HEAD_DIM = 128
GDN_HEADS = GDN_WIDTH // GDN_HEAD_DIM
GDN_CHUNK = 64
CONV_WIDTH = 4
XATTN_HEADS = 4
XATTN_HEAD_DIM = D_MODEL // XATTN_HEADS
D_FF = 4 * D_MODEL
ROPE_THETA = 10000.0
NORM_EPS = 1e-6
IN_SPLITS = (RET_WIDTH, RET_WIDTH, RET_WIDTH, RET_WIDTH,
             GDN_WIDTH, GDN_WIDTH, GDN_WIDTH, GDN_WIDTH, GDN_HEADS, GDN_HEADS)
IN_WIDTH = 4 * RET_WIDTH + 4 * GDN_WIDTH + 2 * GDN_HEADS

kernel_name = "hymba_retention_gdn_xattn_layer"

F32 = jnp.float32


def rmsnorm(x, gain):
    xf = x.astype(F32)
    y = xf * lax.rsqrt(jnp.mean(xf * xf, axis=-1, keepdims=True) + NORM_EPS)
    return (y * gain.astype(F32)).astype(x.dtype)


def l2norm(x):
    xf = x.astype(F32)
    return xf * lax.rsqrt(jnp.sum(xf * xf, axis=-1, keepdims=True) + NORM_EPS)


def rotary(x, positions):
    half = x.shape[-1] // 2
    inv_freq = ROPE_THETA ** (-jnp.arange(half, dtype=F32) / half)
    ang = positions.astype(F32)[..., None] * inv_freq
    cos = jnp.cos(ang)[:, :, None, :]
    sin = jnp.sin(ang)[:, :, None, :]
    xf = x.astype(F32)
    x1, x2 = xf[..., :half], xf[..., half:]
    return jnp.concatenate([x1 * cos - x2 * sin, x2 * cos + x1 * sin], axis=-1)


def causal_short_conv(x, w):
    k_width, ch = w.shape
    return lax.conv_general_dilated(
        x, w[:, None, :].astype(x.dtype), window_strides=(1,),
        padding=[(k_width - 1, 0)], dimension_numbers=('NWC', 'WIO', 'NWC'),
        feature_group_count=ch)


def multiscale_retention(q, k, v, positions):
    B, S, H, Dh = q.shape
    C = RET_CHUNK
    N = S // C
    q = rotary(q, positions)
    k = rotary(k, positions) * (Dh ** -0.5)
    v = v.astype(F32)
    log_gamma = jnp.log1p(-jnp.exp2(-5.0 - jnp.arange(H, dtype=F32)))
    idx = jnp.arange(C, dtype=F32)
    rel = idx[:, None] - idx[None, :]
    causal = rel >= 0
    decay_intra = jnp.where(causal[None],
                            jnp.exp(jnp.maximum(rel, 0.0)[None] * log_gamma[:, None, None]),
                            0.0)
    decay_q = jnp.exp((idx + 1.0)[None, :] * log_gamma[:, None])
    decay_k = jnp.exp((C - 1.0 - idx)[None, :] * log_gamma[:, None])
    decay_chunk = jnp.exp(C * log_gamma)

    def to_chunks(t):
        return t.reshape(B, N, C, H, Dh).transpose(1, 0, 3, 2, 4)

    def step(state, xs):
        qc, kc, vc = xs
        scores = jnp.einsum('bhid,bhjd->bhij', qc, kc) * decay_intra
        out = (jnp.einsum('bhij,bhjd->bhid', scores, vc)
               + jnp.einsum('bhid,bhde->bhie', qc, state) * decay_q[..., None])
        state = (decay_chunk[:, None, None] * state
                 + jnp.einsum('bhjd,bhje->bhde', kc * decay_k[..., None], vc))
        return state, out

    state0 = jnp.zeros((B, H, Dh, Dh), F32)
    _, out = lax.scan(step, state0, (to_chunks(q), to_chunks(k), to_chunks(v)))
    return out.transpose(1, 0, 3, 2, 4).reshape(B, S, H, Dh)


def gated_delta_rule(q, k, v, log_decay, beta):
    B, S, H, D = q.shape
    C = GDN_CHUNK
    N = S // C
    q = l2norm(q) * (D ** -0.5)
    k = l2norm(k)
    v = v.astype(F32)

    def to_chunks(t):
        return t.reshape(B, N, C, H, D).transpose(0, 3, 1, 2, 4)

    def to_chunks_h(t):
        return t.reshape(B, N, C, H).transpose(0, 3, 1, 2)

    qc, kc, vc = to_chunks(q), to_chunks(k), to_chunks(v)
    g = jnp.cumsum(to_chunks_h(log_decay.astype(F32)), axis=-1)
    bc = to_chunks_h(beta.astype(F32))
    idx = jnp.arange(C)
    incl = idx[:, None] >= idx[None, :]
    strict = idx[:, None] > idx[None, :]
    diff = g[..., :, None] - g[..., None, :]
    decay_incl = jnp.where(incl, jnp.exp(jnp.where(incl, diff, 0.0)), 0.0)
    k_beta = kc * bc[..., None]
    v_beta = vc * bc[..., None]
    a_strict = jnp.where(strict, jnp.einsum('bhnid,bhnjd->bhnij', k_beta, kc) * decay_incl, 0.0)
    lhs = jnp.eye(C, dtype=F32) + a_strict
    rhs = jnp.concatenate([v_beta, k_beta * jnp.exp(g)[..., None]], axis=-1)
    sol = lax.linalg.triangular_solve(lhs, rhs, left_side=True, lower=True, unit_diagonal=True)
    u, w = sol[..., :D], sol[..., D:]
    qk = jnp.einsum('bhnid,bhnjd->bhnij', qc, kc) * decay_incl
    g_last = g[..., -1]
    q_dec = qc * jnp.exp(g)[..., None]
    k_dec = kc * jnp.exp(g_last[..., None] - g)[..., None]

    def step(state, xs):
        qk_n, q_dec_n, k_dec_n, u_n, w_n, gl_n = xs
        v_new = u_n - jnp.einsum('bhcd,bhde->bhce', w_n, state)
        out = (jnp.einsum('bhcd,bhde->bhce', q_dec_n, state)
               + jnp.einsum('bhij,bhje->bhie', qk_n, v_new))
        state = (jnp.exp(gl_n)[..., None, None] * state
                 + jnp.einsum('bhcd,bhce->bhde', k_dec_n, v_new))
        return state, out

    xs = tuple(jnp.moveaxis(t, 2, 0) for t in (qk, q_dec, k_dec, u, w, g_last))
    state0 = jnp.zeros((B, H, D, D), F32)
    _, out = lax.scan(step, state0, xs)
    return out.transpose(1, 0, 3, 2, 4).reshape(B, S, H, D)


def setup_inputs(seed: int = 0) -> dict:
    key = jax.random.key(seed)
    ks = jax.random.split(key, 24)
    L = DEPTH

    def dense(k, fan_in, fan_out):
        return jax.random.normal(k, (L, fan_in, fan_out), F32) * (fan_in ** -0.5)

    def gain(k, shape):
        return 1.0 + 0.02 * jax.random.normal(k, shape, F32)

    x = jax.random.normal(ks[0], (BATCH, SEQ, D_MODEL), F32)
    mem = jax.random.normal(ks[1], (BATCH, MEM_LEN, D_MODEL), F32)
    offset = jax.random.randint(ks[2], (BATCH, 1), 0, 4096, dtype=jnp.int32)
    positions = offset + jnp.arange(SEQ, dtype=jnp.int32)[None, :]
    dt = jnp.exp(jax.random.uniform(ks[3], (L, GDN_HEADS), F32, np.log(1e-3), np.log(1e-1)))
    return {
        "x": x,
        "mem": mem,
        "positions": positions,
        "mix_norm": gain(ks[4], (L, D_MODEL)),
        "w_in": dense(ks[5], D_MODEL, IN_WIDTH),
        "ret_norm": gain(ks[6], (L, RET_HEADS, RET_HEAD_DIM)),
        "gdn_conv": 0.5 * jax.random.normal(ks[7], (L, CONV_WIDTH, 3 * GDN_WIDTH), F32),
        "gdn_a_log": jnp.log(jax.random.uniform(ks[8], (L, GDN_HEADS), F32, 1.0, 16.0)),
        "gdn_dt_bias": dt + jnp.log(-jnp.expm1(-dt)),
        "gdn_norm": gain(ks[9], (L, GDN_HEAD_DIM)),
        "w_mix_out": dense(ks[10], MIX_WIDTH, D_MODEL),
        "xattn_norm": gain(ks[11], (L, D_MODEL)),
        "mem_norm": gain(ks[12], (L, D_MODEL)),
        "w_xq": dense(ks[13], D_MODEL, D_MODEL),
        "w_xk": dense(ks[14], D_MODEL, D_MODEL),
        "w_xv": dense(ks[15], D_MODEL, D_MODEL),
        "w_xo": dense(ks[16], D_MODEL, D_MODEL),
        "mlp_norm": gain(ks[17], (L, D_MODEL)),
        "w_up": dense(ks[18], D_MODEL, D_FF),
        "w_down": dense(ks[19], D_FF, D_MODEL),
        "final_norm": gain(ks[20], (D_MODEL,)),
    }


def reference(x, mem, positions, mix_norm, w_in, ret_norm, gdn_conv, gdn_a_log, gdn_dt_bias,
              gdn_norm, w_mix_out, xattn_norm, mem_norm, w_xq, w_xk, w_xv, w_xo,
              mlp_norm, w_up, w_down, final_norm):
    B, S, _ = x.shape
    M = mem.shape[1]
    split_points = np.cumsum(IN_SPLITS)[:-1]
    h = x
    for l in range(DEPTH):
        hn = rmsnorm(h, mix_norm[l])
        proj = hn @ w_in[l]
        rq, rk, rv, rg, gq, gk, gv, gz, ga, gb = jnp.split(proj, split_points, axis=-1)

        ret = multiscale_retention(rq.reshape(B, S, RET_HEADS, RET_HEAD_DIM),
                                   rk.reshape(B, S, RET_HEADS, RET_HEAD_DIM),
                                   rv.reshape(B, S, RET_HEADS, RET_HEAD_DIM), positions)
        ret = rmsnorm(ret, ret_norm[l]).reshape(B, S, RET_WIDTH)
        ret = jax.nn.silu(rg.astype(F32)) * ret

        qkv = jax.nn.silu(causal_short_conv(jnp.concatenate([gq, gk, gv], axis=-1), gdn_conv[l]))
        cq, ck, cv = jnp.split(qkv, 3, axis=-1)
        log_decay = -jnp.exp(gdn_a_log[l].astype(F32)) * jax.nn.softplus(
            ga.astype(F32) + gdn_dt_bias[l].astype(F32))
        beta = jax.nn.sigmoid(gb.astype(F32))
        gdn = gated_delta_rule(cq.reshape(B, S, GDN_HEADS, GDN_HEAD_DIM),
                               ck.reshape(B, S, GDN_HEADS, GDN_HEAD_DIM),
                               cv.reshape(B, S, GDN_HEADS, GDN_HEAD_DIM), log_decay, beta)
        gdn = rmsnorm(gdn, gdn_norm[l]) * jax.nn.silu(
            gz.reshape(B, S, GDN_HEADS, GDN_HEAD_DIM).astype(F32))

        mixed = jnp.concatenate([ret, gdn.reshape(B, S, GDN_WIDTH)], axis=-1).astype(h.dtype)
        h = h + mixed @ w_mix_out[l]

        hn = rmsnorm(h, xattn_norm[l])
        mn = rmsnorm(mem, mem_norm[l])
        q = (hn @ w_xq[l]).reshape(B, S, XATTN_HEADS, XATTN_HEAD_DIM)
        k = (mn @ w_xk[l]).reshape(B, M, XATTN_HEADS, XATTN_HEAD_DIM)
        v = (mn @ w_xv[l]).reshape(B, M, XATTN_HEADS, XATTN_HEAD_DIM)
        scores = jnp.einsum('bshd,bmhd->bhsm', q, k).astype(F32) * (XATTN_HEAD_DIM ** -0.5)
        probs = jax.nn.softmax(scores, axis=-1).astype(v.dtype)
        att = jnp.einsum('bhsm,bmhd->bshd', probs, v).reshape(B, S, D_MODEL)
        h = h + att @ w_xo[l]

        hn = rmsnorm(h, mlp_norm[l])
        h = h + jnp.square(jax.nn.relu(hn @ w_up[l])) @ w_down[l]
    return rmsnorm(h, final_norm)
```

```python
import numpy as np
from contextlib import ExitStack
import concourse.bass as bass
import concourse.mybir as mybir
from concourse.bass_utils import run_bass_kernel_spmd

F32 = mybir.dt.float32
BF16 = mybir.dt.bfloat16
I32 = mybir.dt.int32
AF = mybir.ActivationFunctionType
ALU = mybir.AluOpType
AX = mybir.AxisListType

D = 4096
NK = D // 128
MEM = 256
DFF = 4 * D
EPS = 1e-6
SEM_LIMIT = 30000

ENGS = ("pe", "act", "dve", "pool", "sp")


class T:
    __slots__ = ("name", "t", "lw", "rd", "dsem", "dcnt", "excl")

    def __init__(self, name, t, excl=False):
        self.name, self.t, self.lw, self.rd, self.dsem, self.dcnt, self.excl = name, t, None, {}, None, 0, excl

    def __getitem__(self, idx):
        return self.t[idx]


class Prog:
    def __init__(self, nc, semstack):
        self.nc, self.semstack, self.stack = nc, semstack, semstack
        self.q = {e: [] for e in ENGS}
        self.seen = {e: {} for e in ENGS}
        self.sems, self.cnt, self.cur, self.epoch = {}, {}, {}, {}
        for e in ENGS:
            self.epoch[e] = 0
            self._newkey(e)
        self.ndsem = 0
        self.ntile = 0
        self.dtiles = []

    def _newkey(self, e):
        key = f"{e}#{self.epoch[e]}"
        self.epoch[e] += 1
        self.sems[key] = self.semstack.enter_context(self.nc.semaphore("s_" + key.replace("#", "_")))
        self.cnt[key] = 0
        self.cur[e] = key

    def sb(self, name, shape, dt):
        self.ntile += 1
        return T(name, self.stack.enter_context(self.nc.sbuf_tensor(f"{name}_{self.ntile}", list(shape), dt)))

    def ps(self, name, shape, dt=F32):
        self.ntile += 1
        return T(name, self.stack.enter_context(self.nc.psum_tensor(f"{name}_{self.ntile}", list(shape), dt)), excl=True)

    def view(self, name, ap):
        return T(name, ap)

    def _dsem(self, tile):
        if tile.dsem is None:
            self.ndsem += 1
            key = f"d{self.ndsem}"
            self.sems[key] = self.semstack.enter_context(self.nc.semaphore("s_" + key))
            tile.dsem = key
            self.dtiles.append(tile)
        return tile.dsem

    def _waits(self, e, reads, writes):
        need = {}
        for t in reads:
            if t.lw is not None:
                k, v = t.lw
                if need.get(k, 0) < v:
                    need[k] = v
        for t in writes:
            if t.lw is not None:
                k, v = t.lw
                if need.get(k, 0) < v:
                    need[k] = v
            for k, v in t.rd.items():
                if need.get(k, 0) < v:
                    need[k] = v
        seen = self.seen[e]
        for k, v in need.items():
            if e == "pe" and k.startswith("pe#"):
                continue
            if seen.get(k, 0) >= v:
                continue
            seen[k] = v
            sem = self.sems[k]
            self.q[e].append(lambda eng, sem=sem, v=v: eng.wait_ge(sem, v))

    def op(self, e, fn, reads=(), writes=()):
        xr = [t for t in reads if t.excl and t not in writes]
        if xr:
            writes = list(writes) + xr
        self._waits(e, reads, writes)
        if self.cnt[self.cur[e]] >= SEM_LIMIT:
            self._newkey(e)
        key = self.cur[e]
        self.cnt[key] += 1
        n = self.cnt[key]
        sem = self.sems[key]
        self.q[e].append(lambda eng, fn=fn, sem=sem: fn(eng).then_inc(sem, 1))
        for t in writes:
            t.lw = (key, n)
            t.rd = {}
        for t in reads:
            if t not in writes:
                if t.rd.get(key, 0) < n:
                    t.rd[key] = n

    def dma(self, e, out, in_, reads=(), writes=(), semtile=None):
        self._waits(e, reads, writes)
        st = semtile if semtile is not None else (writes[0] if writes else reads[0])
        key = self._dsem(st)
        st.dcnt += 16
        v = st.dcnt
        sem = self.sems[key]
        self.q[e].append(lambda eng, out=out, in_=in_, sem=sem: eng.dma_start(out=out, in_=in_).then_inc(sem, 16))
        for t in writes:
            t.lw = (key, v)
            t.rd = {}
        for t in reads:
            if t not in writes:
                if t.rd.get(key, 0) < v:
                    t.rd[key] = v

    def finish(self):
        for t in self.dtiles:
            sem, v = self.sems[t.dsem], t.dcnt
            self.q["sp"].append(lambda eng, sem=sem, v=v: eng.wait_ge(sem, v))

    def barrier(self):
        final = {k: v for k, v in self.cnt.items() if v > 0}
        for t in self.dtiles:
            if t.dcnt > 0:
                final[t.dsem] = t.dcnt
        for e in ENGS:
            seen = self.seen[e]
            for k, v in final.items():
                if seen.get(k, 0) >= v:
                    continue
                seen[k] = v
                sem = self.sems[k]
                self.q[e].append(lambda eng, sem=sem, v=v: eng.wait_ge(sem, v))

    def run(self):
        q = self.q
        self.q = {e: [] for e in ENGS}
        self._emit(q)

    def _emit(self, q):
        self = type("Q", (), {"q": q, "nc": self.nc})()
        with self.nc.Block() as block:
            @block.tensor
            def _(eng):
                for f in self.q["pe"]:
                    f(eng)

            @block.scalar
            def _(eng):
                for f in self.q["act"]:
                    f(eng)

            @block.vector
            def _(eng):
                for f in self.q["dve"]:
                    f(eng)

            @block.gpsimd
            def _(eng):
                for f in self.q["pool"]:
                    f(eng)

            @block.sync
            def _(eng):
                for f in self.q["sp"]:
                    f(eng)


def mm(p, ot, oap, lt, lap, rt, rap, start=True, stop=True):
    p.op("pe", lambda e: e.matmul(oap, lhsT=lap, rhs=rap, start=start, stop=stop), reads=[lt, rt], writes=[ot])


def tr(p, ot, oap, it, iap, ident):
    p.op("pe", lambda e: e.transpose(oap, iap, ident[:]), reads=[it, ident], writes=[ot])


def ts(p, eng, ot, oap, it, iap, s1, s2, op0, op1=None, extra=()):
    if op1 is None:
        p.op(eng, lambda e: e.tensor_scalar(out=oap, in0=iap, scalar1=s1, scalar2=None, op0=op0), reads=[it, *extra], writes=[ot])
    else:
        p.op(eng, lambda e: e.tensor_scalar(out=oap, in0=iap, scalar1=s1, scalar2=s2, op0=op0, op1=op1), reads=[it, *extra], writes=[ot])


def tt(p, eng, ot, oap, at, aap, bt, bap, op):
    p.op(eng, lambda e: e.tensor_tensor(out=oap, in0=aap, in1=bap, op=op), reads=[at, bt], writes=[ot])


def stt(p, eng, ot, oap, at, aap, scalar, bt, bap, op0, op1, extra=()):
    p.op(eng, lambda e: e.scalar_tensor_tensor(out=oap, in0=aap, scalar=scalar, in1=bap, op0=op0, op1=op1),
         reads=[at, bt, *extra], writes=[ot])


def act(p, ot, oap, it, iap, func, bias=None, scale=None, accum=None, accum_t=None, extra=()):
    kw = {}
    if bias is not None:
        kw["bias"] = bias
    if scale is not None:
        kw["scale"] = scale
    if accum is not None:
        kw["accum_out"] = accum
    w = [ot] + ([accum_t] if accum_t is not None else [])
    p.op("act", lambda e: e.activation(out=oap, in_=iap, func=func, **kw), reads=[it, *extra], writes=w)


def cp(p, eng, ot, oap, it, iap):
    if eng == "act":
        p.op("act", lambda e: e.copy(out=oap, in_=iap), reads=[it], writes=[ot])
    else:
        p.op(eng, lambda e: e.tensor_copy(out=oap, in_=iap), reads=[it], writes=[ot])


def make_ident(p, dt, name="ident"):
    ident = p.sb(name, [128, 128], dt)
    p.op("pool", lambda e: e.memset(ident[:], 1.0), writes=[ident])
    p.op("pool", lambda e: e.affine_select(out=ident[:], in_=ident[:], pattern=[[-1, 128]], compare_op=ALU.is_equal,
                                           fill=0.0, base=0, channel_multiplier=1), reads=[ident], writes=[ident])
    return ident


class WStream:
    def __init__(self, p, nbuf, name="wg"):
        self.p = p
        self.slots = [p.sb(f"{name}{i}", [128, 4096], BF16) for i in range(nbuf)]
        self.i = 0

    def load(self, src_ap):
        t = self.slots[self.i % len(self.slots)]
        self.i += 1
        self.p.dma("pool", t[:], src_ap, writes=[t])
        return t


class Ring:
    def __init__(self, tiles):
        self.tiles, self.i = tiles, 0

    def next(self):
        t = self.tiles[self.i % len(self.tiles)]
        self.i += 1
        return t


def rstd_from_ss(p, ot, oap, it, iap, inv_n):
    ts(p, "dve", ot, oap, it, iap, float(inv_n), float(EPS), ALU.mult, ALU.add)
    act(p, ot, oap, ot, oap, AF.Sqrt)
    p.op("dve", lambda e: e.reciprocal(out=oap, in_=oap), reads=[ot], writes=[ot])


def norm_transpose(p, src_t, src_ap, gT, dstT, dst_col0, tm, junk_ss, ident, trps):
    ss, rs = junk_ss
    act(p, tm, tm[:], src_t, src_ap, AF.Square, accum=ss[:, 0:1], accum_t=ss)
    rstd_from_ss(p, rs, rs[:, 0:1], ss, ss[:, 0:1], 1.0 / D)
    ts(p, "dve", tm, tm[:], src_t, src_ap, rs[:, 0:1], None, ALU.mult, extra=[rs])
    for kb in range(4):
        pt = trps.next()
        for kk in range(8):
            k = kb * 8 + kk
            tr(p, pt, pt[:, kk * 128:(kk + 1) * 128], tm, tm[:, k * 128:(k + 1) * 128], ident)
        p.op("dve", lambda e, pt=pt, kb=kb: e.tensor_tensor(
            out=dstT[:, kb * 8:(kb + 1) * 8, dst_col0:dst_col0 + 128],
            in0=pt[:].rearrange("p (k t) -> p k t", k=8),
            in1=gT[:, kb * 8:(kb + 1) * 8].unsqueeze(2).to_broadcast([128, 8, 128]), op=ALU.mult),
            reads=[pt, gT], writes=[dstT])


def phase1(p, S, TT, io):
    NT = S // TT
    NS = TT // 128
    with ExitStack() as st:
        p.stack = st
        identb = make_ident(p, BF16, "identb")
        identf = make_ident(p, F32, "identf")

        def cload(name, shape, dt, src):
            t = p.sb(name, shape, dt)
            p.dma("sp", t[:], src, writes=[t])
            return t
        gT = cload("gT", [128, NK], F32, io["mix_gT"])
        invf = cload("invf", [128, 1], F32, io["inv_freq"])
        rmaskT = cload("rmaskT", [128, 2, 128], F32, io["ret_maskT"])
        rdq = cload("rdq", [128, 2, 128], F32, io["ret_dq"])
        rdk = cload("rdk", [128, 2], F32, io["ret_dk"])
        rdc = cload("rdc", [128, 2], F32, io["ret_dc"])
        rgain = cload("rgain", [128, 512], F32, io["ret_gain"].partition_broadcast(128))
        ggain = cload("ggain", [128, 128], F32, io["gdn_gain"].partition_broadcast(128))
        convw = cload("convw", [128, 12, 4], F32, io["convw"])
        alog = cload("alog", [128, 4], F32, io["a_log"].partition_broadcast(128))
        dtb = cload("dtb", [128, 4], F32, io["dt_bias"].partition_broadcast(128))
        tri = cload("tri", [128, 128], F32, io["tri"])
        bones = cload("bones", [128, 128], F32, io["bones"])
        ones = cload("ones", [128, 128], F32, io["ones"])
        mstrict = cload("mstrict", [128, 128], F32, io["mstrict"])
        minclT = cload("minclT", [128, 128], F32, io["minclT"])
        wab = cload("wabf", [128, NK, 8], F32, io["w_ab"])
        wabb = p.sb("wabb", [128, NK, 8], BF16)
        cp(p, "dve", wabb, wabb[:], wab, wab[:])
        aneg = p.sb("aneg", [128, 4], F32)
        act(p, aneg, aneg[:], alog, alog[:], AF.Exp)
        ts(p, "dve", aneg, aneg[:], aneg, aneg[:], -1.0, None, ALU.mult)

        ws = WStream(p, 3)
        xt = p.sb("xt", [128, D], F32)
        tm = p.sb("tm", [128, D], BF16)
        ss = p.sb("ss", [128, 1], F32)
        rs = p.sb("rs", [128, 1], F32)
        hnT = p.sb("hnT", [128, NK, TT], BF16)
        pfr = p.sb("pfr", [128, 8, TT], BF16)
        pfg = p.sb("pfg", [128, 12, 3 + TT], BF16)
        p.op("dve", lambda e: e.memset(pfg[:, :, 0:3], 0.0), writes=[pfg])
        ptm = [p.sb(f"ptm{s}", [128, 1536], BF16) for s in range(NS)]
        pab = [p.sb(f"pab{s}", [128, 8], F32) for s in range(NS)]

        def ring(name, shape, dt, n):
            return Ring([p.sb(f"{name}{i}", shape, dt) for i in range(n)])
        posi = p.sb("posi", [128, TT], I32)
        r_f = ring("f128w", [128, 128], F32, 6)
        r_ki = ring("ki", [128, 128], I32, 1)
        r_sc = ring("sincos", [128, 2, 128], F32, 2)
        r_rq = ring("rq", [128, 2, 2, 128], BF16, 2)
        r_rqd = ring("rqd", [128, 2, 2, 128], BF16, 2)
        r_rk = ring("rk", [128, 2, 2, 128], BF16, 2)
        r_gq = ring("gq", [128, 4, 3, 128], BF16, 2)
        rstate = [[p.sb(f"rstate{h}{d}", [128, 256], F32) for d in range(2)] for h in range(2)]
        rstateb = [[p.sb(f"rstateb{h}{d}", [128, 256], BF16) for d in range(2)] for h in range(2)]
        gstate = [p.sb(f"gstate{j}", [128, 128], F32) for j in range(4)]
        gstateb = [p.sb(f"gstateb{j}", [128, 128], BF16) for j in range(4)]
        for h in range(2):
            for d in range(2):
                p.op("dve", lambda e, h=h, d=d: e.memset(rstate[h][d][:], 0.0), writes=[rstate[h][d]])
                p.op("dve", lambda e, h=h, d=d: e.memset(rstateb[h][d][:], 0.0), writes=[rstateb[h][d]])
        for j in range(4):
            p.op("dve", lambda e, j=j: e.memset(gstate[j][:], 0.0), writes=[gstate[j]])
            p.op("dve", lambda e, j=j: e.memset(gstateb[j][:], 0.0), writes=[gstateb[j]])
        mixo = ring("mixo", [128, 1024], BF16, 2)
        pacc = Ring([p.ps(f"pacc{i}", [128, 512], F32) for i in range(2)])
        ptr = Ring([p.ps(f"ptr{i}", [128, 1024], BF16) for i in range(2)])
        pmx = Ring([p.ps(f"pmx{i}", [128, 512], F32) for i in range(4)])
        r_osum = ring("osum", [128, 8], F32, 2)
        g_ab = ring("gab", [128, 40], F32, 2)
        rr = [dict(sT=ring(f"sT{h}", [128, 128], BF16, 2), ktm=ring(f"ktm{h}", [128, 256], BF16, 1),
                   of=ring(f"of{h}", [128, 256], F32, 1), sg=ring(f"sg{h}", [128, 256], F32, 1),
                   junk=ring(f"rjunk{h}", [128, 256], F32, 1), sm=ring(f"rsm{h}", [128, 8], F32, 2)) for h in range(2)]
        gr = [dict(raw=ring(f"raw{j}", [128, 384], BF16, 1), junk=ring(f"gjunk{j}", [128, 256], F32, 1),
                   sm=ring(f"gsm{j}", [128, 8], F32, 4), b=ring(f"gb{j}", [128, 128], BF16, 10),
                   fT=ring(f"fT{j}", [128, 384], BF16, 1), E=ring(f"gE{j}", [128, 128], F32, 5),
                   P=ring(f"gP{j}", [128, 128], F32, 7)) for j in range(4)]

        w_fm, w_tm = io["w_fm"], io["w_tm"]
        x, pos, mixed = io["x"], io["pos"], io["mixed"]
        mixed_rows = mixed if callable(mixed) else (lambda r0: mixed[r0:r0 + 128, :])

        def reduce_sin(dst_t, dst_ap, ang, shift):
            src = ang
            if shift != 0.0:
                sh = r_f.next()
                ts(p, "dve", sh, sh[:], ang, ang[:], float(shift), None, ALU.add)
                src = sh
            ki = r_ki.next()
            kf = r_f.next()
            t2 = r_f.next()
            ts(p, "dve", ki, ki[:], src, src[:], float(1 / (2 * np.pi)), None, ALU.mult)
            cp(p, "dve", kf, kf[:], ki, ki[:])
            stt(p, "dve", t2, t2[:], kf, kf[:], float(-2 * np.pi), src, src[:], ALU.mult, ALU.add)
            ts(p, "dve", kf, kf[:], t2, t2[:], float(np.pi), float(-2 * np.pi), ALU.is_gt, ALU.mult)
            tt(p, "dve", t2, t2[:], t2, t2[:], kf, kf[:], ALU.add)
            ts(p, "dve", kf, kf[:], t2, t2[:], float(-np.pi), float(2 * np.pi), ALU.is_lt, ALU.mult)
            tt(p, "dve", t2, t2[:], t2, t2[:], kf, kf[:], ALU.add)
            ts(p, "dve", t2, t2[:], t2, t2[:], float(-np.pi), float(np.pi), ALU.max, ALU.min)
            act(p, dst_t, dst_ap, t2, t2[:], AF.Sin)

        for ti in range(NT):
            tok0 = ti * TT
            for s in range(NS):
                p.dma("sp", xt[:], x[tok0 + s * 128: tok0 + (s + 1) * 128, :], writes=[xt])
                norm_transpose(p, xt, xt[:], gT, hnT, s * 128, tm, (ss, rs), identb, ptr)
            if ti > 0:
                p.op("dve", lambda e: e.tensor_copy(out=pfg[:, :, 0:3], in_=pfg[:, :, TT:TT + 3]), reads=[pfg], writes=[pfg])
            p.dma("sp", posi[:], pos[0:1, tok0:tok0 + TT].partition_broadcast(128), writes=[posi])
            for blk in range(20):
                wt = ws.load(w_fm[blk])
                wv = wt[:].rearrange("p (k c) -> p k c", k=NK)
                for c0 in range(0, TT, 512):
                    cw = min(512, TT - c0)
                    ps = pacc.next()
                    for k in range(NK):
                        mm(p, ps, ps[:, 0:cw], wt, wv[:, k, :], hnT, hnT[:, k, c0:c0 + cw], start=(k == 0), stop=(k == NK - 1))
                    if blk < 8:
                        cp(p, "act", pfr, pfr[:, blk, c0:c0 + cw], ps, ps[:, 0:cw])
                    else:
                        cp(p, "act", pfg, pfg[:, blk - 8, 3 + c0:3 + c0 + cw], ps, ps[:, 0:cw])
            for n in range(3):
                pss = [pmx.next() for _ in range(NS)]
                for kg in range(4):
                    wt = ws.load(w_tm[n * 4 + kg])
                    wv = wt[:].rearrange("p (k c) -> p k c", k=8)
                    for s in range(NS):
                        for k2 in range(8):
                            k = kg * 8 + k2
                            mm(p, pss[s], pss[s][:], hnT, hnT[:, k, s * 128:(s + 1) * 128], wt, wv[:, k2, :],
                               start=(k == 0), stop=(k == NK - 1))
                for s in range(NS):
                    cp(p, "act" if s % 2 == 0 else "dve", ptm[s], ptm[s][:, n * 512:(n + 1) * 512], pss[s], pss[s][:])
            for s in range(NS):
                ps = pacc.next()
                for k in range(NK):
                    mm(p, ps, ps[:, 0:8], hnT, hnT[:, k, s * 128:(s + 1) * 128], wabb, wabb[:, k, :],
                       start=(k == 0), stop=(k == NK - 1))
                cp(p, "act", pab[s], pab[s][:], ps, ps[:, 0:8])

            for s in range(NS):
                c0 = s * 128
                mo = mixo.next()
                pt = ptm[s]
                ang = r_f.next()
                cp(p, "dve", ang, ang[:], posi, posi[:, c0:c0 + 128])
                ts(p, "dve", ang, ang[:], ang, ang[:], invf[:, 0:1], None, ALU.mult, extra=[invf])
                sc = r_sc.next()
                reduce_sin(sc, sc[:, 0, :], ang, 0.0)
                reduce_sin(sc, sc[:, 1, :], ang, np.pi / 2)
                sinT, cosT = sc[:, 0, :], sc[:, 1, :]
                rq, rqd, rk = r_rq.next(), r_rqd.next(), r_rk.next()
                for h in range(2):
                    for qk, dst in ((0, rq), (1, rk)):
                        x1 = pfr[:, h * 4 + qk * 2 + 0, c0:c0 + 128]
                        x2 = pfr[:, h * 4 + qk * 2 + 1, c0:c0 + 128]
                        t1, t2 = r_f.next(), r_f.next()
                        tt(p, "dve", t1, t1[:], pfr, x1, sc, cosT, ALU.mult)
                        tt(p, "dve", t2, t2[:], pfr, x2, sc, sinT, ALU.mult)
                        tt(p, "dve", dst, dst[:, h, 0, :], t1, t1[:], t2, t2[:], ALU.subtract)
                        t1, t2 = r_f.next(), r_f.next()
                        tt(p, "dve", t1, t1[:], pfr, x2, sc, cosT, ALU.mult)
                        tt(p, "dve", t2, t2[:], pfr, x1, sc, sinT, ALU.mult)
                        tt(p, "dve", dst, dst[:, h, 1, :], t1, t1[:], t2, t2[:], ALU.add)
                    for half in range(2):
                        tt(p, "dve", rqd, rqd[:, h, half, :], rq, rq[:, h, half, :], rdq, rdq[:, h, :], ALU.mult)
                gq = r_gq.next()
                for j in range(4):
                    for c in range(3):
                        b = j * 3 + c
                        t1 = r_f.next()
                        ts(p, "dve", t1, t1[:], pfg, pfg[:, b, c0 + 3:c0 + 3 + 128], convw[:, b, 3:4], None, ALU.mult, extra=[convw])
                        for w in (2, 1, 0):
                            stt(p, "dve", t1, t1[:], pfg, pfg[:, b, c0 + w:c0 + w + 128], convw[:, b, w:w + 1], t1, t1[:],
                                ALU.mult, ALU.add, extra=[convw])
                        act(p, gq, gq[:, j, c, :], t1, t1[:], AF.Silu)

                def ret_head(h, mo=mo, pt=pt, rq=rq, rqd=rqd, rk=rk):
                    R = rr[h]
                    ptk = ptr.next()
                    for half in range(2):
                        tr(p, ptk, ptk[:, half * 128:(half + 1) * 128], rk, rk[:, h, half, :], identb)
                    ktm = R["ktm"].next()
                    ts(p, "dve", ktm, ktm[:], ptk, ptk[:, 0:256], rdk[:, h:h + 1], None, ALU.mult, extra=[rdk])
                    psc = pmx.next()
                    for half in range(2):
                        mm(p, psc, psc[:, 0:128], rk, rk[:, h, half, :], rq, rq[:, h, half, :], start=(half == 0), stop=(half == 1))
                    sT = R["sT"].next()
                    tt(p, "dve", sT, sT[:], psc, psc[:, 0:128], rmaskT, rmaskT[:, h, :], ALU.mult)
                    yield
                    vap = pt[:, h * 256:(h + 1) * 256]
                    po = pmx.next()
                    mm(p, po, po[:, 0:256], sT, sT[:], pt, vap, start=True, stop=False)
                    for half in range(2):
                        mm(p, po, po[:, 0:256], rqd, rqd[:, h, half, :], rstateb[h][half], rstateb[h][half][:],
                           start=False, stop=(half == 1))
                    of = R["of"].next()
                    cp(p, "act", of, of[:], po, po[:, 0:256])
                    yield
                    for half in range(2):
                        pst = pmx.next()
                        mm(p, pst, pst[:, 0:256], ktm, ktm[:, half * 128:(half + 1) * 128], pt, vap)
                        stt(p, "dve", rstate[h][half], rstate[h][half][:], rstate[h][half], rstate[h][half][:], rdc[:, h:h + 1],
                            pst, pst[:, 0:256], ALU.mult, ALU.add, extra=[rdc])
                        cp(p, "act", rstateb[h][half], rstateb[h][half][:], rstate[h][half], rstate[h][half][:])
                        yield
                    junk = R["junk"].next()
                    sq = R["sm"].next()
                    act(p, junk, junk[:], of, of[:], AF.Square, accum=sq[:, 0:1], accum_t=sq)
                    sg = R["sg"].next()
                    act(p, sg, sg[:], pt, pt[:, 512 + h * 256: 512 + (h + 1) * 256], AF.Silu)
                    yield
                    act(p, sq, sq[:, 1:2], sq, sq[:, 0:1], AF.Ln, bias=float(EPS), scale=1.0 / 256)
                    act(p, sq, sq[:, 1:2], sq, sq[:, 1:2], AF.Exp, scale=-0.5)
                    tt(p, "dve", sg, sg[:], sg, sg[:], rgain, rgain[:, h * 256:(h + 1) * 256], ALU.mult)
                    yield
                    stt(p, "dve", mo, mo[:, h * 256:(h + 1) * 256], of, of[:], sq[:, 1:2], sg, sg[:], ALU.mult, ALU.mult, extra=[sq])

                ab = g_ab.next()
                pb8 = pab[s]
                tt(p, "dve", ab, ab[:, 24:28], pb8, pb8[:, 0:4], dtb, dtb[:], ALU.add)
                stt(p, "dve", ab, ab[:, 0:4], ab, ab[:, 24:28], -1.0, ab, ab[:, 24:28], ALU.mult, ALU.max)
                act(p, ab, ab[:, 0:4], ab, ab[:, 0:4], AF.Exp, scale=-1.0)
                act(p, ab, ab[:, 0:4], ab, ab[:, 0:4], AF.Ln, bias=1.0)
                stt(p, "dve", ab, ab[:, 0:4], ab, ab[:, 24:28], 0.0, ab, ab[:, 0:4], ALU.max, ALU.add)
                tt(p, "dve", ab, ab[:, 0:4], ab, ab[:, 0:4], aneg, aneg[:], ALU.mult)
                act(p, ab, ab[:, 4:8], pb8, pb8[:, 4:8], AF.Sigmoid)
                ts(p, "dve", ab, ab[:, 28:32], ab, ab[:, 4:8], -1.0, None, ALU.mult)
                pg = pmx.next()
                mm(p, pg, pg[:, 0:4], tri, tri[:], ab, ab[:, 0:4])
                mm(p, pg, pg[:, 4:8], bones, bones[:], ab, ab[:, 0:4])
                cp(p, "dve", ab, ab[:, 8:16], pg, pg[:, 0:8])
                act(p, ab, ab[:, 16:20], ab, ab[:, 8:12], AF.Exp)
                tt(p, "dve", ab, ab[:, 24:28], ab, ab[:, 12:16], ab, ab[:, 8:12], ALU.subtract)
                act(p, ab, ab[:, 20:24], ab, ab[:, 24:28], AF.Exp)
                tt(p, "dve", ab, ab[:, 32:36], ab, ab[:, 4:8], ab, ab[:, 16:20], ALU.mult)
                osum = r_osum.next()

                def gdn_head(j, mo=mo, pt=pt, gq=gq, ab=ab, osum=osum):
                    G = gr[j]
                    pq = ptr.next()
                    for c in range(3):
                        tr(p, pq, pq[:, c * 128:(c + 1) * 128], gq, gq[:, j, c, :], identb)
                    raw = G["raw"].next()
                    cp(p, "act", raw, raw[:], pq, pq[:, 0:384])
                    yield
                    nq = G["sm"].next()
                    junk = G["junk"].next()
                    act(p, junk, junk[:, 0:128], raw, raw[:, 0:128], AF.Square, accum=nq[:, 0:1], accum_t=nq)
                    act(p, junk, junk[:, 128:256], raw, raw[:, 128:256], AF.Square, accum=nq[:, 1:2], accum_t=nq)
                    yield
                    act(p, nq, nq[:, 2:4], nq, nq[:, 0:2], AF.Ln, bias=float(EPS))
                    act(p, nq, nq[:, 2:4], nq, nq[:, 2:4], AF.Exp, scale=-0.5)
                    yield
                    ts(p, "dve", nq, nq[:, 4:5], nq, nq[:, 2:3], float(128 ** -0.5), None, ALU.mult)
                    tt(p, "dve", nq, nq[:, 5:6], nq, nq[:, 4:5], ab, ab[:, 16 + j:17 + j], ALU.mult)
                    tt(p, "dve", nq, nq[:, 6:7], nq, nq[:, 3:4], ab, ab[:, 32 + j:33 + j], ALU.mult)
                    tt(p, "dve", nq, nq[:, 7:8], nq, nq[:, 3:4], ab, ab[:, 20 + j:21 + j], ALU.mult)
                    qh, qd, kh, kbg, kdc, vbt = (G["b"].next() for _ in range(6))
                    ts(p, "dve", qh, qh[:], raw, raw[:, 0:128], nq[:, 4:5], None, ALU.mult, extra=[nq])
                    ts(p, "dve", qd, qd[:], raw, raw[:, 0:128], nq[:, 5:6], None, ALU.mult, extra=[nq])
                    ts(p, "dve", kh, kh[:], raw, raw[:, 128:256], nq[:, 3:4], None, ALU.mult, extra=[nq])
                    yield
                    ts(p, "dve", kbg, kbg[:], raw, raw[:, 128:256], nq[:, 6:7], None, ALU.mult, extra=[nq])
                    ts(p, "dve", kdc, kdc[:], raw, raw[:, 128:256], nq[:, 7:8], None, ALU.mult, extra=[nq])
                    ts(p, "dve", vbt, vbt[:], raw, raw[:, 256:384], ab[:, 4 + j:5 + j], None, ALU.mult, extra=[ab])
                    pb = ptr.next()
                    tr(p, pb, pb[:, 0:128], qh, qh[:], identb)
                    tr(p, pb, pb[:, 128:256], qd, qd[:], identb)
                    tr(p, pb, pb[:, 256:384], kh, kh[:], identb)
                    fT = G["fT"].next()
                    cp(p, "act", fT, fT[:], pb, pb[:, 0:384])
                    qhT, qdT, khT = fT[:, 0:128], fT[:, 128:256], fT[:, 256:384]
                    yield
                    trl = G["E"].next()
                    ts(p, "dve", trl, trl[:], tri, tri[:], ab[:, j:j + 1], None, ALU.mult, extra=[ab])
                    yield
                    pgb = pmx.next()
                    mm(p, pgb, pgb[:, 0:128], ones, ones[:], trl, trl[:])
                    E = G["E"].next()
                    ts(p, "dve", E, E[:], pgb, pgb[:, 0:128], ab[:, 8 + j:9 + j], 0.0, ALU.subtract, ALU.max, extra=[ab])
                    ET = G["E"].next()
                    ts(p, "dve", ET, ET[:], pgb, pgb[:, 0:128], ab[:, 8 + j:9 + j], 0.0, ALU.subtract, ALU.min, extra=[ab])
                    egl = G["sm"].next()
                    act(p, egl, egl[:, 0:1], pgb, pgb[:, 63:64], AF.Exp)
                    act(p, egl, egl[:, 1:2], pgb, pgb[:, 127:128], AF.Exp)
                    yield
                    act(p, E, E[:], E, E[:], AF.Exp, scale=-1.0)
                    act(p, ET, ET[:], ET, ET[:], AF.Exp)
                    yield
                    tt(p, "dve", E, E[:], E, E[:], mstrict, mstrict[:], ALU.mult)
                    tt(p, "dve", ET, ET[:], ET, ET[:], minclT, minclT[:], ALU.mult)
                    pqk = pmx.next()
                    mm(p, pqk, pqk[:, 0:128], fT, khT, fT, qhT)
                    qkT = G["b"].next()
                    tt(p, "dve", qkT, qkT[:], pqk, pqk[:, 0:128], ET, ET[:], ALU.mult)
                    yield
                    pkk = pmx.next()
                    mm(p, pkk, pkk[:, 0:128], fT, khT, fT, khT)
                    N = G["P"].next()
                    stt(p, "dve", N, N[:], pkk, pkk[:, 0:128], ab[:, 28 + j:29 + j], E, E[:], ALU.mult, ALU.mult, extra=[ab])
                    yield
                    pnt = pmx.next()
                    mm(p, pnt, pnt[:, 0:128], N, N[:], identf, identf[:])
                    NTt = G["P"].next()
                    cp(p, "act", NTt, NTt[:], pnt, pnt[:, 0:128])
                    yield
                    Q = G["P"].next()
                    tt(p, "dve", Q, Q[:], NTt, NTt[:], identf, identf[:], ALU.add)
                    P, PT = N, NTt
                    pend = None
                    for it in range(5):
                        pp = pmx.next()
                        mm(p, pp, pp[:, 0:128], PT, PT[:], P, P[:])
                        mm(p, pp, pp[:, 128:256], P, P[:], PT, PT[:])
                        P2, P2T = G["P"].next(), G["P"].next()
                        cp(p, "act", P2, P2[:], pp, pp[:, 0:128])
                        cp(p, "dve", P2T, P2T[:], pp, pp[:, 128:256])
                        if pend is not None:
                            pqq = pmx.next()
                            mm(p, pqq, pqq[:, 0:128], pend, pend[:], Q, Q[:])
                            Q2 = G["P"].next()
                            tt(p, "dve", Q2, Q2[:], pqq, pqq[:, 0:128], Q, Q[:], ALU.add)
                            Q = Q2
                        yield
                        pend = P2
                        P, PT = P2, P2T
                    pqq = pmx.next()
                    mm(p, pqq, pqq[:, 0:128], pend, pend[:], Q, Q[:])
                    Q2 = G["P"].next()
                    tt(p, "dve", Q2, Q2[:], pqq, pqq[:, 0:128], Q, Q[:], ALU.add)
                    Q = Q2
                    yield
                    TiT = G["b"].next()
                    cp(p, "act", TiT, TiT[:], Q, Q[:])
                    yield
                    pw = pmx.next()
                    mm(p, pw, pw[:, 0:128], kbg, kbg[:], TiT, TiT[:])
                    nwT = G["b"].next()
                    ts(p, "dve", nwT, nwT[:], pw, pw[:, 0:128], -1.0, None, ALU.mult)
                    yield
                    vn = G["b"].next()
                    oc = G["E"].next()
                    for cx in range(2):
                        r0, r1 = cx * 64, cx * 64 + 64
                        pv = pmx.next()
                        mm(p, pv, pv[:, 0:128], TiT, TiT[:], vbt, vbt[:], start=True, stop=False)
                        mm(p, pv, pv[:, 0:128], nwT, nwT[:], gstateb[j], gstateb[j][:], start=False, stop=True)
                        cp(p, "act", vn, vn[r0:r1, :], pv, pv[r0:r1, 0:128])
                        yield
                        po = pmx.next()
                        mm(p, po, po[:, 0:128], fT, qdT, gstateb[j], gstateb[j][:], start=True, stop=False)
                        mm(p, po, po[:, 0:128], qkT, qkT[r0:r1, :], vn, vn[r0:r1, :], start=False, stop=True)
                        cp(p, "act", oc, oc[r0:r1, :], po, po[r0:r1, 0:128])
                        pst = pmx.next()
                        mm(p, pst, pst[:, 0:128], kdc, kdc[r0:r1, :], vn, vn[r0:r1, :])
                        stt(p, "dve", gstate[j], gstate[j][:], gstate[j], gstate[j][:], egl[:, cx:cx + 1], pst, pst[:, 0:128],
                            ALU.mult, ALU.add, extra=[egl])
                        yield
                        cp(p, "act", gstateb[j], gstateb[j][:], gstate[j], gstate[j][:])
                        yield
                    junk2 = G["junk"].next()
                    act(p, junk2, junk2[:, 0:128], oc, oc[:], AF.Square, accum=osum[:, j:j + 1], accum_t=osum)
                    sgz = G["E"].next()
                    act(p, sgz, sgz[:], pt, pt[:, 1024 + j * 128: 1024 + (j + 1) * 128], AF.Silu)
                    yield
                    act(p, osum, osum[:, 4 + j:5 + j], osum, osum[:, j:j + 1], AF.Ln, bias=float(EPS), scale=1.0 / 128)
                    act(p, osum, osum[:, 4 + j:5 + j], osum, osum[:, 4 + j:5 + j], AF.Exp, scale=-0.5)
                    tt(p, "dve", sgz, sgz[:], sgz, sgz[:], ggain, ggain[:], ALU.mult)
                    yield
                    stt(p, "dve", mo, mo[:, 512 + j * 128: 512 + (j + 1) * 128], oc, oc[:], osum[:, 4 + j:5 + j], sgz, sgz[:],
                        ALU.mult, ALU.mult, extra=[osum])

                chains = [gdn_head(0), ret_head(0), gdn_head(1), gdn_head(2), ret_head(1), gdn_head(3)]
                while chains:
                    for ch in list(chains):
                        try:
                            next(ch)
                        except StopIteration:
                            chains.remove(ch)
                if "mixed_tiles" in io:
                    p.dma("sp", mixed_rows(tok0 + c0), mo[:], reads=[mo], writes=[io["mixed_tiles"][ti]], semtile=mo)
                else:
                    p.dma("sp", mixed_rows(tok0 + c0), mo[:], reads=[mo])
            if "exchange" in io:
                io["exchange"](ti)
        p.finish()
        p.run()


OFF = dict(rq=0, rk=2048, rv=4096, rg=6144, gq=8192, gk=10240, gv=12288, gz=14336, ga=16384, gb=16400)


def _granules_ws(w, cols):
    sub = w[:, cols]
    return np.ascontiguousarray(sub.reshape(NK, 128, len(cols)).transpose(1, 0, 2)).reshape(128, NK * len(cols))


def p1_host_inputs(inp, b, g, S, TT):
    w = inp["w_in"][0]
    fm_cols = []
    for h in range(2):
        hr = 2 * g + h
        fm_cols += list(range(OFF["rq"] + hr * 256, OFF["rq"] + hr * 256 + 256))
        fm_cols += list(range(OFF["rk"] + hr * 256, OFF["rk"] + hr * 256 + 256))
    for j in range(4):
        hg = 4 * g + j
        for nm in ("gq", "gk", "gv"):
            fm_cols += list(range(OFF[nm] + hg * 128, OFF[nm] + hg * 128 + 128))
    tm_cols = []
    for nm, width, nh in (("rv", 256, 2), ("rg", 256, 2), ("gz", 128, 4)):
        for h in range(nh):
            hh = (2 * g + h) if nh == 2 else (4 * g + h)
            tm_cols += list(range(OFF[nm] + hh * width, OFF[nm] + hh * width + width))
    ab_cols = [OFF["ga"] + 4 * g + j for j in range(4)] + [OFF["gb"] + 4 * g + j for j in range(4)]
    w_fm = np.stack([_granules_ws(w, fm_cols[i * 128:(i + 1) * 128]) for i in range(20)])
    wt_ = w[:, tm_cols].reshape(4, 8, 128, 3, 512)
    w_tm = np.ascontiguousarray(wt_.transpose(3, 0, 2, 1, 4)).reshape(12, 128, 4096)
    w_ab = np.ascontiguousarray(w[:, ab_cols].reshape(NK, 128, 8).transpose(1, 0, 2))
    conv = inp["gdn_conv"][0]
    convw = np.zeros((128, 12, 4), np.float32)
    for j in range(4):
        hg = 4 * g + j
        for c in range(3):
            convw[:, j * 3 + c, :] = conv[:, c * 2048 + hg * 128: c * 2048 + hg * 128 + 128].T
    f32 = np.float32
    hs = np.array([2 * g, 2 * g + 1], f32)
    lg = np.log1p(-np.exp2(-5.0 - hs)).astype(f32)
    idx = np.arange(128, dtype=f32)
    rel = idx[None, :] - idx[:, None]
    maskT = np.where(rel >= 0, np.exp(np.maximum(rel, 0)[None] * lg[:, None, None]), 0.0).astype(f32)
    ret_maskT = np.ascontiguousarray((maskT * f32(1 / 16)).transpose(1, 0, 2))
    dq = np.exp((idx + 1.0)[None, :] * lg[:, None]).astype(f32)
    ret_dq = np.ascontiguousarray(np.broadcast_to(dq[None], (128, 2, 128))).astype(f32)
    dk = np.exp((127.0 - idx)[None, :] * lg[:, None]).astype(f32) * f32(1 / 16)
    ret_dk = np.ascontiguousarray(dk.T)
    ret_dc = np.ascontiguousarray(np.broadcast_to(np.exp(128.0 * lg)[None], (128, 2))).astype(f32)
    inv_freq = (10000.0 ** (-np.arange(128, dtype=f32) / f32(128))).astype(f32).reshape(128, 1)
    blk = (np.arange(128)[:, None] // 64) == (np.arange(128)[None, :] // 64)
    ii, jj = np.arange(128)[:, None], np.arange(128)[None, :]
    tri = (blk & (ii <= jj)).astype(f32)
    bones = blk.astype(f32)
    mstrict = (blk & (ii > jj)).astype(f32)
    minclT = (blk & (jj >= ii)).astype(f32)
    d = {
        "x": np.ascontiguousarray(inp["x"][b]),
        "pos": np.ascontiguousarray(inp["positions"][b][None, :]).astype(np.int32),
        "mix_gT": np.ascontiguousarray(inp["mix_norm"][0].reshape(NK, 128).T),
        "w_fm": w_fm, "w_tm": w_tm, "w_ab": w_ab, "convw": convw,
        "ret_gain": np.ascontiguousarray(inp["ret_norm"][0][2 * g:2 * g + 2].reshape(1, 512)),
        "gdn_gain": np.ascontiguousarray(inp["gdn_norm"][0].reshape(1, 128)),
        "a_log": np.ascontiguousarray(inp["gdn_a_log"][0][4 * g:4 * g + 4].reshape(1, 4)),
        "dt_bias": np.ascontiguousarray(inp["gdn_dt_bias"][0][4 * g:4 * g + 4].reshape(1, 4)),
        "inv_freq": inv_freq, "ret_maskT": ret_maskT, "ret_dq": ret_dq, "ret_dk": ret_dk, "ret_dc": ret_dc,
        "tri": tri, "bones": bones, "ones": np.ones((128, 128), f32), "mstrict": mstrict, "minclT": minclT,
    }
    return d


P1_SPECS = lambda S: {
    "x": ([S, D], F32), "pos": ([1, S], I32), "mix_gT": ([128, NK], F32),
    "w_fm": ([20, 128, 4096], F32), "w_tm": ([12, 128, 4096], F32), "w_ab": ([128, NK, 8], F32),
    "convw": ([128, 12, 4], F32), "ret_gain": ([1, 512], F32), "gdn_gain": ([1, 128], F32),
    "a_log": ([1, 4], F32), "dt_bias": ([1, 4], F32), "inv_freq": ([128, 1], F32),
    "ret_maskT": ([128, 2, 128], F32), "ret_dq": ([128, 2, 128], F32), "ret_dk": ([128, 2], F32),
    "ret_dc": ([128, 2], F32), "tri": ([128, 128], F32), "bones": ([128, 128], F32), "ones": ([128, 128], F32),
    "mstrict": ([128, 128], F32), "minclT": ([128, 128], F32),
}


def transpose_rows(p, tm, dstT, col0, ident, trps, gT=None, eng_alt=("dve", "act")):
    for kb in range(4):
        pt = trps.next()
        for kk in range(8):
            k = kb * 8 + kk
            tr(p, pt, pt[:, kk * 128:(kk + 1) * 128], tm, tm[:, k * 128:(k + 1) * 128], ident)
        if gT is not None:
            p.op("dve", lambda e, pt=pt, kb=kb: e.tensor_tensor(
                out=dstT[:, kb * 8:(kb + 1) * 8, col0:col0 + 128],
                in0=pt[:].rearrange("p (k t) -> p k t", k=8),
                in1=gT[:, kb * 8:(kb + 1) * 8].unsqueeze(2).to_broadcast([128, 8, 128]), op=ALU.mult),
                reads=[pt, gT], writes=[dstT])
        else:
            cp(p, eng_alt[kb % 2], dstT, dstT[:, kb * 8:(kb + 1) * 8, col0:col0 + 128], pt, pt[:].rearrange("p (k t) -> p k t", k=8))


def phase2(p, Tc, TT, io, gathered=None, gathered_rows=None):
    NT = Tc // TT
    NS = TT // 128
    with ExitStack() as st:
        p.stack = st
        identb = make_ident(p, BF16, "identb")

        def cload(name, shape, dt, src):
            t = p.sb(name, shape, dt)
            p.dma("sp", t[:], src, writes=[t])
            return t
        gx = cload("gx", [128, NK], F32, io["xattn_gT"])
        gm = cload("gm", [128, NK], F32, io["mem_gT"])
        gl = cload("gl", [128, NK], F32, io["mlp_gT"])

        ws = WStream(p, 3)
        h = p.sb("h", [128, NS, D], F32)
        hs = [p.view(f"h{s}", h.t[:, s, :]) for s in range(NS)]
        TB = max(TT, MEM)
        bufA = p.sb("bufA", [128, NK, TB], BF16)
        bufB = p.sb("bufB", [128, NK, TB], BF16)
        aTh = [p.view(f"aT{i}", bufB.t[:, i * 16:(i + 1) * 16, :]) for i in range(2)]
        tm = p.sb("tm", [128, D], BF16)
        ss = p.sb("ss", [128, 1], F32)
        rs = p.sb("rs", [128, 1], F32)
        gfin = p.sb("gfin", [128, 2048], F32)
        kTr = Ring([p.sb(f"kTh{i}", [128, 8, 256], BF16) for i in range(1)])
        vr = Ring([p.sb(f"vh{i}", [128, 2, 1024], BF16) for i in range(1)])
        if gathered is not None:
            sel = cload("sel", [128, 4], F32, io["sel"].partition_broadcast(128))
            cand = p.sb("cand", [128, 4, 1024], BF16)
        er = Ring([p.sb(f"e{i}", [128, 256], F32) for i in range(4)])
        pbr = Ring([p.sb(f"pb{i}", [128, 256], BF16) for i in range(4)])
        pTr = Ring([p.sb(f"pT{i}", [128, 2, TT], BF16) for i in range(2)])
        smr = Ring([p.sb(f"sm{i}", [128, 4], F32) for i in range(8)])
        rlr = Ring([p.sb(f"rl{i}", [128, 512], F32) for i in range(1)])
        vst = Ring([p.sb(f"vst{i}", [128, 512], BF16) for i in range(2)])
        pacc = Ring([p.ps(f"pacc{i}", [128, 512], F32) for i in range(6)])
        ptr = Ring([p.ps(f"ptr{i}", [128, 1024], BF16) for i in range(2)])

        x, mixed_in, mem, out = io["x"], io.get("mixed_in"), io["mem"], io["out"]
        kT_scr = p.view("kT_scr", io["kT_scr"])
        v_scr = p.view("v_scr", io["v_scr"])
        evi = [0]

        def evac_copy(ot, oap, ps, pap, scale=None):
            evi[0] += 1
            if evi[0] % 2:
                if scale is None:
                    cp(p, "act", ot, oap, ps, pap)
                else:
                    p.op("act", lambda e: e.mul(out=oap, in_=pap, mul=float(scale)), reads=[ps], writes=[ot])
            else:
                if scale is None:
                    cp(p, "dve", ot, oap, ps, pap)
                else:
                    ts(p, "dve", ot, oap, ps, pap, float(scale), None, ALU.mult)

        def rms_rows(src_t, src_ap):
            act(p, tm, tm[:], src_t, src_ap, AF.Square, accum=ss[:, 0:1], accum_t=ss)
            rstd_from_ss(p, rs, rs[:, 0:1], ss, ss[:, 0:1], 1.0 / D)
            ts(p, "dve", tm, tm[:], src_t, src_ap, rs[:, 0:1], None, ALU.mult, extra=[rs])

        def linear_as(actT, wsrc, n_kg, kpg, nblocks, sink):
            for n in range(nblocks):
                pss = [pacc.next() for _ in range(NS)]
                for kg in range(n_kg):
                    wt = ws.load(wsrc(n, kg))
                    wv = wt[:].rearrange("p (k c) -> p k c", k=kpg)
                    for s in range(NS):
                        for k2 in range(kpg):
                            k = kg * kpg + k2
                            mm(p, pss[s], pss[s][:], actT[0], actT[1](k, s), wt, wv[:, k2, :],
                               start=(k == 0), stop=(k == n_kg * kpg - 1))
                for s in range(NS):
                    sink(n, s, pss[s])

        def add_to_h(n, s, ps):
            tt(p, "dve", hs[s], hs[s][:, n * 512:(n + 1) * 512], ps, ps[:], hs[s], hs[s][:, n * 512:(n + 1) * 512], ALU.add)

        for mt in range(2):
            p.dma("sp", hs[0][:], mem[mt * 128:(mt + 1) * 128, :], writes=[hs[0]])
            rms_rows(hs[0], hs[0][:])
            transpose_rows(p, tm, bufA, mt * 128, identb, ptr, gT=gm)
        kst = bufB[:].rearrange("p k t -> p (k t)")[:, 0:NK * MEM].rearrange("p (k m) -> p k m", k=NK)
        for cb in range(NK):
            wt = ws.load(io["w_xk"][cb])
            wv = wt[:].rearrange("p (k c) -> p k c", k=NK)
            ps = pacc.next()
            for k in range(NK):
                mm(p, ps, ps[:, 0:256], wt, wv[:, k, :], bufA, bufA[:, k, 0:256], start=(k == 0), stop=(k == NK - 1))
            evac_copy(bufB, kst[:, cb, :], ps, ps[:, 0:256])
        p.dma("sp", kT_scr[:], kst, reads=[bufB], writes=[kT_scr])
        for n in range(8):
            pss = [pacc.next() for _ in range(2)]
            for kg in range(4):
                wt = ws.load(io["w_xv"][n * 4 + kg])
                wv = wt[:].rearrange("p (k c) -> p k c", k=8)
                for mt in range(2):
                    for k2 in range(8):
                        k = kg * 8 + k2
                        mm(p, pss[mt], pss[mt][:], bufA, bufA[:, k, mt * 128:(mt + 1) * 128], wt, wv[:, k2, :],
                           start=(k == 0), stop=(k == NK - 1))
            for mt in range(2):
                vs = vst.next()
                evac_copy(vs, vs[:], pss[mt], pss[mt][:])
                p.dma("sp", v_scr[:, mt, n * 512:(n + 1) * 512], vs[:], reads=[vs], writes=[v_scr])

        for ti in range(NT):
            tok0 = ti * TT
            for s in range(NS):
                r0 = tok0 + s * 128
                p.dma("sp", hs[s][:], x[r0:r0 + 128, :], writes=[hs[s]])
                for g in range(4):
                    if gathered is None:
                        p.dma("sp", tm[:, g * 1024:(g + 1) * 1024], mixed_in[g, r0:r0 + 128, :], writes=[tm])
                    else:
                        for q in range(4):
                            p.dma("sp", cand[:, q, :], gathered_rows(g, q * Tc + r0), reads=[gathered], writes=[cand])
                        dst = tm[:, g * 1024:(g + 1) * 1024]
                        ts(p, "dve", tm, dst, cand, cand[:, 0, :], sel[:, 0:1], None, ALU.mult, extra=[sel])
                        for q in range(1, 4):
                            stt(p, "dve", tm, dst, cand, cand[:, q, :], sel[:, q:q + 1], tm, dst, ALU.mult, ALU.add, extra=[sel])
                transpose_rows(p, tm, bufA, s * 128, identb, ptr)
            linear_as((bufA, lambda k, s: bufA[:, k, s * 128:(s + 1) * 128]), lambda n, kg: io["w_mo"][n * 4 + kg], 4, 8, 8, add_to_h)
            for s in range(NS):
                rms_rows(hs[s], hs[s][:])
                transpose_rows(p, tm, bufA, s * 128, identb, ptr, gT=gx)
            for cb in range(NK):
                wt = ws.load(io["w_xq"][cb])
                wv = wt[:].rearrange("p (k c) -> p k c", k=NK)
                ps = pacc.next()
                for k in range(NK):
                    mm(p, ps, ps[:, 0:TT], wt, wv[:, k, :], bufA, bufA[:, k, 0:TT], start=(k == 0), stop=(k == NK - 1))
                evac_copy(bufB, bufB[:, cb, 0:TT], ps, ps[:, 0:TT], scale=1.0 / 32.0)
            for hd in range(4):
                kTh, vh = kTr.next(), vr.next()
                p.dma("sp", kTh[:], kT_scr[:, hd * 8:(hd + 1) * 8, :], reads=[kT_scr], writes=[kTh])
                p.dma("sp", vh[:], v_scr[:, :, hd * 1024:(hd + 1) * 1024], reads=[v_scr], writes=[vh])
                pT = pTr.next()
                def soft_chain(s, hd=hd, kTh=kTh, pT=pT):
                    psc = pacc.next()
                    for c in range(8):
                        mm(p, psc, psc[:, 0:256], bufB, bufB[:, hd * 8 + c, s * 128:(s + 1) * 128], kTh, kTh[:, c, :],
                           start=(c == 0), stop=(c == 7))
                    sm = smr.next()
                    p.op("dve", lambda e, psc=psc, sm=sm: e.reduce_max(out=sm[:, 0:1], in_=psc[:, 0:256], axis=AX.X), reads=[psc], writes=[sm])
                    yield
                    ts(p, "dve", sm, sm[:, 1:2], sm, sm[:, 0:1], -1.0, None, ALU.mult)
                    ee = er.next()
                    act(p, ee, ee[:], psc, psc[:, 0:256], AF.Exp, bias=sm[:, 1:2], accum=sm[:, 2:3], accum_t=sm, extra=[sm])
                    yield
                    p.op("dve", lambda e, sm=sm: e.reciprocal(out=sm[:, 3:4], in_=sm[:, 2:3]), reads=[sm], writes=[sm])
                    pb = pbr.next()
                    ts(p, "dve", pb, pb[:], ee, ee[:], sm[:, 3:4], None, ALU.mult, extra=[sm])
                    yield
                    ptp = ptr.next()
                    for mt in range(2):
                        tr(p, ptp, ptp[:, mt * 128:(mt + 1) * 128], pb, pb[:, mt * 128:(mt + 1) * 128], identb)
                    cp(p, "act", pT, pT[:, :, s * 128:(s + 1) * 128], ptp, ptp[:, 0:256].rearrange("p (m t) -> p m t", m=2))
                chains = [soft_chain(s) for s in range(NS)]
                while chains:
                    for ch in list(chains):
                        try:
                            next(ch)
                        except StopIteration:
                            chains.remove(ch)
                for cb in range(8):
                    ps = pacc.next()
                    for mt in range(2):
                        mm(p, ps, ps[:, 0:TT], vh, vh[:, mt, cb * 128:(cb + 1) * 128], pT, pT[:, mt, :], start=(mt == 0), stop=(mt == 1))
                    evac_copy(bufA, bufA[:, hd * 8 + cb, 0:TT], ps, ps[:, 0:TT])
            linear_as((bufA, lambda k, s: bufA[:, k, s * 128:(s + 1) * 128]), lambda n, kg: io["w_xo"][n * 4 + kg], 4, 8, 8, add_to_h)
            for s in range(NS):
                rms_rows(hs[s], hs[s][:])
                transpose_rows(p, tm, bufA, s * 128, identb, ptr, gT=gl)

            def up_group(G):
                aT = aTh[G % 2]
                for cb in range(16):
                    wt = ws.load(io["w_up"][G * 16 + cb])
                    wv = wt[:].rearrange("p (k c) -> p k c", k=NK)
                    ps = pacc.next()
                    for k in range(NK):
                        mm(p, ps, ps[:, 0:TT], wt, wv[:, k, :], bufA, bufA[:, k, 0:TT], start=(k == 0), stop=(k == NK - 1))
                    rl = rlr.next()
                    act(p, rl, rl[:, 0:TT], ps, ps[:, 0:TT], AF.Relu)
                    tt(p, "dve", aT, aT[:, cb, 0:TT], rl, rl[:, 0:TT], rl, rl[:, 0:TT], ALU.mult)

            def down_group(G):
                aT = aTh[G % 2]
                linear_as((aT, lambda k, s: aT[:, k, s * 128:(s + 1) * 128]),
                          lambda n, kg: io["w_down"][(G * 8 + n) * 2 + kg], 2, 8, 8, add_to_h)
            NG = DFF // 2048
            up_group(0)
            for G in range(NG):
                if G + 1 < NG:
                    up_group(G + 1)
                down_group(G)
            for s in range(NS):
                r0 = tok0 + s * 128
                act(p, tm, tm[:], hs[s], hs[s][:], AF.Square, accum=ss[:, 0:1], accum_t=ss)
                rstd_from_ss(p, rs, rs[:, 0:1], ss, ss[:, 0:1], 1.0 / D)
                for hf in range(2):
                    p.dma("sp", gfin[:], io["final_gain"][0:1, hf * 2048:(hf + 1) * 2048].partition_broadcast(128), writes=[gfin])
                    stt(p, "dve", hs[s], hs[s][:, hf * 2048:(hf + 1) * 2048], hs[s], hs[s][:, hf * 2048:(hf + 1) * 2048], rs[:, 0:1],
                        gfin, gfin[:], ALU.mult, ALU.mult, extra=[rs])
                p.dma("sp", out[r0:r0 + 128, :], hs[s][:], reads=[hs[s]])
        p.finish()
        p.run()


def _gran_ws(w, nblk):
    K_, N_ = w.shape
    return np.ascontiguousarray(w.reshape(NK, 128, nblk, 128).transpose(2, 1, 0, 3)).reshape(nblk, 128, NK * 128)


def _gran_as(w):
    K_ = w.shape[0]
    nkg = K_ // 1024
    a = w.reshape(nkg, 8, 128, 8, 512)
    return np.ascontiguousarray(a.transpose(3, 0, 2, 1, 4)).reshape(8 * nkg, 128, 8 * 512)


def _gran_down(w):
    a = w.reshape(8, 2, 8, 128, 8, 512)
    return np.ascontiguousarray(a.transpose(0, 4, 1, 3, 2, 5)).reshape(128, 128, 8 * 512)


MIX_PERM = np.concatenate([np.concatenate([np.arange(g * 512, (g + 1) * 512), np.arange(2048 + g * 512, 2048 + (g + 1) * 512)])
                           for g in range(4)])


def p2_shared_inputs(inp):
    return {
        "xattn_gT": np.ascontiguousarray(inp["xattn_norm"][0].reshape(NK, 128).T),
        "mem_gT": np.ascontiguousarray(inp["mem_norm"][0].reshape(NK, 128).T),
        "mlp_gT": np.ascontiguousarray(inp["mlp_norm"][0].reshape(NK, 128).T),
        "final_gain": np.ascontiguousarray(inp["final_norm"].reshape(1, D)),
        "w_mo": _gran_as(inp["w_mix_out"][0][MIX_PERM]), "w_xq": _gran_ws(inp["w_xq"][0], 32), "w_xk": _gran_ws(inp["w_xk"][0], 32),
        "w_xv": _gran_as(inp["w_xv"][0]), "w_xo": _gran_as(inp["w_xo"][0]),
        "w_up": _gran_ws(inp["w_up"][0], 128), "w_down": _gran_down(inp["w_down"][0]),
    }


P2_SPECS = lambda Tc: {
    "x": ([Tc, D], F32), "mem": ([MEM, D], F32),
    "xattn_gT": ([128, NK], F32), "mem_gT": ([128, NK], F32), "mlp_gT": ([128, NK], F32), "final_gain": ([1, D], F32),
    "w_mo": ([32, 128, 4096], F32), "w_xq": ([32, 128, 4096], F32), "w_xk": ([32, 128, 4096], F32),
    "w_xv": ([32, 128, 4096], F32), "w_xo": ([32, 128, 4096], F32), "w_up": ([128, 128, 4096], F32),
    "w_down": ([128, 128, 4096], F32),
}


TT1 = 512
TT2 = 512
CH = 512


def build_p1(S, TT):
    nc = bass.Bass("TRN2", target_bir_lowering=False)
    io = {k: nc.dram_tensor(k, sh, dt, kind="ExternalInput").ap() for k, (sh, dt) in P1_SPECS(S).items()}
    io["mixed"] = nc.dram_tensor("mixed", [S, 1024], BF16, kind="ExternalOutput").ap()
    with ExitStack() as outer:
        phase1(Prog(nc, outer), S, TT, io)
    return nc


def build_p2(Tc, TT):
    nc = bass.Bass("TRN2", target_bir_lowering=False)
    io = {k: nc.dram_tensor(k, sh, dt, kind="ExternalInput").ap() for k, (sh, dt) in P2_SPECS(Tc).items()}
    io["mixed_in"] = nc.dram_tensor("mixed_in", [4, Tc, 1024], BF16, kind="ExternalInput").ap()
    io["out"] = nc.dram_tensor("out", [Tc, D], F32, kind="ExternalOutput").ap()
    io["kT_scr"] = nc.dram_tensor("kT_scr", [128, NK, 256], BF16).ap()
    io["v_scr"] = nc.dram_tensor("v_scr", [128, 2, D], BF16).ap()
    with ExitStack() as outer:
        phase2(Prog(nc, outer), Tc, TT, io)
    return nc


def build_fused(S, tt1, tt2):
    Tc = S // 4
    nc = bass.Bass("TRN2", target_bir_lowering=False)
    io1 = {k: nc.dram_tensor(k, sh, dt, kind="ExternalInput").ap() for k, (sh, dt) in P1_SPECS(S).items()}
    NCH = S // CH
    mixed_loc = [nc.dram_tensor(f"mixed_loc{k}", [CH, 1024], BF16) for k in range(NCH)]
    mixed_all = [nc.dram_tensor(f"mixed_all{k}", [4 * CH, 1024], BF16) for k in range(NCH)]
    io1["mixed"] = lambda r0: mixed_loc[r0 // CH][r0 % CH:r0 % CH + 128, :]
    io2 = {}
    for k, (sh, dt) in P2_SPECS(Tc).items():
        io2[k] = nc.dram_tensor("x2" if k == "x" else k, sh, dt, kind="ExternalInput").ap()
    io2["sel"] = nc.dram_tensor("sel", [1, 4], F32, kind="ExternalInput").ap()
    io2["out"] = nc.dram_tensor("out", [Tc, D], F32, kind="ExternalOutput").ap()
    io2["kT_scr"] = nc.dram_tensor("kT_scr", [128, NK, 256], BF16).ap()
    io2["v_scr"] = nc.dram_tensor("v_scr", [128, 2, D], BF16).ap()
    with ExitStack() as outer:
        p = Prog(nc, outer)
        ccsem = outer.enter_context(nc.semaphore("cc_sem"))
        p.sems["cc"] = ccsem
        assert tt1 == CH
        mlv = [p.view(f"mlv{k}", mixed_loc[k].ap()) for k in range(NCH)]
        io1["mixed_tiles"] = mlv

        def exchange(k):
            p._waits("pool", [mlv[k]], [])
            p.q["pool"].append(lambda eng, k=k: eng.collective_compute(
                "AllGather", ALU.bypass, replica_groups=[[0, 1, 2, 3], [4, 5, 6, 7]],
                ins=[mixed_loc[k].ap().opt()], outs=[mixed_all[k].ap().opt()]).then_inc(ccsem))
        io1["exchange"] = exchange
        phase1(p, S, tt1, io1)
        p.barrier()
        gat = p.view("mixed_all", mixed_all[0].ap())
        gat.lw = ("cc", NCH)
        phase2(p, Tc, tt2, io2, gathered=gat,
               gathered_rows=lambda g, row: mixed_all[row // CH][g * CH + row % CH: g * CH + row % CH + 128, :])
    return nc


_DBG = {}


def kernel(**inputs):
    inp = {k: np.asarray(v) for k, v in inputs.items()}
    B, S, _ = inp["x"].shape
    assert B == 2
    Tc = S // 4
    tt1, tt2 = min(TT1, S), min(TT2, Tc)
    nc = build_fused(S, tt1, tt2)
    shared = p2_shared_inputs(inp)
    maps = []
    for c in range(8):
        b, r = c // 4, c % 4
        m = p1_host_inputs(inp, b, r, S, tt1)
        m.update(shared)
        m["x2"] = np.ascontiguousarray(inp["x"][b, r * Tc:(r + 1) * Tc])
        m["mem"] = np.ascontiguousarray(inp["mem"][b])
        sel = np.zeros((1, 4), np.float32)
        sel[0, r] = 1.0
        m["sel"] = sel
        maps.append(m)
    res = run_bass_kernel_spmd(nc, maps, core_ids=list(range(8)))
    out = np.empty((B, S, D), np.float32)
    for c in range(8):
        b, r = c // 4, c % 4
        out[b, r * Tc:(r + 1) * Tc] = np.asarray(res.results[c]["out"])
    return out
```

```python
import numpy as np
from contextlib import ExitStack
import concourse.bass as bass
import concourse.mybir as mybir
from concourse.bass_utils import run_bass_kernel_spmd

F32 = mybir.dt.float32
BF16 = mybir.dt.bfloat16
I32 = mybir.dt.int32
AF = mybir.ActivationFunctionType
ALU = mybir.AluOpType
AX = mybir.AxisListType

D = 4096
NK = D // 128
MEM = 256
DFF = 4 * D
EPS = 1e-6
SEM_LIMIT = 30000

ENGS = ("pe", "act", "dve", "pool", "sp")


class T:
    __slots__ = ("name", "t", "lw", "rd", "dsem", "dcnt", "excl")

    def __init__(self, name, t, excl=False):
        self.name, self.t, self.lw, self.rd, self.dsem, self.dcnt, self.excl = name, t, None, {}, None, 0, excl

    def __getitem__(self, idx):
        return self.t[idx]


class Prog:
    def __init__(self, nc, semstack):
        self.nc, self.semstack, self.stack = nc, semstack, semstack
        self.q = {e: [] for e in ENGS}
        self.seen = {e: {} for e in ENGS}
        self.sems, self.cnt, self.cur, self.epoch = {}, {}, {}, {}
        for e in ENGS:
            self.epoch[e] = 0
            self._newkey(e)
        self.ndsem = 0
        self.ntile = 0
        self.dtiles = []

    def _newkey(self, e):
        key = f"{e}#{self.epoch[e]}"
        self.epoch[e] += 1
        self.sems[key] = self.semstack.enter_context(self.nc.semaphore("s_" + key.replace("#", "_")))
        self.cnt[key] = 0
        self.cur[e] = key

    def sb(self, name, shape, dt):
        self.ntile += 1
        return T(name, self.stack.enter_context(self.nc.sbuf_tensor(f"{name}_{self.ntile}", list(shape), dt)))

    def ps(self, name, shape, dt=F32):
        self.ntile += 1
        return T(name, self.stack.enter_context(self.nc.psum_tensor(f"{name}_{self.ntile}", list(shape), dt)), excl=True)

    def view(self, name, ap):
        return T(name, ap)

    def _dsem(self, tile):
        if tile.dsem is None:
            self.ndsem += 1
            key = f"d{self.ndsem}"
            self.sems[key] = self.semstack.enter_context(self.nc.semaphore("s_" + key))
            tile.dsem = key
            self.dtiles.append(tile)
        return tile.dsem

    def _waits(self, e, reads, writes):
        need = {}
        for t in reads:
            if t.lw is not None:
                k, v = t.lw
                if need.get(k, 0) < v:
                    need[k] = v
        for t in writes:
            if t.lw is not None:
                k, v = t.lw
                if need.get(k, 0) < v:
                    need[k] = v
            for k, v in t.rd.items():
                if need.get(k, 0) < v:
                    need[k] = v
        seen = self.seen[e]
        for k, v in need.items():
            if e == "pe" and k.startswith("pe#"):
                continue
            if seen.get(k, 0) >= v:
                continue
            seen[k] = v
            sem = self.sems[k]
            self.q[e].append(lambda eng, sem=sem, v=v: eng.wait_ge(sem, v))

    def op(self, e, fn, reads=(), writes=()):
        xr = [t for t in reads if t.excl and t not in writes]
        if xr:
            writes = list(writes) + xr
        self._waits(e, reads, writes)
        if self.cnt[self.cur[e]] >= SEM_LIMIT:
            self._newkey(e)
        key = self.cur[e]
        self.cnt[key] += 1
        n = self.cnt[key]
        sem = self.sems[key]
        self.q[e].append(lambda eng, fn=fn, sem=sem: fn(eng).then_inc(sem, 1))
        for t in writes:
            t.lw = (key, n)
            t.rd = {}
        for t in reads:
            if t not in writes:
                if t.rd.get(key, 0) < n:
                    t.rd[key] = n

    def dma(self, e, out, in_, reads=(), writes=(), semtile=None):
        self._waits(e, reads, writes)
        st = semtile if semtile is not None else (writes[0] if writes else reads[0])
        key = self._dsem(st)
        st.dcnt += 16
        v = st.dcnt
        sem = self.sems[key]
        self.q[e].append(lambda eng, out=out, in_=in_, sem=sem: eng.dma_start(out=out, in_=in_).then_inc(sem, 16))
        for t in writes:
            t.lw = (key, v)
            t.rd = {}
        for t in reads:
            if t not in writes:
                if t.rd.get(key, 0) < v:
                    t.rd[key] = v

    def finish(self):
        for t in self.dtiles:
            sem, v = self.sems[t.dsem], t.dcnt
            self.q["sp"].append(lambda eng, sem=sem, v=v: eng.wait_ge(sem, v))

    def barrier(self):
        final = {k: v for k, v in self.cnt.items() if v > 0}
        for t in self.dtiles:
            if t.dcnt > 0:
                final[t.dsem] = t.dcnt
        for e in ENGS:
            seen = self.seen[e]
            for k, v in final.items():
                if seen.get(k, 0) >= v:
                    continue
                seen[k] = v
                sem = self.sems[k]
                self.q[e].append(lambda eng, sem=sem, v=v: eng.wait_ge(sem, v))

    def run(self):
        q = self.q
        self.q = {e: [] for e in ENGS}
        self._emit(q)

    def _emit(self, q):
        self = type("Q", (), {"q": q, "nc": self.nc})()
        with self.nc.Block() as block:
            @block.tensor
            def _(eng):
                for f in self.q["pe"]:
                    f(eng)

            @block.scalar
            def _(eng):
                for f in self.q["act"]:
                    f(eng)

            @block.vector
            def _(eng):
                for f in self.q["dve"]:
                    f(eng)

            @block.gpsimd
            def _(eng):
                for f in self.q["pool"]:
                    f(eng)

            @block.sync
            def _(eng):
                for f in self.q["sp"]:
                    f(eng)


def mm(p, ot, oap, lt, lap, rt, rap, start=True, stop=True):
    p.op("pe", lambda e: e.matmul(oap, lhsT=lap, rhs=rap, start=start, stop=stop), reads=[lt, rt], writes=[ot])


def tr(p, ot, oap, it, iap, ident):
    p.op("pe", lambda e: e.transpose(oap, iap, ident[:]), reads=[it, ident], writes=[ot])


def ts(p, eng, ot, oap, it, iap, s1, s2, op0, op1=None, extra=()):
    if op1 is None:
        p.op(eng, lambda e: e.tensor_scalar(out=oap, in0=iap, scalar1=s1, scalar2=None, op0=op0), reads=[it, *extra], writes=[ot])
    else:
        p.op(eng, lambda e: e.tensor_scalar(out=oap, in0=iap, scalar1=s1, scalar2=s2, op0=op0, op1=op1), reads=[it, *extra], writes=[ot])


def tt(p, eng, ot, oap, at, aap, bt, bap, op):
    p.op(eng, lambda e: e.tensor_tensor(out=oap, in0=aap, in1=bap, op=op), reads=[at, bt], writes=[ot])


def stt(p, eng, ot, oap, at, aap, scalar, bt, bap, op0, op1, extra=()):
    p.op(eng, lambda e: e.scalar_tensor_tensor(out=oap, in0=aap, scalar=scalar, in1=bap, op0=op0, op1=op1),
         reads=[at, bt, *extra], writes=[ot])


def act(p, ot, oap, it, iap, func, bias=None, scale=None, accum=None, accum_t=None, extra=()):
    kw = {}
    if bias is not None:
        kw["bias"] = bias
    if scale is not None:
        kw["scale"] = scale
    if accum is not None:
        kw["accum_out"] = accum
    w = [ot] + ([accum_t] if accum_t is not None else [])
    p.op("act", lambda e: e.activation(out=oap, in_=iap, func=func, **kw), reads=[it, *extra], writes=w)


def cp(p, eng, ot, oap, it, iap):
    if eng == "act":
        p.op("act", lambda e: e.copy(out=oap, in_=iap), reads=[it], writes=[ot])
    else:
        p.op(eng, lambda e: e.tensor_copy(out=oap, in_=iap), reads=[it], writes=[ot])


def make_ident(p, dt, name="ident"):
    ident = p.sb(name, [128, 128], dt)
    p.op("pool", lambda e: e.memset(ident[:], 1.0), writes=[ident])
    p.op("pool", lambda e: e.affine_select(out=ident[:], in_=ident[:], pattern=[[-1, 128]], compare_op=ALU.is_equal,
                                           fill=0.0, base=0, channel_multiplier=1), reads=[ident], writes=[ident])
    return ident


class WStream:
    def __init__(self, p, nbuf, name="wg"):
        self.p = p
        self.slots = [p.sb(f"{name}{i}", [128, 4096], BF16) for i in range(nbuf)]
        self.i = 0

    def load(self, src_ap):
        t = self.slots[self.i % len(self.slots)]
        self.i += 1
        self.p.dma("pool", t[:], src_ap, writes=[t])
        return t


class Ring:
    def __init__(self, tiles):
        self.tiles, self.i = tiles, 0

    def next(self):
        t = self.tiles[self.i % len(self.tiles)]
        self.i += 1
        return t


def rstd_from_ss(p, ot, oap, it, iap, inv_n):
    ts(p, "dve", ot, oap, it, iap, float(inv_n), float(EPS), ALU.mult, ALU.add)
    act(p, ot, oap, ot, oap, AF.Sqrt)
    p.op("dve", lambda e: e.reciprocal(out=oap, in_=oap), reads=[ot], writes=[ot])


def norm_transpose(p, src_t, src_ap, gT, dstT, dst_col0, tm, junk_ss, ident, trps):
    ss, rs = junk_ss
    act(p, tm, tm[:], src_t, src_ap, AF.Square, accum=ss[:, 0:1], accum_t=ss)
    rstd_from_ss(p, rs, rs[:, 0:1], ss, ss[:, 0:1], 1.0 / D)
    ts(p, "dve", tm, tm[:], src_t, src_ap, rs[:, 0:1], None, ALU.mult, extra=[rs])
    for kb in range(4):
        pt = trps.next()
        for kk in range(8):
            k = kb * 8 + kk
            tr(p, pt, pt[:, kk * 128:(kk + 1) * 128], tm, tm[:, k * 128:(k + 1) * 128], ident)
        p.op("dve", lambda e, pt=pt, kb=kb: e.tensor_tensor(
            out=dstT[:, kb * 8:(kb + 1) * 8, dst_col0:dst_col0 + 128],
            in0=pt[:].rearrange("p (k t) -> p k t", k=8),
            in1=gT[:, kb * 8:(kb + 1) * 8].unsqueeze(2).to_broadcast([128, 8, 128]), op=ALU.mult),
            reads=[pt, gT], writes=[dstT])


def phase1(p, S, TT, io):
    NT = S // TT
    NS = TT // 128
    with ExitStack() as st:
        p.stack = st
        identb = make_ident(p, BF16, "identb")
        identf = make_ident(p, F32, "identf")

        def cload(name, shape, dt, src):
            t = p.sb(name, shape, dt)
            p.dma("sp", t[:], src, writes=[t])
            return t
        gT = cload("gT", [128, NK], F32, io["mix_gT"])
        invf = cload("invf", [128, 1], F32, io["inv_freq"])
        rmaskT = cload("rmaskT", [128, 2, 128], F32, io["ret_maskT"])
        rdq = cload("rdq", [128, 2, 128], F32, io["ret_dq"])
        rdk = cload("rdk", [128, 2], F32, io["ret_dk"])
        rdc = cload("rdc", [128, 2], F32, io["ret_dc"])
        rgain = cload("rgain", [128, 512], F32, io["ret_gain"].partition_broadcast(128))
        ggain = cload("ggain", [128, 128], F32, io["gdn_gain"].partition_broadcast(128))
        convw = cload("convw", [128, 12, 4], F32, io["convw"])
        alog = cload("alog", [128, 4], F32, io["a_log"].partition_broadcast(128))
        dtb = cload("dtb", [128, 4], F32, io["dt_bias"].partition_broadcast(128))
        tri = cload("tri", [128, 128], F32, io["tri"])
        bones = cload("bones", [128, 128], F32, io["bones"])
        ones = cload("ones", [128, 128], F32, io["ones"])
        mstrict = cload("mstrict", [128, 128], F32, io["mstrict"])
        minclT = cload("minclT", [128, 128], F32, io["minclT"])
        wab = cload("wabf", [128, NK, 8], F32, io["w_ab"])
        wabb = p.sb("wabb", [128, NK, 8], BF16)
        cp(p, "dve", wabb, wabb[:], wab, wab[:])
        aneg = p.sb("aneg", [128, 4], F32)
        act(p, aneg, aneg[:], alog, alog[:], AF.Exp)
        ts(p, "dve", aneg, aneg[:], aneg, aneg[:], -1.0, None, ALU.mult)

        ws = WStream(p, 3)
        xt = p.sb("xt", [128, D], F32)
        tm = p.sb("tm", [128, D], BF16)
        ss = p.sb("ss", [128, 1], F32)
        rs = p.sb("rs", [128, 1], F32)
        hnT = p.sb("hnT", [128, NK, TT], BF16)
        pfr = p.sb("pfr", [128, 8, TT], BF16)
        pfg = p.sb("pfg", [128, 12, 3 + TT], BF16)
        p.op("dve", lambda e: e.memset(pfg[:, :, 0:3], 0.0), writes=[pfg])
        ptm = [p.sb(f"ptm{s}", [128, 1536], BF16) for s in range(NS)]
        pab = [p.sb(f"pab{s}", [128, 8], F32) for s in range(NS)]

        def ring(name, shape, dt, n):
            return Ring([p.sb(f"{name}{i}", shape, dt) for i in range(n)])
        posi = p.sb("posi", [128, TT], I32)
        r_f = ring("f128w", [128, 128], F32, 6)
        r_ki = ring("ki", [128, 128], I32, 1)
        r_sc = ring("sincos", [128, 2, 128], F32, 2)
        r_rq = ring("rq", [128, 2, 2, 128], BF16, 2)
        r_rqd = ring("rqd", [128, 2, 2, 128], BF16, 2)
        r_rk = ring("rk", [128, 2, 2, 128], BF16, 2)
        r_gq = ring("gq", [128, 4, 3, 128], BF16, 2)
        rstate = [[p.sb(f"rstate{h}{d}", [128, 256], F32) for d in range(2)] for h in range(2)]
        rstateb = [[p.sb(f"rstateb{h}{d}", [128, 256], BF16) for d in range(2)] for h in range(2)]
        gstate = [p.sb(f"gstate{j}", [128, 128], F32) for j in range(4)]
        gstateb = [p.sb(f"gstateb{j}", [128, 128], BF16) for j in range(4)]
        for h in range(2):
            for d in range(2):
                p.op("dve", lambda e, h=h, d=d: e.memset(rstate[h][d][:], 0.0), writes=[rstate[h][d]])
                p.op("dve", lambda e, h=h, d=d: e.memset(rstateb[h][d][:], 0.0), writes=[rstateb[h][d]])
        for j in range(4):
            p.op("dve", lambda e, j=j: e.memset(gstate[j][:], 0.0), writes=[gstate[j]])
            p.op("dve", lambda e, j=j: e.memset(gstateb[j][:], 0.0), writes=[gstateb[j]])
        mixo = ring("mixo", [128, 1024], BF16, 2)
        pacc = Ring([p.ps(f"pacc{i}", [128, 512], F32) for i in range(2)])
        ptr = Ring([p.ps(f"ptr{i}", [128, 1024], BF16) for i in range(2)])
        pmx = Ring([p.ps(f"pmx{i}", [128, 512], F32) for i in range(4)])
        r_osum = ring("osum", [128, 8], F32, 2)
        g_ab = ring("gab", [128, 40], F32, 2)
        rr = [dict(sT=ring(f"sT{h}", [128, 128], BF16, 2), ktm=ring(f"ktm{h}", [128, 256], BF16, 1),
                   of=ring(f"of{h}", [128, 256], F32, 1), sg=ring(f"sg{h}", [128, 256], F32, 1),
                   junk=ring(f"rjunk{h}", [128, 256], F32, 1), sm=ring(f"rsm{h}", [128, 8], F32, 2)) for h in range(2)]
        gr = [dict(raw=ring(f"raw{j}", [128, 384], BF16, 1), junk=ring(f"gjunk{j}", [128, 256], F32, 1),
                   sm=ring(f"gsm{j}", [128, 8], F32, 4), b=ring(f"gb{j}", [128, 128], BF16, 10),
                   fT=ring(f"fT{j}", [128, 384], BF16, 1), E=ring(f"gE{j}", [128, 128], F32, 5),
                   P=ring(f"gP{j}", [128, 128], F32, 7)) for j in range(4)]

        w_fm, w_tm = io["w_fm"], io["w_tm"]
        x, pos, mixed = io["x"], io["pos"], io["mixed"]
        mixed_rows = mixed if callable(mixed) else (lambda r0: mixed[r0:r0 + 128, :])

        def reduce_sin(dst_t, dst_ap, ang, shift):
            src = ang
            if shift != 0.0:
                sh = r_f.next()
                ts(p, "dve", sh, sh[:], ang, ang[:], float(shift), None, ALU.add)
                src = sh
            ki = r_ki.next()
            kf = r_f.next()
            t2 = r_f.next()
            ts(p, "dve", ki, ki[:], src, src[:], float(1 / (2 * np.pi)), None, ALU.mult)
            cp(p, "dve", kf, kf[:], ki, ki[:])
            stt(p, "dve", t2, t2[:], kf, kf[:], float(-2 * np.pi), src, src[:], ALU.mult, ALU.add)
            ts(p, "dve", kf, kf[:], t2, t2[:], float(np.pi), float(-2 * np.pi), ALU.is_gt, ALU.mult)
            tt(p, "dve", t2, t2[:], t2, t2[:], kf, kf[:], ALU.add)
            ts(p, "dve", kf, kf[:], t2, t2[:], float(-np.pi), float(2 * np.pi), ALU.is_lt, ALU.mult)
            tt(p, "dve", t2, t2[:], t2, t2[:], kf, kf[:], ALU.add)
            ts(p, "dve", t2, t2[:], t2, t2[:], float(-np.pi), float(np.pi), ALU.max, ALU.min)
            act(p, dst_t, dst_ap, t2, t2[:], AF.Sin)

        def norm_chain(tile, s):
            r0 = tile * TT + s * 128
            p.dma("sp", xt[:], x[r0:r0 + 128, :], writes=[xt])
            yield
            act(p, tm, tm[:], xt, xt[:], AF.Square, accum=ss[:, 0:1], accum_t=ss)
            yield
            act(p, rs, rs[:, 0:1], ss, ss[:, 0:1], AF.Ln, bias=float(EPS), scale=1.0 / D)
            act(p, rs, rs[:, 0:1], rs, rs[:, 0:1], AF.Exp, scale=-0.5)
            yield
            ts(p, "dve", tm, tm[:], xt, xt[:], rs[:, 0:1], None, ALU.mult, extra=[rs])
            yield
            for kb in range(4):
                pt_ = ptr.next()
                for kk in range(8):
                    k = kb * 8 + kk
                    tr(p, pt_, pt_[:, kk * 128:(kk + 1) * 128], tm, tm[:, k * 128:(k + 1) * 128], identb)
                p.op("dve", lambda e, pt_=pt_, kb=kb: e.tensor_tensor(
                    out=hnT[:, kb * 8:(kb + 1) * 8, s * 128:(s + 1) * 128],
                    in0=pt_[:].rearrange("p (k t) -> p k t", k=8),
                    in1=gT[:, kb * 8:(kb + 1) * 8].unsqueeze(2).to_broadcast([128, 8, 128]), op=ALU.mult),
                    reads=[pt_, gT], writes=[hnT])
                yield

        for ti in range(NT):
            tok0 = ti * TT
            if ti == 0:
                for s in range(NS):
                    for _ in norm_chain(0, s):
                        pass
            if ti > 0:
                p.op("dve", lambda e: e.tensor_copy(out=pfg[:, :, 0:3], in_=pfg[:, :, TT:TT + 3]), reads=[pfg], writes=[pfg])
            p.dma("sp", posi[:], pos[0:1, tok0:tok0 + TT].partition_broadcast(128), writes=[posi])
            for blk in range(20):
                wt = ws.load(w_fm[blk])
                wv = wt[:].rearrange("p (k c) -> p k c", k=NK)
                for c0 in range(0, TT, 512):
                    cw = min(512, TT - c0)
                    ps = pacc.next()
                    for k in range(NK):
                        mm(p, ps, ps[:, 0:cw], wt, wv[:, k, :], hnT, hnT[:, k, c0:c0 + cw], start=(k == 0), stop=(k == NK - 1))
                    if blk < 8:
                        cp(p, "act", pfr, pfr[:, blk, c0:c0 + cw], ps, ps[:, 0:cw])
                    else:
                        cp(p, "act", pfg, pfg[:, blk - 8, 3 + c0:3 + c0 + cw], ps, ps[:, 0:cw])
            for n in range(3):
                pss = [pmx.next() for _ in range(NS)]
                for kg in range(4):
                    wt = ws.load(w_tm[n * 4 + kg])
                    wv = wt[:].rearrange("p (k c) -> p k c", k=8)
                    for s in range(NS):
                        for k2 in range(8):
                            k = kg * 8 + k2
                            mm(p, pss[s], pss[s][:], hnT, hnT[:, k, s * 128:(s + 1) * 128], wt, wv[:, k2, :],
                               start=(k == 0), stop=(k == NK - 1))
                for s in range(NS):
                    cp(p, "act" if s % 2 == 0 else "dve", ptm[s], ptm[s][:, n * 512:(n + 1) * 512], pss[s], pss[s][:])
            for s in range(NS):
                ps = pacc.next()
                for k in range(NK):
                    mm(p, ps, ps[:, 0:8], hnT, hnT[:, k, s * 128:(s + 1) * 128], wabb, wabb[:, k, :],
                       start=(k == 0), stop=(k == NK - 1))
                cp(p, "act", pab[s], pab[s][:], ps, ps[:, 0:8])

            for s in range(NS):
                c0 = s * 128
                mo = mixo.next()
                pt = ptm[s]
                ang = r_f.next()
                cp(p, "dve", ang, ang[:], posi, posi[:, c0:c0 + 128])
                ts(p, "dve", ang, ang[:], ang, ang[:], invf[:, 0:1], None, ALU.mult, extra=[invf])
                sc = r_sc.next()
                reduce_sin(sc, sc[:, 0, :], ang, 0.0)
                reduce_sin(sc, sc[:, 1, :], ang, np.pi / 2)
                sinT, cosT = sc[:, 0, :], sc[:, 1, :]
                rq, rqd, rk = r_rq.next(), r_rqd.next(), r_rk.next()
                for h in range(2):
                    for qk, dst in ((0, rq), (1, rk)):
                        x1 = pfr[:, h * 4 + qk * 2 + 0, c0:c0 + 128]
                        x2 = pfr[:, h * 4 + qk * 2 + 1, c0:c0 + 128]
                        t1, t2 = r_f.next(), r_f.next()
                        tt(p, "dve", t1, t1[:], pfr, x1, sc, cosT, ALU.mult)
                        tt(p, "dve", t2, t2[:], pfr, x2, sc, sinT, ALU.mult)
                        tt(p, "dve", dst, dst[:, h, 0, :], t1, t1[:], t2, t2[:], ALU.subtract)
                        t1, t2 = r_f.next(), r_f.next()
                        tt(p, "dve", t1, t1[:], pfr, x2, sc, cosT, ALU.mult)
                        tt(p, "dve", t2, t2[:], pfr, x1, sc, sinT, ALU.mult)
                        tt(p, "dve", dst, dst[:, h, 1, :], t1, t1[:], t2, t2[:], ALU.add)
                    for half in range(2):
                        tt(p, "dve", rqd, rqd[:, h, half, :], rq, rq[:, h, half, :], rdq, rdq[:, h, :], ALU.mult)
                gq = r_gq.next()
                for j in range(4):
                    for c in range(3):
                        b = j * 3 + c
                        t1 = r_f.next()
                        ts(p, "dve", t1, t1[:], pfg, pfg[:, b, c0 + 3:c0 + 3 + 128], convw[:, b, 3:4], None, ALU.mult, extra=[convw])
                        for w in (2, 1, 0):
                            stt(p, "dve", t1, t1[:], pfg, pfg[:, b, c0 + w:c0 + w + 128], convw[:, b, w:w + 1], t1, t1[:],
                                ALU.mult, ALU.add, extra=[convw])
                        act(p, gq, gq[:, j, c, :], t1, t1[:], AF.Silu)

                def ret_head(h, mo=mo, pt=pt, rq=rq, rqd=rqd, rk=rk):
                    R = rr[h]
                    ptk = ptr.next()
                    for half in range(2):
                        tr(p, ptk, ptk[:, half * 128:(half + 1) * 128], rk, rk[:, h, half, :], identb)
                    ktm = R["ktm"].next()
                    ts(p, "dve", ktm, ktm[:], ptk, ptk[:, 0:256], rdk[:, h:h + 1], None, ALU.mult, extra=[rdk])
                    psc = pmx.next()
                    for half in range(2):
                        mm(p, psc, psc[:, 0:128], rk, rk[:, h, half, :], rq, rq[:, h, half, :], start=(half == 0), stop=(half == 1))
                    sT = R["sT"].next()
                    tt(p, "dve", sT, sT[:], psc, psc[:, 0:128], rmaskT, rmaskT[:, h, :], ALU.mult)
                    yield
                    vap = pt[:, h * 256:(h + 1) * 256]
                    po = pmx.next()
                    mm(p, po, po[:, 0:256], sT, sT[:], pt, vap, start=True, stop=False)
                    for half in range(2):
                        mm(p, po, po[:, 0:256], rqd, rqd[:, h, half, :], rstateb[h][half], rstateb[h][half][:],
                           start=False, stop=(half == 1))
                    of = R["of"].next()
                    cp(p, "act", of, of[:], po, po[:, 0:256])
                    yield
                    for half in range(2):
                        pst = pmx.next()
                        mm(p, pst, pst[:, 0:256], ktm, ktm[:, half * 128:(half + 1) * 128], pt, vap)
                        stt(p, "dve", rstate[h][half], rstate[h][half][:], rstate[h][half], rstate[h][half][:], rdc[:, h:h + 1],
                            pst, pst[:, 0:256], ALU.mult, ALU.add, extra=[rdc])
                        cp(p, "act", rstateb[h][half], rstateb[h][half][:], rstate[h][half], rstate[h][half][:])
                        yield
                    junk = R["junk"].next()
                    sq = R["sm"].next()
                    act(p, junk, junk[:], of, of[:], AF.Square, accum=sq[:, 0:1], accum_t=sq)
                    sg = R["sg"].next()
                    act(p, sg, sg[:], pt, pt[:, 512 + h * 256: 512 + (h + 1) * 256], AF.Silu)
                    yield
                    act(p, sq, sq[:, 1:2], sq, sq[:, 0:1], AF.Ln, bias=float(EPS), scale=1.0 / 256)
                    act(p, sq, sq[:, 1:2], sq, sq[:, 1:2], AF.Exp, scale=-0.5)
                    tt(p, "dve", sg, sg[:], sg, sg[:], rgain, rgain[:, h * 256:(h + 1) * 256], ALU.mult)
                    yield
                    stt(p, "dve", mo, mo[:, h * 256:(h + 1) * 256], of, of[:], sq[:, 1:2], sg, sg[:], ALU.mult, ALU.mult, extra=[sq])

                ab = g_ab.next()
                pb8 = pab[s]
                tt(p, "dve", ab, ab[:, 24:28], pb8, pb8[:, 0:4], dtb, dtb[:], ALU.add)
                stt(p, "dve", ab, ab[:, 0:4], ab, ab[:, 24:28], -1.0, ab, ab[:, 24:28], ALU.mult, ALU.max)
                act(p, ab, ab[:, 0:4], ab, ab[:, 0:4], AF.Exp, scale=-1.0)
                act(p, ab, ab[:, 0:4], ab, ab[:, 0:4], AF.Ln, bias=1.0)
                stt(p, "dve", ab, ab[:, 0:4], ab, ab[:, 24:28], 0.0, ab, ab[:, 0:4], ALU.max, ALU.add)
                tt(p, "dve", ab, ab[:, 0:4], ab, ab[:, 0:4], aneg, aneg[:], ALU.mult)
                act(p, ab, ab[:, 4:8], pb8, pb8[:, 4:8], AF.Sigmoid)
                ts(p, "dve", ab, ab[:, 28:32], ab, ab[:, 4:8], -1.0, None, ALU.mult)
                pg = pmx.next()
                mm(p, pg, pg[:, 0:4], tri, tri[:], ab, ab[:, 0:4])
                mm(p, pg, pg[:, 4:8], bones, bones[:], ab, ab[:, 0:4])
                cp(p, "dve", ab, ab[:, 8:16], pg, pg[:, 0:8])
                act(p, ab, ab[:, 16:20], ab, ab[:, 8:12], AF.Exp)
                tt(p, "dve", ab, ab[:, 24:28], ab, ab[:, 12:16], ab, ab[:, 8:12], ALU.subtract)
                act(p, ab, ab[:, 20:24], ab, ab[:, 24:28], AF.Exp)
                tt(p, "dve", ab, ab[:, 32:36], ab, ab[:, 4:8], ab, ab[:, 16:20], ALU.mult)
                osum = r_osum.next()

                def gdn_head(j, mo=mo, pt=pt, gq=gq, ab=ab, osum=osum):
                    G = gr[j]
                    pq = ptr.next()
                    for c in range(3):
                        tr(p, pq, pq[:, c * 128:(c + 1) * 128], gq, gq[:, j, c, :], identb)
                    raw = G["raw"].next()
                    cp(p, "act", raw, raw[:], pq, pq[:, 0:384])
                    yield
                    nq = G["sm"].next()
                    junk = G["junk"].next()
                    act(p, junk, junk[:, 0:128], raw, raw[:, 0:128], AF.Square, accum=nq[:, 0:1], accum_t=nq)
                    act(p, junk, junk[:, 128:256], raw, raw[:, 128:256], AF.Square, accum=nq[:, 1:2], accum_t=nq)
                    yield
                    act(p, nq, nq[:, 2:4], nq, nq[:, 0:2], AF.Ln, bias=float(EPS))
                    act(p, nq, nq[:, 2:4], nq, nq[:, 2:4], AF.Exp, scale=-0.5)
                    yield
                    ts(p, "dve", nq, nq[:, 4:5], nq, nq[:, 2:3], float(128 ** -0.5), None, ALU.mult)
                    tt(p, "dve", nq, nq[:, 5:6], nq, nq[:, 4:5], ab, ab[:, 16 + j:17 + j], ALU.mult)
                    tt(p, "dve", nq, nq[:, 6:7], nq, nq[:, 3:4], ab, ab[:, 32 + j:33 + j], ALU.mult)
                    tt(p, "dve", nq, nq[:, 7:8], nq, nq[:, 3:4], ab, ab[:, 20 + j:21 + j], ALU.mult)
                    qh, qd, kh, kbg, kdc, vbt = (G["b"].next() for _ in range(6))
                    ts(p, "dve", qh, qh[:], raw, raw[:, 0:128], nq[:, 4:5], None, ALU.mult, extra=[nq])
                    ts(p, "dve", qd, qd[:], raw, raw[:, 0:128], nq[:, 5:6], None, ALU.mult, extra=[nq])
                    ts(p, "dve", kh, kh[:], raw, raw[:, 128:256], nq[:, 3:4], None, ALU.mult, extra=[nq])
                    yield
                    ts(p, "dve", kbg, kbg[:], raw, raw[:, 128:256], nq[:, 6:7], None, ALU.mult, extra=[nq])
                    ts(p, "dve", kdc, kdc[:], raw, raw[:, 128:256], nq[:, 7:8], None, ALU.mult, extra=[nq])
                    ts(p, "dve", vbt, vbt[:], raw, raw[:, 256:384], ab[:, 4 + j:5 + j], None, ALU.mult, extra=[ab])
                    pb = ptr.next()
                    tr(p, pb, pb[:, 0:128], qh, qh[:], identb)
                    tr(p, pb, pb[:, 128:256], qd, qd[:], identb)
                    tr(p, pb, pb[:, 256:384], kh, kh[:], identb)
                    fT = G["fT"].next()
                    cp(p, "act", fT, fT[:], pb, pb[:, 0:384])
                    qhT, qdT, khT = fT[:, 0:128], fT[:, 128:256], fT[:, 256:384]
                    yield
                    trl = G["E"].next()
                    ts(p, "dve", trl, trl[:], tri, tri[:], ab[:, j:j + 1], None, ALU.mult, extra=[ab])
                    yield
                    pgb = pmx.next()
                    mm(p, pgb, pgb[:, 0:128], ones, ones[:], trl, trl[:])
                    E = G["E"].next()
                    ts(p, "dve", E, E[:], pgb, pgb[:, 0:128], ab[:, 8 + j:9 + j], 0.0, ALU.subtract, ALU.max, extra=[ab])
                    ET = G["E"].next()
                    ts(p, "dve", ET, ET[:], pgb, pgb[:, 0:128], ab[:, 8 + j:9 + j], 0.0, ALU.subtract, ALU.min, extra=[ab])
                    egl = G["sm"].next()
                    act(p, egl, egl[:, 0:1], pgb, pgb[:, 63:64], AF.Exp)
                    act(p, egl, egl[:, 1:2], pgb, pgb[:, 127:128], AF.Exp)
                    yield
                    act(p, E, E[:], E, E[:], AF.Exp, scale=-1.0)
                    act(p, ET, ET[:], ET, ET[:], AF.Exp)
                    yield
                    tt(p, "dve", E, E[:], E, E[:], mstrict, mstrict[:], ALU.mult)
                    tt(p, "dve", ET, ET[:], ET, ET[:], minclT, minclT[:], ALU.mult)
                    pqk = pmx.next()
                    mm(p, pqk, pqk[:, 0:128], fT, khT, fT, qhT)
                    qkT = G["b"].next()
                    tt(p, "dve", qkT, qkT[:], pqk, pqk[:, 0:128], ET, ET[:], ALU.mult)
                    yield
                    pkk = pmx.next()
                    mm(p, pkk, pkk[:, 0:128], fT, khT, fT, khT)
                    N = G["P"].next()
                    stt(p, "dve", N, N[:], pkk, pkk[:, 0:128], ab[:, 28 + j:29 + j], E, E[:], ALU.mult, ALU.mult, extra=[ab])
                    yield
                    pnt = pmx.next()
                    mm(p, pnt, pnt[:, 0:128], N, N[:], identf, identf[:])
                    NTt = G["P"].next()
                    cp(p, "act", NTt, NTt[:], pnt, pnt[:, 0:128])
                    yield
                    Q = G["P"].next()
                    tt(p, "dve", Q, Q[:], NTt, NTt[:], identf, identf[:], ALU.add)
                    P, PT = N, NTt
                    pend = None
                    for it in range(5):
                        pp = pmx.next()
                        mm(p, pp, pp[:, 0:128], PT, PT[:], P, P[:])
                        mm(p, pp, pp[:, 128:256], P, P[:], PT, PT[:])
                        P2, P2T = G["P"].next(), G["P"].next()
                        cp(p, "act", P2, P2[:], pp, pp[:, 0:128])
                        cp(p, "dve", P2T, P2T[:], pp, pp[:, 128:256])
                        if pend is not None:
                            pqq = pmx.next()
                            mm(p, pqq, pqq[:, 0:128], pend, pend[:], Q, Q[:])
                            Q2 = G["P"].next()
                            tt(p, "dve", Q2, Q2[:], pqq, pqq[:, 0:128], Q, Q[:], ALU.add)
                            Q = Q2
                        yield
                        pend = P2
                        P, PT = P2, P2T
                    pqq = pmx.next()
                    mm(p, pqq, pqq[:, 0:128], pend, pend[:], Q, Q[:])
                    Q2 = G["P"].next()
                    tt(p, "dve", Q2, Q2[:], pqq, pqq[:, 0:128], Q, Q[:], ALU.add)
                    Q = Q2
                    yield
                    TiT = G["b"].next()
                    cp(p, "act", TiT, TiT[:], Q, Q[:])
                    yield
                    pw = pmx.next()
                    mm(p, pw, pw[:, 0:128], kbg, kbg[:], TiT, TiT[:])
                    nwT = G["b"].next()
                    ts(p, "dve", nwT, nwT[:], pw, pw[:, 0:128], -1.0, None, ALU.mult)
                    yield
                    vn = G["b"].next()
                    oc = G["E"].next()
                    for cx in range(2):
                        r0, r1 = cx * 64, cx * 64 + 64
                        pv = pmx.next()
                        mm(p, pv, pv[:, 0:128], TiT, TiT[:], vbt, vbt[:], start=True, stop=False)
                        mm(p, pv, pv[:, 0:128], nwT, nwT[:], gstateb[j], gstateb[j][:], start=False, stop=True)
                        cp(p, "act", vn, vn[r0:r1, :], pv, pv[r0:r1, 0:128])
                        yield
                        po = pmx.next()
                        mm(p, po, po[:, 0:128], fT, qdT, gstateb[j], gstateb[j][:], start=True, stop=False)
                        mm(p, po, po[:, 0:128], qkT, qkT[r0:r1, :], vn, vn[r0:r1, :], start=False, stop=True)
                        cp(p, "act", oc, oc[r0:r1, :], po, po[r0:r1, 0:128])
                        pst = pmx.next()
                        mm(p, pst, pst[:, 0:128], kdc, kdc[r0:r1, :], vn, vn[r0:r1, :])
                        stt(p, "dve", gstate[j], gstate[j][:], gstate[j], gstate[j][:], egl[:, cx:cx + 1], pst, pst[:, 0:128],
                            ALU.mult, ALU.add, extra=[egl])
                        yield
                        cp(p, "act", gstateb[j], gstateb[j][:], gstate[j], gstate[j][:])
                        yield
                    junk2 = G["junk"].next()
                    act(p, junk2, junk2[:, 0:128], oc, oc[:], AF.Square, accum=osum[:, j:j + 1], accum_t=osum)
                    sgz = G["E"].next()
                    act(p, sgz, sgz[:], pt, pt[:, 1024 + j * 128: 1024 + (j + 1) * 128], AF.Silu)
                    yield
                    act(p, osum, osum[:, 4 + j:5 + j], osum, osum[:, j:j + 1], AF.Ln, bias=float(EPS), scale=1.0 / 128)
                    act(p, osum, osum[:, 4 + j:5 + j], osum, osum[:, 4 + j:5 + j], AF.Exp, scale=-0.5)
                    tt(p, "dve", sgz, sgz[:], sgz, sgz[:], ggain, ggain[:], ALU.mult)
                    yield
                    stt(p, "dve", mo, mo[:, 512 + j * 128: 512 + (j + 1) * 128], oc, oc[:], osum[:, 4 + j:5 + j], sgz, sgz[:],
                        ALU.mult, ALU.mult, extra=[osum])

                chains = [gdn_head(0), ret_head(0), gdn_head(1), gdn_head(2), ret_head(1), gdn_head(3)]
                if ti + 1 < NT:
                    chains.append(norm_chain(ti + 1, s))
                while chains:
                    for ch in list(chains):
                        try:
                            next(ch)
                        except StopIteration:
                            chains.remove(ch)
                if "mixed_tiles" in io:
                    p.dma("sp", mixed_rows(tok0 + c0), mo[:], reads=[mo], writes=[io["mixed_tiles"][ti]], semtile=mo)
                else:
                    p.dma("sp", mixed_rows(tok0 + c0), mo[:], reads=[mo])
            if "exchange" in io:
                io["exchange"](ti)
        p.finish()
        p.run()


OFF = dict(rq=0, rk=2048, rv=4096, rg=6144, gq=8192, gk=10240, gv=12288, gz=14336, ga=16384, gb=16400)


def _granules_ws(w, cols):
    sub = w[:, cols]
    return np.ascontiguousarray(sub.reshape(NK, 128, len(cols)).transpose(1, 0, 2)).reshape(128, NK * len(cols))


def p1_host_inputs(inp, b, g, S, TT):
    w = inp["w_in"][0]
    fm_cols = []
    for h in range(2):
        hr = 2 * g + h
        fm_cols += list(range(OFF["rq"] + hr * 256, OFF["rq"] + hr * 256 + 256))
        fm_cols += list(range(OFF["rk"] + hr * 256, OFF["rk"] + hr * 256 + 256))
    for j in range(4):
        hg = 4 * g + j
        for nm in ("gq", "gk", "gv"):
            fm_cols += list(range(OFF[nm] + hg * 128, OFF[nm] + hg * 128 + 128))
    tm_cols = []
    for nm, width, nh in (("rv", 256, 2), ("rg", 256, 2), ("gz", 128, 4)):
        for h in range(nh):
            hh = (2 * g + h) if nh == 2 else (4 * g + h)
            tm_cols += list(range(OFF[nm] + hh * width, OFF[nm] + hh * width + width))
    ab_cols = [OFF["ga"] + 4 * g + j for j in range(4)] + [OFF["gb"] + 4 * g + j for j in range(4)]
    w_fm = np.stack([_granules_ws(w, fm_cols[i * 128:(i + 1) * 128]) for i in range(20)])
    wt_ = w[:, tm_cols].reshape(4, 8, 128, 3, 512)
    w_tm = np.ascontiguousarray(wt_.transpose(3, 0, 2, 1, 4)).reshape(12, 128, 4096)
    w_ab = np.ascontiguousarray(w[:, ab_cols].reshape(NK, 128, 8).transpose(1, 0, 2))
    conv = inp["gdn_conv"][0]
    convw = np.zeros((128, 12, 4), np.float32)
    for j in range(4):
        hg = 4 * g + j
        for c in range(3):
            convw[:, j * 3 + c, :] = conv[:, c * 2048 + hg * 128: c * 2048 + hg * 128 + 128].T
    f32 = np.float32
    hs = np.array([2 * g, 2 * g + 1], f32)
    lg = np.log1p(-np.exp2(-5.0 - hs)).astype(f32)
    idx = np.arange(128, dtype=f32)
    rel = idx[None, :] - idx[:, None]
    maskT = np.where(rel >= 0, np.exp(np.maximum(rel, 0)[None] * lg[:, None, None]), 0.0).astype(f32)
    ret_maskT = np.ascontiguousarray((maskT * f32(1 / 16)).transpose(1, 0, 2))
    dq = np.exp((idx + 1.0)[None, :] * lg[:, None]).astype(f32)
    ret_dq = np.ascontiguousarray(np.broadcast_to(dq[None], (128, 2, 128))).astype(f32)
    dk = np.exp((127.0 - idx)[None, :] * lg[:, None]).astype(f32) * f32(1 / 16)
    ret_dk = np.ascontiguousarray(dk.T)
    ret_dc = np.ascontiguousarray(np.broadcast_to(np.exp(128.0 * lg)[None], (128, 2))).astype(f32)
    inv_freq = (10000.0 ** (-np.arange(128, dtype=f32) / f32(128))).astype(f32).reshape(128, 1)
    blk = (np.arange(128)[:, None] // 64) == (np.arange(128)[None, :] // 64)
    ii, jj = np.arange(128)[:, None], np.arange(128)[None, :]
    tri = (blk & (ii <= jj)).astype(f32)
    bones = blk.astype(f32)
    mstrict = (blk & (ii > jj)).astype(f32)
    minclT = (blk & (jj >= ii)).astype(f32)
    d = {
        "x": np.ascontiguousarray(inp["x"][b]),
        "pos": np.ascontiguousarray(inp["positions"][b][None, :]).astype(np.int32),
        "mix_gT": np.ascontiguousarray(inp["mix_norm"][0].reshape(NK, 128).T),
        "w_fm": w_fm, "w_tm": w_tm, "w_ab": w_ab, "convw": convw,
        "ret_gain": np.ascontiguousarray(inp["ret_norm"][0][2 * g:2 * g + 2].reshape(1, 512)),
        "gdn_gain": np.ascontiguousarray(inp["gdn_norm"][0].reshape(1, 128)),
        "a_log": np.ascontiguousarray(inp["gdn_a_log"][0][4 * g:4 * g + 4].reshape(1, 4)),
        "dt_bias": np.ascontiguousarray(inp["gdn_dt_bias"][0][4 * g:4 * g + 4].reshape(1, 4)),
        "inv_freq": inv_freq, "ret_maskT": ret_maskT, "ret_dq": ret_dq, "ret_dk": ret_dk, "ret_dc": ret_dc,
        "tri": tri, "bones": bones, "ones": np.ones((128, 128), f32), "mstrict": mstrict, "minclT": minclT,
    }
    return d


P1_SPECS = lambda S: {
    "x": ([S, D], F32), "pos": ([1, S], I32), "mix_gT": ([128, NK], F32),
    "w_fm": ([20, 128, 4096], F32), "w_tm": ([12, 128, 4096], F32), "w_ab": ([128, NK, 8], F32),
    "convw": ([128, 12, 4], F32), "ret_gain": ([1, 512], F32), "gdn_gain": ([1, 128], F32),
    "a_log": ([1, 4], F32), "dt_bias": ([1, 4], F32), "inv_freq": ([128, 1], F32),
    "ret_maskT": ([128, 2, 128], F32), "ret_dq": ([128, 2, 128], F32), "ret_dk": ([128, 2], F32),
    "ret_dc": ([128, 2], F32), "tri": ([128, 128], F32), "bones": ([128, 128], F32), "ones": ([128, 128], F32),
    "mstrict": ([128, 128], F32), "minclT": ([128, 128], F32),
}


def transpose_rows(p, tm, dstT, col0, ident, trps, gT=None, eng_alt=("dve", "act")):
    for kb in range(4):
        pt = trps.next()
        for kk in range(8):
            k = kb * 8 + kk
            tr(p, pt, pt[:, kk * 128:(kk + 1) * 128], tm, tm[:, k * 128:(k + 1) * 128], ident)
        if gT is not None:
            p.op("dve", lambda e, pt=pt, kb=kb: e.tensor_tensor(
                out=dstT[:, kb * 8:(kb + 1) * 8, col0:col0 + 128],
                in0=pt[:].rearrange("p (k t) -> p k t", k=8),
                in1=gT[:, kb * 8:(kb + 1) * 8].unsqueeze(2).to_broadcast([128, 8, 128]), op=ALU.mult),
                reads=[pt, gT], writes=[dstT])
        else:
            cp(p, eng_alt[kb % 2], dstT, dstT[:, kb * 8:(kb + 1) * 8, col0:col0 + 128], pt, pt[:].rearrange("p (k t) -> p k t", k=8))


def phase2(p, Tc, TT, io, gathered=None, gathered_rows=None):
    NT = Tc // TT
    NS = TT // 128
    with ExitStack() as st:
        p.stack = st
        identb = make_ident(p, BF16, "identb")

        def cload(name, shape, dt, src):
            t = p.sb(name, shape, dt)
            p.dma("sp", t[:], src, writes=[t])
            return t
        gx = cload("gx", [128, NK], F32, io["xattn_gT"])
        gm = cload("gm", [128, NK], F32, io["mem_gT"])
        gl = cload("gl", [128, NK], F32, io["mlp_gT"])

        ws = WStream(p, 3)
        h = p.sb("h", [128, NS, D], F32)
        hs = [p.view(f"h{s}", h.t[:, s, :]) for s in range(NS)]
        TB = max(TT, MEM)
        bufA = p.sb("bufA", [128, NK, TB], BF16)
        bufB = p.sb("bufB", [128, NK, TB], BF16)
        aTh = [p.view(f"aT{i}", bufB.t[:, i * 16:(i + 1) * 16, :]) for i in range(2)]
        tm = p.sb("tm", [128, D], BF16)
        ss = p.sb("ss", [128, 1], F32)
        rs = p.sb("rs", [128, 1], F32)
        gfin = p.sb("gfin", [128, 2048], F32)
        kTr = Ring([p.sb(f"kTh{i}", [128, 8, 256], BF16) for i in range(1)])
        vr = Ring([p.sb(f"vh{i}", [128, 2, 1024], BF16) for i in range(1)])
        if gathered is not None:
            sel = cload("sel", [128, 4], F32, io["sel"].partition_broadcast(128))
            cand = p.sb("cand", [128, 4, 1024], BF16)
        er = Ring([p.sb(f"e{i}", [128, 256], F32) for i in range(4)])
        pbr = Ring([p.sb(f"pb{i}", [128, 256], BF16) for i in range(4)])
        pTr = Ring([p.sb(f"pT{i}", [128, 2, TT], BF16) for i in range(2)])
        smr = Ring([p.sb(f"sm{i}", [128, 4], F32) for i in range(8)])
        rlr = Ring([p.sb(f"rl{i}", [128, 512], F32) for i in range(1)])
        vst = Ring([p.sb(f"vst{i}", [128, 512], BF16) for i in range(2)])
        pacc = Ring([p.ps(f"pacc{i}", [128, 512], F32) for i in range(6)])
        ptr = Ring([p.ps(f"ptr{i}", [128, 1024], BF16) for i in range(2)])

        x, mixed_in, mem, out = io["x"], io.get("mixed_in"), io["mem"], io["out"]
        kT_scr = p.view("kT_scr", io["kT_scr"])
        v_scr = p.view("v_scr", io["v_scr"])
        evi = [0]

        def evac_copy(ot, oap, ps, pap, scale=None):
            evi[0] += 1
            if evi[0] % 2:
                if scale is None:
                    cp(p, "act", ot, oap, ps, pap)
                else:
                    p.op("act", lambda e: e.mul(out=oap, in_=pap, mul=float(scale)), reads=[ps], writes=[ot])
            else:
                if scale is None:
                    cp(p, "dve", ot, oap, ps, pap)
                else:
                    ts(p, "dve", ot, oap, ps, pap, float(scale), None, ALU.mult)

        def rms_rows(src_t, src_ap):
            act(p, tm, tm[:], src_t, src_ap, AF.Square, accum=ss[:, 0:1], accum_t=ss)
            rstd_from_ss(p, rs, rs[:, 0:1], ss, ss[:, 0:1], 1.0 / D)
            ts(p, "dve", tm, tm[:], src_t, src_ap, rs[:, 0:1], None, ALU.mult, extra=[rs])

        def linear_as(actT, wsrc, n_kg, kpg, nblocks, sink):
            for n in range(nblocks):
                pss = [pacc.next() for _ in range(NS)]
                for kg in range(n_kg):
                    wt = ws.load(wsrc(n, kg))
                    wv = wt[:].rearrange("p (k c) -> p k c", k=kpg)
                    for s in range(NS):
                        for k2 in range(kpg):
                            k = kg * kpg + k2
                            mm(p, pss[s], pss[s][:], actT[0], actT[1](k, s), wt, wv[:, k2, :],
                               start=(k == 0), stop=(k == n_kg * kpg - 1))
                for s in range(NS):
                    sink(n, s, pss[s])

        def add_to_h(n, s, ps):
            tt(p, "dve", hs[s], hs[s][:, n * 512:(n + 1) * 512], ps, ps[:], hs[s], hs[s][:, n * 512:(n + 1) * 512], ALU.add)

        for mt in range(2):
            p.dma("sp", hs[0][:], mem[mt * 128:(mt + 1) * 128, :], writes=[hs[0]])
            rms_rows(hs[0], hs[0][:])
            transpose_rows(p, tm, bufA, mt * 128, identb, ptr, gT=gm)
        kst = bufB[:].rearrange("p k t -> p (k t)")[:, 0:NK * MEM].rearrange("p (k m) -> p k m", k=NK)
        for cb in range(NK):
            wt = ws.load(io["w_xk"][cb])
            wv = wt[:].rearrange("p (k c) -> p k c", k=NK)
            ps = pacc.next()
            for k in range(NK):
                mm(p, ps, ps[:, 0:256], wt, wv[:, k, :], bufA, bufA[:, k, 0:256], start=(k == 0), stop=(k == NK - 1))
            evac_copy(bufB, kst[:, cb, :], ps, ps[:, 0:256])
        p.dma("sp", kT_scr[:], kst, reads=[bufB], writes=[kT_scr])
        for n in range(8):
            pss = [pacc.next() for _ in range(2)]
            for kg in range(4):
                wt = ws.load(io["w_xv"][n * 4 + kg])
                wv = wt[:].rearrange("p (k c) -> p k c", k=8)
                for mt in range(2):
                    for k2 in range(8):
                        k = kg * 8 + k2
                        mm(p, pss[mt], pss[mt][:], bufA, bufA[:, k, mt * 128:(mt + 1) * 128], wt, wv[:, k2, :],
                           start=(k == 0), stop=(k == NK - 1))
            for mt in range(2):
                vs = vst.next()
                evac_copy(vs, vs[:], pss[mt], pss[mt][:])
                p.dma("sp", v_scr[:, mt, n * 512:(n + 1) * 512], vs[:], reads=[vs], writes=[v_scr])

        for ti in range(NT):
            tok0 = ti * TT
            for s in range(NS):
                r0 = tok0 + s * 128
                p.dma("sp", hs[s][:], x[r0:r0 + 128, :], writes=[hs[s]])
                for g in range(4):
                    if gathered is None:
                        p.dma("sp", tm[:, g * 1024:(g + 1) * 1024], mixed_in[g, r0:r0 + 128, :], writes=[tm])
                    else:
                        for q in range(4):
                            p.dma("sp", cand[:, q, :], gathered_rows(g, q * Tc + r0), reads=[gathered], writes=[cand])
                        dst = tm[:, g * 1024:(g + 1) * 1024]
                        ts(p, "dve", tm, dst, cand, cand[:, 0, :], sel[:, 0:1], None, ALU.mult, extra=[sel])
                        for q in range(1, 4):
                            stt(p, "dve", tm, dst, cand, cand[:, q, :], sel[:, q:q + 1], tm, dst, ALU.mult, ALU.add, extra=[sel])
                transpose_rows(p, tm, bufA, s * 128, identb, ptr)
            linear_as((bufA, lambda k, s: bufA[:, k, s * 128:(s + 1) * 128]), lambda n, kg: io["w_mo"][n * 4 + kg], 4, 8, 8, add_to_h)
            for s in range(NS):
                rms_rows(hs[s], hs[s][:])
                transpose_rows(p, tm, bufA, s * 128, identb, ptr, gT=gx)
            for cb in range(NK):
                wt = ws.load(io["w_xq"][cb])
                wv = wt[:].rearrange("p (k c) -> p k c", k=NK)
                ps = pacc.next()
                for k in range(NK):
                    mm(p, ps, ps[:, 0:TT], wt, wv[:, k, :], bufA, bufA[:, k, 0:TT], start=(k == 0), stop=(k == NK - 1))
                evac_copy(bufB, bufB[:, cb, 0:TT], ps, ps[:, 0:TT], scale=1.0 / 32.0)
            for hd in range(4):
                kTh, vh = kTr.next(), vr.next()
                p.dma("sp", kTh[:], kT_scr[:, hd * 8:(hd + 1) * 8, :], reads=[kT_scr], writes=[kTh])
                p.dma("sp", vh[:], v_scr[:, :, hd * 1024:(hd + 1) * 1024], reads=[v_scr], writes=[vh])
                pT = pTr.next()
                def soft_chain(s, hd=hd, kTh=kTh, pT=pT):
                    psc = pacc.next()
                    for c in range(8):
                        mm(p, psc, psc[:, 0:256], bufB, bufB[:, hd * 8 + c, s * 128:(s + 1) * 128], kTh, kTh[:, c, :],
                           start=(c == 0), stop=(c == 7))
                    sm = smr.next()
                    p.op("dve", lambda e, psc=psc, sm=sm: e.reduce_max(out=sm[:, 0:1], in_=psc[:, 0:256], axis=AX.X), reads=[psc], writes=[sm])
                    yield
                    ts(p, "dve", sm, sm[:, 1:2], sm, sm[:, 0:1], -1.0, None, ALU.mult)
                    ee = er.next()
                    act(p, ee, ee[:], psc, psc[:, 0:256], AF.Exp, bias=sm[:, 1:2], accum=sm[:, 2:3], accum_t=sm, extra=[sm])
                    yield
                    p.op("dve", lambda e, sm=sm: e.reciprocal(out=sm[:, 3:4], in_=sm[:, 2:3]), reads=[sm], writes=[sm])
                    pb = pbr.next()
                    ts(p, "dve", pb, pb[:], ee, ee[:], sm[:, 3:4], None, ALU.mult, extra=[sm])
                    yield
                    ptp = ptr.next()
                    for mt in range(2):
                        tr(p, ptp, ptp[:, mt * 128:(mt + 1) * 128], pb, pb[:, mt * 128:(mt + 1) * 128], identb)
                    cp(p, "act", pT, pT[:, :, s * 128:(s + 1) * 128], ptp, ptp[:, 0:256].rearrange("p (m t) -> p m t", m=2))
                chains = [soft_chain(s) for s in range(NS)]
                while chains:
                    for ch in list(chains):
                        try:
                            next(ch)
                        except StopIteration:
                            chains.remove(ch)
                for cb in range(8):
                    ps = pacc.next()
                    for mt in range(2):
                        mm(p, ps, ps[:, 0:TT], vh, vh[:, mt, cb * 128:(cb + 1) * 128], pT, pT[:, mt, :], start=(mt == 0), stop=(mt == 1))
                    evac_copy(bufA, bufA[:, hd * 8 + cb, 0:TT], ps, ps[:, 0:TT])
            linear_as((bufA, lambda k, s: bufA[:, k, s * 128:(s + 1) * 128]), lambda n, kg: io["w_xo"][n * 4 + kg], 4, 8, 8, add_to_h)
            for s in range(NS):
                rms_rows(hs[s], hs[s][:])
                transpose_rows(p, tm, bufA, s * 128, identb, ptr, gT=gl)

            def up_group(G):
                aT = aTh[G % 2]
                for cb in range(16):
                    wt = ws.load(io["w_up"][G * 16 + cb])
                    wv = wt[:].rearrange("p (k c) -> p k c", k=NK)
                    ps = pacc.next()
                    for k in range(NK):
                        mm(p, ps, ps[:, 0:TT], wt, wv[:, k, :], bufA, bufA[:, k, 0:TT], start=(k == 0), stop=(k == NK - 1))
                    rl = rlr.next()
                    act(p, rl, rl[:, 0:TT], ps, ps[:, 0:TT], AF.Relu)
                    tt(p, "dve", aT, aT[:, cb, 0:TT], rl, rl[:, 0:TT], rl, rl[:, 0:TT], ALU.mult)

            def down_group(G):
                aT = aTh[G % 2]
                linear_as((aT, lambda k, s: aT[:, k, s * 128:(s + 1) * 128]),
                          lambda n, kg: io["w_down"][(G * 8 + n) * 2 + kg], 2, 8, 8, add_to_h)
            NG = DFF // 2048
            up_group(0)
            for G in range(NG):
                if G + 1 < NG:
                    up_group(G + 1)
                down_group(G)
            for s in range(NS):
                r0 = tok0 + s * 128
                act(p, tm, tm[:], hs[s], hs[s][:], AF.Square, accum=ss[:, 0:1], accum_t=ss)
                rstd_from_ss(p, rs, rs[:, 0:1], ss, ss[:, 0:1], 1.0 / D)
                for hf in range(2):
                    p.dma("sp", gfin[:], io["final_gain"][0:1, hf * 2048:(hf + 1) * 2048].partition_broadcast(128), writes=[gfin])
                    stt(p, "dve", hs[s], hs[s][:, hf * 2048:(hf + 1) * 2048], hs[s], hs[s][:, hf * 2048:(hf + 1) * 2048], rs[:, 0:1],
                        gfin, gfin[:], ALU.mult, ALU.mult, extra=[rs])
                p.dma("sp", out[r0:r0 + 128, :], hs[s][:], reads=[hs[s]])
        p.finish()
        p.run()


def _gran_ws(w, nblk):
    K_, N_ = w.shape
    return np.ascontiguousarray(w.reshape(NK, 128, nblk, 128).transpose(2, 1, 0, 3)).reshape(nblk, 128, NK * 128)


def _gran_as(w):
    K_ = w.shape[0]
    nkg = K_ // 1024
    a = w.reshape(nkg, 8, 128, 8, 512)
    return np.ascontiguousarray(a.transpose(3, 0, 2, 1, 4)).reshape(8 * nkg, 128, 8 * 512)


def _gran_down(w):
    a = w.reshape(8, 2, 8, 128, 8, 512)
    return np.ascontiguousarray(a.transpose(0, 4, 1, 3, 2, 5)).reshape(128, 128, 8 * 512)


MIX_PERM = np.concatenate([np.concatenate([np.arange(g * 512, (g + 1) * 512), np.arange(2048 + g * 512, 2048 + (g + 1) * 512)])
                           for g in range(4)])


def p2_shared_inputs(inp):
    return {
        "xattn_gT": np.ascontiguousarray(inp["xattn_norm"][0].reshape(NK, 128).T),
        "mem_gT": np.ascontiguousarray(inp["mem_norm"][0].reshape(NK, 128).T),
        "mlp_gT": np.ascontiguousarray(inp["mlp_norm"][0].reshape(NK, 128).T),
        "final_gain": np.ascontiguousarray(inp["final_norm"].reshape(1, D)),
        "w_mo": _gran_as(inp["w_mix_out"][0][MIX_PERM]), "w_xq": _gran_ws(inp["w_xq"][0], 32), "w_xk": _gran_ws(inp["w_xk"][0], 32),
        "w_xv": _gran_as(inp["w_xv"][0]), "w_xo": _gran_as(inp["w_xo"][0]),
        "w_up": _gran_ws(inp["w_up"][0], 128), "w_down": _gran_down(inp["w_down"][0]),
    }


P2_SPECS = lambda Tc: {
    "x": ([Tc, D], F32), "mem": ([MEM, D], F32),
    "xattn_gT": ([128, NK], F32), "mem_gT": ([128, NK], F32), "mlp_gT": ([128, NK], F32), "final_gain": ([1, D], F32),
    "w_mo": ([32, 128, 4096], F32), "w_xq": ([32, 128, 4096], F32), "w_xk": ([32, 128, 4096], F32),
    "w_xv": ([32, 128, 4096], F32), "w_xo": ([32, 128, 4096], F32), "w_up": ([128, 128, 4096], F32),
    "w_down": ([128, 128, 4096], F32),
}


TT1 = 512
TT2 = 512
CH = 512


def build_p1(S, TT):
    nc = bass.Bass("TRN2", target_bir_lowering=False)
    io = {k: nc.dram_tensor(k, sh, dt, kind="ExternalInput").ap() for k, (sh, dt) in P1_SPECS(S).items()}
    io["mixed"] = nc.dram_tensor("mixed", [S, 1024], BF16, kind="ExternalOutput").ap()
    with ExitStack() as outer:
        phase1(Prog(nc, outer), S, TT, io)
    return nc


def build_p2(Tc, TT):
    nc = bass.Bass("TRN2", target_bir_lowering=False)
    io = {k: nc.dram_tensor(k, sh, dt, kind="ExternalInput").ap() for k, (sh, dt) in P2_SPECS(Tc).items()}
    io["mixed_in"] = nc.dram_tensor("mixed_in", [4, Tc, 1024], BF16, kind="ExternalInput").ap()
    io["out"] = nc.dram_tensor("out", [Tc, D], F32, kind="ExternalOutput").ap()
    io["kT_scr"] = nc.dram_tensor("kT_scr", [128, NK, 256], BF16).ap()
    io["v_scr"] = nc.dram_tensor("v_scr", [128, 2, D], BF16).ap()
    with ExitStack() as outer:
        phase2(Prog(nc, outer), Tc, TT, io)
    return nc


def build_fused(S, tt1, tt2):
    Tc = S // 4
    nc = bass.Bass("TRN2", target_bir_lowering=False)
    io1 = {k: nc.dram_tensor(k, sh, dt, kind="ExternalInput").ap() for k, (sh, dt) in P1_SPECS(S).items()}
    NCH = S // CH
    mixed_loc = [nc.dram_tensor(f"mixed_loc{k}", [CH, 1024], BF16) for k in range(NCH)]
    mixed_all = [nc.dram_tensor(f"mixed_all{k}", [4 * CH, 1024], BF16) for k in range(NCH)]
    io1["mixed"] = lambda r0: mixed_loc[r0 // CH][r0 % CH:r0 % CH + 128, :]
    io2 = {}
    for k, (sh, dt) in P2_SPECS(Tc).items():
        io2[k] = nc.dram_tensor("x2" if k == "x" else k, sh, dt, kind="ExternalInput").ap()
    io2["sel"] = nc.dram_tensor("sel", [1, 4], F32, kind="ExternalInput").ap()
    io2["out"] = nc.dram_tensor("out", [Tc, D], F32, kind="ExternalOutput").ap()
    io2["kT_scr"] = nc.dram_tensor("kT_scr", [128, NK, 256], BF16).ap()
    io2["v_scr"] = nc.dram_tensor("v_scr", [128, 2, D], BF16).ap()
    with ExitStack() as outer:
        p = Prog(nc, outer)
        ccsem = outer.enter_context(nc.semaphore("cc_sem"))
        p.sems["cc"] = ccsem
        assert tt1 == CH
        mlv = [p.view(f"mlv{k}", mixed_loc[k].ap()) for k in range(NCH)]
        io1["mixed_tiles"] = mlv

        def exchange(k):
            p._waits("pool", [mlv[k]], [])
            p.q["pool"].append(lambda eng, k=k: eng.collective_compute(
                "AllGather", ALU.bypass, replica_groups=[[0, 1, 2, 3], [4, 5, 6, 7]],
                ins=[mixed_loc[k].ap().opt()], outs=[mixed_all[k].ap().opt()]).then_inc(ccsem))
        io1["exchange"] = exchange
        phase1(p, S, tt1, io1)
        p.barrier()
        gat = p.view("mixed_all", mixed_all[0].ap())
        gat.lw = ("cc", NCH)
        phase2(p, Tc, tt2, io2, gathered=gat,
               gathered_rows=lambda g, row: mixed_all[row // CH][g * CH + row % CH: g * CH + row % CH + 128, :])
    return nc


_DBG = {}


def kernel(**inputs):
    inp = {k: np.asarray(v) for k, v in inputs.items()}
    B, S, _ = inp["x"].shape
    assert B == 2
    Tc = S // 4
    tt1, tt2 = min(TT1, S), min(TT2, Tc)
    nc = build_fused(S, tt1, tt2)
    shared = p2_shared_inputs(inp)
    maps = []
    for c in range(8):
        b, r = c // 4, c % 4
        m = p1_host_inputs(inp, b, r, S, tt1)
        m.update(shared)
        m["x2"] = np.ascontiguousarray(inp["x"][b, r * Tc:(r + 1) * Tc])
        m["mem"] = np.ascontiguousarray(inp["mem"][b])
        sel = np.zeros((1, 4), np.float32)
        sel[0, r] = 1.0
        m["sel"] = sel
        maps.append(m)
    res = run_bass_kernel_spmd(nc, maps, core_ids=list(range(8)))
    out = np.empty((B, S, D), np.float32)
    for c in range(8):
        b, r = c // 4, c % 4
        out[b, r * Tc:(r + 1) * Tc] = np.asarray(res.results[c]["out"])
    return out
```

```python
import numpy as np
from contextlib import ExitStack
import concourse.bass as bass
import concourse.mybir as mybir
from concourse.bass_utils import run_bass_kernel_spmd

F32 = mybir.dt.float32
BF16 = mybir.dt.bfloat16
I32 = mybir.dt.int32
AF = mybir.ActivationFunctionType
ALU = mybir.AluOpType
AX = mybir.AxisListType

D = 4096
NK = D // 128
MEM = 256
DFF = 4 * D
EPS = 1e-6
SEM_LIMIT = 30000

ENGS = ("pe", "act", "dve", "pool", "sp")


class T:
    __slots__ = ("name", "t", "lw", "rd", "dsem", "dcnt", "excl")

    def __init__(self, name, t, excl=False):
        self.name, self.t, self.lw, self.rd, self.dsem, self.dcnt, self.excl = name, t, None, {}, None, 0, excl

    def __getitem__(self, idx):
        return self.t[idx]


class Prog:
    def __init__(self, nc, semstack):
        self.nc, self.semstack, self.stack = nc, semstack, semstack
        self.q = {e: [] for e in ENGS}
        self.seen = {e: {} for e in ENGS}
        self.sems, self.cnt, self.cur, self.epoch = {}, {}, {}, {}
        for e in ENGS:
            self.epoch[e] = 0
            self._newkey(e)
        self.ndsem = 0
        self.ntile = 0
        self.dtiles = []

    def _newkey(self, e):
        key = f"{e}#{self.epoch[e]}"
        self.epoch[e] += 1
        self.sems[key] = self.semstack.enter_context(self.nc.semaphore("s_" + key.replace("#", "_")))
        self.cnt[key] = 0
        self.cur[e] = key

    def sb(self, name, shape, dt):
        self.ntile += 1
        return T(name, self.stack.enter_context(self.nc.sbuf_tensor(f"{name}_{self.ntile}", list(shape), dt)))

    def ps(self, name, shape, dt=F32):
        self.ntile += 1
        return T(name, self.stack.enter_context(self.nc.psum_tensor(f"{name}_{self.ntile}", list(shape), dt)), excl=True)

    def view(self, name, ap):
        return T(name, ap)

    def _dsem(self, tile):
        if tile.dsem is None:
            self.ndsem += 1
            key = f"d{self.ndsem}"
            self.sems[key] = self.semstack.enter_context(self.nc.semaphore("s_" + key))
            tile.dsem = key
            self.dtiles.append(tile)
        return tile.dsem

    def _waits(self, e, reads, writes):
        need = {}
        for t in reads:
            if t.lw is not None:
                k, v = t.lw
                if need.get(k, 0) < v:
                    need[k] = v
        for t in writes:
            if t.lw is not None:
                k, v = t.lw
                if need.get(k, 0) < v:
                    need[k] = v
            for k, v in t.rd.items():
                if need.get(k, 0) < v:
                    need[k] = v
        seen = self.seen[e]
        for k, v in need.items():
            if e == "pe" and k.startswith("pe#"):
                continue
            if seen.get(k, 0) >= v:
                continue
            seen[k] = v
            sem = self.sems[k]
            self.q[e].append(lambda eng, sem=sem, v=v: eng.wait_ge(sem, v))

    def op(self, e, fn, reads=(), writes=()):
        xr = [t for t in reads if t.excl and t not in writes]
        if xr:
            writes = list(writes) + xr
        self._waits(e, reads, writes)
        if self.cnt[self.cur[e]] >= SEM_LIMIT:
            self._newkey(e)
        key = self.cur[e]
        self.cnt[key] += 1
        n = self.cnt[key]
        sem = self.sems[key]
        self.q[e].append(lambda eng, fn=fn, sem=sem: fn(eng).then_inc(sem, 1))
        for t in writes:
            t.lw = (key, n)
            t.rd = {}
        for t in reads:
            if t not in writes:
                if t.rd.get(key, 0) < n:
                    t.rd[key] = n

    def dma(self, e, out, in_, reads=(), writes=(), semtile=None):
        self._waits(e, reads, writes)
        st = semtile if semtile is not None else (writes[0] if writes else reads[0])
        key = self._dsem(st)
        st.dcnt += 16
        v = st.dcnt
        sem = self.sems[key]
        self.q[e].append(lambda eng, out=out, in_=in_, sem=sem: eng.dma_start(out=out, in_=in_).then_inc(sem, 16))
        for t in writes:
            t.lw = (key, v)
            t.rd = {}
        for t in reads:
            if t not in writes:
                if t.rd.get(key, 0) < v:
                    t.rd[key] = v

    def finish(self):
        for t in self.dtiles:
            sem, v = self.sems[t.dsem], t.dcnt
            self.q["sp"].append(lambda eng, sem=sem, v=v: eng.wait_ge(sem, v))

    def barrier(self):
        final = {k: v for k, v in self.cnt.items() if v > 0}
        for t in self.dtiles:
            if t.dcnt > 0:
                final[t.dsem] = t.dcnt
        for e in ENGS:
            seen = self.seen[e]
            for k, v in final.items():
                if seen.get(k, 0) >= v:
                    continue
                seen[k] = v
                sem = self.sems[k]
                self.q[e].append(lambda eng, sem=sem, v=v: eng.wait_ge(sem, v))

    def run(self):
        q = self.q
        self.q = {e: [] for e in ENGS}
        self._emit(q)

    def _emit(self, q):
        self = type("Q", (), {"q": q, "nc": self.nc})()
        with self.nc.Block() as block:
            @block.tensor
            def _(eng):
                for f in self.q["pe"]:
                    f(eng)

            @block.scalar
            def _(eng):
                for f in self.q["act"]:
                    f(eng)

            @block.vector
            def _(eng):
                for f in self.q["dve"]:
                    f(eng)

            @block.gpsimd
            def _(eng):
                for f in self.q["pool"]:
                    f(eng)

            @block.sync
            def _(eng):
                for f in self.q["sp"]:
                    f(eng)


def mm(p, ot, oap, lt, lap, rt, rap, start=True, stop=True):
    p.op("pe", lambda e: e.matmul(oap, lhsT=lap, rhs=rap, start=start, stop=stop), reads=[lt, rt], writes=[ot])


def tr(p, ot, oap, it, iap, ident):
    p.op("pe", lambda e: e.transpose(oap, iap, ident[:]), reads=[it, ident], writes=[ot])


def ts(p, eng, ot, oap, it, iap, s1, s2, op0, op1=None, extra=()):
    if op1 is None:
        p.op(eng, lambda e: e.tensor_scalar(out=oap, in0=iap, scalar1=s1, scalar2=None, op0=op0), reads=[it, *extra], writes=[ot])
    else:
        p.op(eng, lambda e: e.tensor_scalar(out=oap, in0=iap, scalar1=s1, scalar2=s2, op0=op0, op1=op1), reads=[it, *extra], writes=[ot])


def tt(p, eng, ot, oap, at, aap, bt, bap, op):
    p.op(eng, lambda e: e.tensor_tensor(out=oap, in0=aap, in1=bap, op=op), reads=[at, bt], writes=[ot])


def stt(p, eng, ot, oap, at, aap, scalar, bt, bap, op0, op1, extra=()):
    p.op(eng, lambda e: e.scalar_tensor_tensor(out=oap, in0=aap, scalar=scalar, in1=bap, op0=op0, op1=op1),
         reads=[at, bt, *extra], writes=[ot])


def act(p, ot, oap, it, iap, func, bias=None, scale=None, accum=None, accum_t=None, extra=()):
    kw = {}
    if bias is not None:
        kw["bias"] = bias
    if scale is not None:
        kw["scale"] = scale
    if accum is not None:
        kw["accum_out"] = accum
    w = [ot] + ([accum_t] if accum_t is not None else [])
    p.op("act", lambda e: e.activation(out=oap, in_=iap, func=func, **kw), reads=[it, *extra], writes=w)


def cp(p, eng, ot, oap, it, iap):
    if eng == "act":
        p.op("act", lambda e: e.copy(out=oap, in_=iap), reads=[it], writes=[ot])
    else:
        p.op(eng, lambda e: e.tensor_copy(out=oap, in_=iap), reads=[it], writes=[ot])


def make_ident(p, dt, name="ident"):
    ident = p.sb(name, [128, 128], dt)
    p.op("pool", lambda e: e.memset(ident[:], 1.0), writes=[ident])
    p.op("pool", lambda e: e.affine_select(out=ident[:], in_=ident[:], pattern=[[-1, 128]], compare_op=ALU.is_equal,
                                           fill=0.0, base=0, channel_multiplier=1), reads=[ident], writes=[ident])
    return ident


class WStream:
    def __init__(self, p, nbuf, name="wg"):
        self.p = p
        self.slots = [p.sb(f"{name}{i}", [128, 4096], BF16) for i in range(nbuf)]
        self.i = 0

    def load(self, src_ap):
        t = self.slots[self.i % len(self.slots)]
        self.i += 1
        self.p.dma("pool", t[:], src_ap, writes=[t])
        return t


class Ring:
    def __init__(self, tiles):
        self.tiles, self.i = tiles, 0

    def next(self):
        t = self.tiles[self.i % len(self.tiles)]
        self.i += 1
        return t


def rstd_from_ss(p, ot, oap, it, iap, inv_n):
    ts(p, "dve", ot, oap, it, iap, float(inv_n), float(EPS), ALU.mult, ALU.add)
    act(p, ot, oap, ot, oap, AF.Sqrt)
    p.op("dve", lambda e: e.reciprocal(out=oap, in_=oap), reads=[ot], writes=[ot])


def norm_transpose(p, src_t, src_ap, gT, dstT, dst_col0, tm, junk_ss, ident, trps):
    ss, rs = junk_ss
    act(p, tm, tm[:], src_t, src_ap, AF.Square, accum=ss[:, 0:1], accum_t=ss)
    rstd_from_ss(p, rs, rs[:, 0:1], ss, ss[:, 0:1], 1.0 / D)
    ts(p, "dve", tm, tm[:], src_t, src_ap, rs[:, 0:1], None, ALU.mult, extra=[rs])
    for kb in range(4):
        pt = trps.next()
        for kk in range(8):
            k = kb * 8 + kk
            tr(p, pt, pt[:, kk * 128:(kk + 1) * 128], tm, tm[:, k * 128:(k + 1) * 128], ident)
        p.op("dve", lambda e, pt=pt, kb=kb: e.tensor_tensor(
            out=dstT[:, kb * 8:(kb + 1) * 8, dst_col0:dst_col0 + 128],
            in0=pt[:].rearrange("p (k t) -> p k t", k=8),
            in1=gT[:, kb * 8:(kb + 1) * 8].unsqueeze(2).to_broadcast([128, 8, 128]), op=ALU.mult),
            reads=[pt, gT], writes=[dstT])


def phase1(p, S, TT, io):
    NT = S // TT
    NS = TT // 128
    with ExitStack() as st:
        p.stack = st
        identb = make_ident(p, BF16, "identb")
        identf = make_ident(p, F32, "identf")

        def cload(name, shape, dt, src):
            t = p.sb(name, shape, dt)
            p.dma("sp", t[:], src, writes=[t])
            return t
        gT = cload("gT", [128, NK], F32, io["mix_gT"])
        invf = cload("invf", [128, 1], F32, io["inv_freq"])
        rmaskT = cload("rmaskT", [128, 2, 128], F32, io["ret_maskT"])
        rdq = cload("rdq", [128, 2, 128], F32, io["ret_dq"])
        rdk = cload("rdk", [128, 2], F32, io["ret_dk"])
        rdc = cload("rdc", [128, 2], F32, io["ret_dc"])
        rgain = cload("rgain", [128, 512], F32, io["ret_gain"].partition_broadcast(128))
        ggain = cload("ggain", [128, 128], F32, io["gdn_gain"].partition_broadcast(128))
        convw = cload("convw", [128, 12, 4], F32, io["convw"])
        alog = cload("alog", [128, 4], F32, io["a_log"].partition_broadcast(128))
        dtb = cload("dtb", [128, 4], F32, io["dt_bias"].partition_broadcast(128))
        tri = cload("tri", [128, 128], F32, io["tri"])
        bones = cload("bones", [128, 128], F32, io["bones"])
        ones = cload("ones", [128, 128], F32, io["ones"])
        mstrict = cload("mstrict", [128, 128], F32, io["mstrict"])
        minclT = cload("minclT", [128, 128], F32, io["minclT"])
        wab = cload("wabf", [128, NK, 8], F32, io["w_ab"])
        wabb = p.sb("wabb", [128, NK, 8], BF16)
        cp(p, "dve", wabb, wabb[:], wab, wab[:])
        aneg = p.sb("aneg", [128, 4], F32)
        act(p, aneg, aneg[:], alog, alog[:], AF.Exp)
        ts(p, "dve", aneg, aneg[:], aneg, aneg[:], -1.0, None, ALU.mult)

        ws = WStream(p, 3)
        xt = p.sb("xt", [128, D], F32)
        tm = p.sb("tm", [128, D], BF16)
        ss = p.sb("ss", [128, 1], F32)
        rs = p.sb("rs", [128, 1], F32)
        hnT = p.sb("hnT", [128, NK, TT], BF16)
        pfr = p.sb("pfr", [128, 8, TT], BF16)
        pfg = p.sb("pfg", [128, 12, 3 + TT], BF16)
        p.op("dve", lambda e: e.memset(pfg[:, :, 0:3], 0.0), writes=[pfg])
        ptm = [p.sb(f"ptm{s}", [128, 1536], BF16) for s in range(NS)]
        pab = [p.sb(f"pab{s}", [128, 8], F32) for s in range(NS)]

        def ring(name, shape, dt, n):
            return Ring([p.sb(f"{name}{i}", shape, dt) for i in range(n)])
        posi = p.sb("posi", [128, TT], I32)
        r_f = ring("f128w", [128, 128], F32, 6)
        r_ki = ring("ki", [128, 128], I32, 1)
        r_sc = ring("sincos", [128, 2, 128], F32, 2)
        r_rq = ring("rq", [128, 2, 2, 128], BF16, 2)
        r_rqd = ring("rqd", [128, 2, 2, 128], BF16, 2)
        r_rk = ring("rk", [128, 2, 2, 128], BF16, 2)
        r_gq = ring("gq", [128, 4, 3, 128], BF16, 2)
        rstate = [[p.sb(f"rstate{h}{d}", [128, 256], F32) for d in range(2)] for h in range(2)]
        rstateb = [[p.sb(f"rstateb{h}{d}", [128, 256], BF16) for d in range(2)] for h in range(2)]
        gstate = [p.sb(f"gstate{j}", [128, 128], F32) for j in range(4)]
        gstateb = [p.sb(f"gstateb{j}", [128, 128], BF16) for j in range(4)]
        for h in range(2):
            for d in range(2):
                p.op("dve", lambda e, h=h, d=d: e.memset(rstate[h][d][:], 0.0), writes=[rstate[h][d]])
                p.op("dve", lambda e, h=h, d=d: e.memset(rstateb[h][d][:], 0.0), writes=[rstateb[h][d]])
        for j in range(4):
            p.op("dve", lambda e, j=j: e.memset(gstate[j][:], 0.0), writes=[gstate[j]])
            p.op("dve", lambda e, j=j: e.memset(gstateb[j][:], 0.0), writes=[gstateb[j]])
        mixo = ring("mixo", [128, 1024], BF16, 2)
        pacc = Ring([p.ps(f"pacc{i}", [128, 512], F32) for i in range(2)])
        ptr = Ring([p.ps(f"ptr{i}", [128, 1024], BF16) for i in range(2)])
        pmx = Ring([p.ps(f"pmx{i}", [128, 512], F32) for i in range(4)])
        r_osum = ring("osum", [128, 8], F32, 2)
        g_ab = ring("gab", [128, 40], F32, 2)
        rr = [dict(sT=ring(f"sT{h}", [128, 128], BF16, 2), ktm=ring(f"ktm{h}", [128, 256], BF16, 1),
                   of=ring(f"of{h}", [128, 256], F32, 1), sg=ring(f"sg{h}", [128, 256], F32, 1),
                   junk=ring(f"rjunk{h}", [128, 256], F32, 1), sm=ring(f"rsm{h}", [128, 8], F32, 2)) for h in range(2)]
        gr = [dict(raw=ring(f"raw{j}", [128, 384], BF16, 1), junk=ring(f"gjunk{j}", [128, 256], F32, 1),
                   sm=ring(f"gsm{j}", [128, 8], F32, 4), b=ring(f"gb{j}", [128, 128], BF16, 10),
                   fT=ring(f"fT{j}", [128, 384], BF16, 1), E=ring(f"gE{j}", [128, 128], F32, 5),
                   P=ring(f"gP{j}", [128, 128], F32, 7)) for j in range(4)]

        w_fm, w_tm = io["w_fm"], io["w_tm"]
        x, pos, mixed = io["x"], io["pos"], io["mixed"]
        mixed_rows = mixed if callable(mixed) else (lambda r0: mixed[r0:r0 + 128, :])

        def reduce_sin(dst_t, dst_ap, ang, shift):
            src = ang
            if shift != 0.0:
                sh = r_f.next()
                ts(p, "dve", sh, sh[:], ang, ang[:], float(shift), None, ALU.add)
                src = sh
            ki = r_ki.next()
            kf = r_f.next()
            t2 = r_f.next()
            ts(p, "dve", ki, ki[:], src, src[:], float(1 / (2 * np.pi)), None, ALU.mult)
            cp(p, "dve", kf, kf[:], ki, ki[:])
            stt(p, "dve", t2, t2[:], kf, kf[:], float(-2 * np.pi), src, src[:], ALU.mult, ALU.add)
            ts(p, "dve", kf, kf[:], t2, t2[:], float(np.pi), float(-2 * np.pi), ALU.is_gt, ALU.mult)
            tt(p, "dve", t2, t2[:], t2, t2[:], kf, kf[:], ALU.add)
            ts(p, "dve", kf, kf[:], t2, t2[:], float(-np.pi), float(2 * np.pi), ALU.is_lt, ALU.mult)
            tt(p, "dve", t2, t2[:], t2, t2[:], kf, kf[:], ALU.add)
            ts(p, "dve", t2, t2[:], t2, t2[:], float(-np.pi), float(np.pi), ALU.max, ALU.min)
            act(p, dst_t, dst_ap, t2, t2[:], AF.Sin)

        def norm_chain(tile, s):
            r0 = tile * TT + s * 128
            p.dma("sp", xt[:], x[r0:r0 + 128, :], writes=[xt])
            yield
            act(p, tm, tm[:], xt, xt[:], AF.Square, accum=ss[:, 0:1], accum_t=ss)
            yield
            act(p, rs, rs[:, 0:1], ss, ss[:, 0:1], AF.Ln, bias=float(EPS), scale=1.0 / D)
            act(p, rs, rs[:, 0:1], rs, rs[:, 0:1], AF.Exp, scale=-0.5)
            yield
            ts(p, "dve", tm, tm[:], xt, xt[:], rs[:, 0:1], None, ALU.mult, extra=[rs])
            yield
            for kb in range(4):
                pt_ = ptr.next()
                for kk in range(8):
                    k = kb * 8 + kk
                    tr(p, pt_, pt_[:, kk * 128:(kk + 1) * 128], tm, tm[:, k * 128:(k + 1) * 128], identb)
                p.op("dve", lambda e, pt_=pt_, kb=kb: e.tensor_tensor(
                    out=hnT[:, kb * 8:(kb + 1) * 8, s * 128:(s + 1) * 128],
                    in0=pt_[:].rearrange("p (k t) -> p k t", k=8),
                    in1=gT[:, kb * 8:(kb + 1) * 8].unsqueeze(2).to_broadcast([128, 8, 128]), op=ALU.mult),
                    reads=[pt_, gT], writes=[hnT])
                yield

        for ti in range(NT):
            tok0 = ti * TT
            if ti == 0:
                for s in range(NS):
                    for _ in norm_chain(0, s):
                        pass
            if ti > 0:
                p.op("dve", lambda e: e.tensor_copy(out=pfg[:, :, 0:3], in_=pfg[:, :, TT:TT + 3]), reads=[pfg], writes=[pfg])
            p.dma("sp", posi[:], pos[0:1, tok0:tok0 + TT].partition_broadcast(128), writes=[posi])
            for blk in range(20):
                wt = ws.load(w_fm[blk])
                wv = wt[:].rearrange("p (k c) -> p k c", k=NK)
                for c0 in range(0, TT, 512):
                    cw = min(512, TT - c0)
                    ps = pacc.next()
                    for k in range(NK):
                        mm(p, ps, ps[:, 0:cw], wt, wv[:, k, :], hnT, hnT[:, k, c0:c0 + cw], start=(k == 0), stop=(k == NK - 1))
                    if blk < 8:
                        cp(p, "act", pfr, pfr[:, blk, c0:c0 + cw], ps, ps[:, 0:cw])
                    else:
                        cp(p, "act", pfg, pfg[:, blk - 8, 3 + c0:3 + c0 + cw], ps, ps[:, 0:cw])
            for n in range(3):
                pss = [pmx.next() for _ in range(NS)]
                for kg in range(4):
                    wt = ws.load(w_tm[n * 4 + kg])
                    wv = wt[:].rearrange("p (k c) -> p k c", k=8)
                    for s in range(NS):
                        for k2 in range(8):
                            k = kg * 8 + k2
                            mm(p, pss[s], pss[s][:], hnT, hnT[:, k, s * 128:(s + 1) * 128], wt, wv[:, k2, :],
                               start=(k == 0), stop=(k == NK - 1))
                for s in range(NS):
                    cp(p, "act" if s % 2 == 0 else "dve", ptm[s], ptm[s][:, n * 512:(n + 1) * 512], pss[s], pss[s][:])
            for s in range(NS):
                ps = pacc.next()
                for k in range(NK):
                    mm(p, ps, ps[:, 0:8], hnT, hnT[:, k, s * 128:(s + 1) * 128], wabb, wabb[:, k, :],
                       start=(k == 0), stop=(k == NK - 1))
                cp(p, "act", pab[s], pab[s][:], ps, ps[:, 0:8])

            for s in range(NS):
                c0 = s * 128
                mo = mixo.next()
                pt = ptm[s]
                ang = r_f.next()
                cp(p, "dve", ang, ang[:], posi, posi[:, c0:c0 + 128])
                ts(p, "dve", ang, ang[:], ang, ang[:], invf[:, 0:1], None, ALU.mult, extra=[invf])
                sc = r_sc.next()
                reduce_sin(sc, sc[:, 0, :], ang, 0.0)
                reduce_sin(sc, sc[:, 1, :], ang, np.pi / 2)
                sinT, cosT = sc[:, 0, :], sc[:, 1, :]
                rq, rqd, rk = r_rq.next(), r_rqd.next(), r_rk.next()
                for h in range(2):
                    for qk, dst in ((0, rq), (1, rk)):
                        x1 = pfr[:, h * 4 + qk * 2 + 0, c0:c0 + 128]
                        x2 = pfr[:, h * 4 + qk * 2 + 1, c0:c0 + 128]
                        t1, t2 = r_f.next(), r_f.next()
                        tt(p, "dve", t1, t1[:], pfr, x1, sc, cosT, ALU.mult)
                        tt(p, "dve", t2, t2[:], pfr, x2, sc, sinT, ALU.mult)
                        tt(p, "dve", dst, dst[:, h, 0, :], t1, t1[:], t2, t2[:], ALU.subtract)
                        t1, t2 = r_f.next(), r_f.next()
                        tt(p, "dve", t1, t1[:], pfr, x2, sc, cosT, ALU.mult)
                        tt(p, "dve", t2, t2[:], pfr, x1, sc, sinT, ALU.mult)
                        tt(p, "dve", dst, dst[:, h, 1, :], t1, t1[:], t2, t2[:], ALU.add)
                    for half in range(2):
                        tt(p, "dve", rqd, rqd[:, h, half, :], rq, rq[:, h, half, :], rdq, rdq[:, h, :], ALU.mult)
                gq = r_gq.next()
                for j in range(4):
                    for c in range(3):
                        b = j * 3 + c
                        t1 = r_f.next()
                        ts(p, "dve", t1, t1[:], pfg, pfg[:, b, c0 + 3:c0 + 3 + 128], convw[:, b, 3:4], None, ALU.mult, extra=[convw])
                        for w in (2, 1, 0):
                            stt(p, "dve", t1, t1[:], pfg, pfg[:, b, c0 + w:c0 + w + 128], convw[:, b, w:w + 1], t1, t1[:],
                                ALU.mult, ALU.add, extra=[convw])
                        act(p, gq, gq[:, j, c, :], t1, t1[:], AF.Silu)

                def ret_head(h, mo=mo, pt=pt, rq=rq, rqd=rqd, rk=rk):
                    R = rr[h]
                    ptk = ptr.next()
                    for half in range(2):
                        tr(p, ptk, ptk[:, half * 128:(half + 1) * 128], rk, rk[:, h, half, :], identb)
                    ktm = R["ktm"].next()
                    ts(p, "dve", ktm, ktm[:], ptk, ptk[:, 0:256], rdk[:, h:h + 1], None, ALU.mult, extra=[rdk])
                    psc = pmx.next()
                    for half in range(2):
                        mm(p, psc, psc[:, 0:128], rk, rk[:, h, half, :], rq, rq[:, h, half, :], start=(half == 0), stop=(half == 1))
                    sT = R["sT"].next()
                    tt(p, "dve", sT, sT[:], psc, psc[:, 0:128], rmaskT, rmaskT[:, h, :], ALU.mult)
                    yield
                    vap = pt[:, h * 256:(h + 1) * 256]
                    po = pmx.next()
                    mm(p, po, po[:, 0:256], sT, sT[:], pt, vap, start=True, stop=False)
                    for half in range(2):
                        mm(p, po, po[:, 0:256], rqd, rqd[:, h, half, :], rstateb[h][half], rstateb[h][half][:],
                           start=False, stop=(half == 1))
                    of = R["of"].next()
                    cp(p, "act", of, of[:], po, po[:, 0:256])
                    yield
                    for half in range(2):
                        pst = pmx.next()
                        mm(p, pst, pst[:, 0:256], ktm, ktm[:, half * 128:(half + 1) * 128], pt, vap)
                        stt(p, "dve", rstate[h][half], rstate[h][half][:], rstate[h][half], rstate[h][half][:], rdc[:, h:h + 1],
                            pst, pst[:, 0:256], ALU.mult, ALU.add, extra=[rdc])
                        cp(p, "act", rstateb[h][half], rstateb[h][half][:], rstate[h][half], rstate[h][half][:])
                        yield
                    junk = R["junk"].next()
                    sq = R["sm"].next()
                    act(p, junk, junk[:], of, of[:], AF.Square, accum=sq[:, 0:1], accum_t=sq)
                    sg = R["sg"].next()
                    act(p, sg, sg[:], pt, pt[:, 512 + h * 256: 512 + (h + 1) * 256], AF.Silu)
                    yield
                    act(p, sq, sq[:, 1:2], sq, sq[:, 0:1], AF.Ln, bias=float(EPS), scale=1.0 / 256)
                    act(p, sq, sq[:, 1:2], sq, sq[:, 1:2], AF.Exp, scale=-0.5)
                    tt(p, "dve", sg, sg[:], sg, sg[:], rgain, rgain[:, h * 256:(h + 1) * 256], ALU.mult)
                    yield
                    stt(p, "dve", mo, mo[:, h * 256:(h + 1) * 256], of, of[:], sq[:, 1:2], sg, sg[:], ALU.mult, ALU.mult, extra=[sq])

                ab = g_ab.next()
                pb8 = pab[s]
                tt(p, "dve", ab, ab[:, 24:28], pb8, pb8[:, 0:4], dtb, dtb[:], ALU.add)
                stt(p, "dve", ab, ab[:, 0:4], ab, ab[:, 24:28], -1.0, ab, ab[:, 24:28], ALU.mult, ALU.max)
                act(p, ab, ab[:, 0:4], ab, ab[:, 0:4], AF.Exp, scale=-1.0)
                act(p, ab, ab[:, 0:4], ab, ab[:, 0:4], AF.Ln, bias=1.0)
                stt(p, "dve", ab, ab[:, 0:4], ab, ab[:, 24:28], 0.0, ab, ab[:, 0:4], ALU.max, ALU.add)
                tt(p, "dve", ab, ab[:, 0:4], ab, ab[:, 0:4], aneg, aneg[:], ALU.mult)
                act(p, ab, ab[:, 4:8], pb8, pb8[:, 4:8], AF.Sigmoid)
                ts(p, "dve", ab, ab[:, 28:32], ab, ab[:, 4:8], -1.0, None, ALU.mult)
                pg = pmx.next()
                mm(p, pg, pg[:, 0:4], tri, tri[:], ab, ab[:, 0:4])
                mm(p, pg, pg[:, 4:8], bones, bones[:], ab, ab[:, 0:4])
                cp(p, "dve", ab, ab[:, 8:16], pg, pg[:, 0:8])
                act(p, ab, ab[:, 16:20], ab, ab[:, 8:12], AF.Exp)
                tt(p, "dve", ab, ab[:, 24:28], ab, ab[:, 12:16], ab, ab[:, 8:12], ALU.subtract)
                act(p, ab, ab[:, 20:24], ab, ab[:, 24:28], AF.Exp)
                tt(p, "dve", ab, ab[:, 32:36], ab, ab[:, 4:8], ab, ab[:, 16:20], ALU.mult)
                osum = r_osum.next()

                def gdn_head(j, mo=mo, pt=pt, gq=gq, ab=ab, osum=osum):
                    G = gr[j]
                    pq = ptr.next()
                    for c in range(3):
                        tr(p, pq, pq[:, c * 128:(c + 1) * 128], gq, gq[:, j, c, :], identb)
                    raw = G["raw"].next()
                    cp(p, "act", raw, raw[:], pq, pq[:, 0:384])
                    yield
                    nq = G["sm"].next()
                    junk = G["junk"].next()
                    act(p, junk, junk[:, 0:128], raw, raw[:, 0:128], AF.Square, accum=nq[:, 0:1], accum_t=nq)
                    act(p, junk, junk[:, 128:256], raw, raw[:, 128:256], AF.Square, accum=nq[:, 1:2], accum_t=nq)
                    yield
                    act(p, nq, nq[:, 2:4], nq, nq[:, 0:2], AF.Ln, bias=float(EPS))
                    act(p, nq, nq[:, 2:4], nq, nq[:, 2:4], AF.Exp, scale=-0.5)
                    yield
                    ts(p, "dve", nq, nq[:, 4:5], nq, nq[:, 2:3], float(128 ** -0.5), None, ALU.mult)
                    tt(p, "dve", nq, nq[:, 5:6], nq, nq[:, 4:5], ab, ab[:, 16 + j:17 + j], ALU.mult)
                    tt(p, "dve", nq, nq[:, 6:7], nq, nq[:, 3:4], ab, ab[:, 32 + j:33 + j], ALU.mult)
                    tt(p, "dve", nq, nq[:, 7:8], nq, nq[:, 3:4], ab, ab[:, 20 + j:21 + j], ALU.mult)
                    qh, qd, kh, kbg, kdc, vbt = (G["b"].next() for _ in range(6))
                    ts(p, "dve", qh, qh[:], raw, raw[:, 0:128], nq[:, 4:5], None, ALU.mult, extra=[nq])
                    ts(p, "dve", qd, qd[:], raw, raw[:, 0:128], nq[:, 5:6], None, ALU.mult, extra=[nq])
                    ts(p, "dve", kh, kh[:], raw, raw[:, 128:256], nq[:, 3:4], None, ALU.mult, extra=[nq])
                    yield
                    ts(p, "dve", kbg, kbg[:], raw, raw[:, 128:256], nq[:, 6:7], None, ALU.mult, extra=[nq])
                    ts(p, "dve", kdc, kdc[:], raw, raw[:, 128:256], nq[:, 7:8], None, ALU.mult, extra=[nq])
                    ts(p, "dve", vbt, vbt[:], raw, raw[:, 256:384], ab[:, 4 + j:5 + j], None, ALU.mult, extra=[ab])
                    pb = ptr.next()
                    tr(p, pb, pb[:, 0:128], qh, qh[:], identb)
                    tr(p, pb, pb[:, 128:256], qd, qd[:], identb)
                    tr(p, pb, pb[:, 256:384], kh, kh[:], identb)
                    fT = G["fT"].next()
                    cp(p, "act", fT, fT[:], pb, pb[:, 0:384])
                    qhT, qdT, khT = fT[:, 0:128], fT[:, 128:256], fT[:, 256:384]
                    yield
                    trl = G["E"].next()
                    ts(p, "dve", trl, trl[:], tri, tri[:], ab[:, j:j + 1], None, ALU.mult, extra=[ab])
                    yield
                    pgb = pmx.next()
                    mm(p, pgb, pgb[:, 0:128], ones, ones[:], trl, trl[:])
                    E = G["E"].next()
                    ts(p, "dve", E, E[:], pgb, pgb[:, 0:128], ab[:, 8 + j:9 + j], 0.0, ALU.subtract, ALU.max, extra=[ab])
                    ET = G["E"].next()
                    ts(p, "dve", ET, ET[:], pgb, pgb[:, 0:128], ab[:, 8 + j:9 + j], 0.0, ALU.subtract, ALU.min, extra=[ab])
                    egl = G["sm"].next()
                    act(p, egl, egl[:, 0:1], pgb, pgb[:, 63:64], AF.Exp)
                    act(p, egl, egl[:, 1:2], pgb, pgb[:, 127:128], AF.Exp)
                    yield
                    act(p, E, E[:], E, E[:], AF.Exp, scale=-1.0)
                    act(p, ET, ET[:], ET, ET[:], AF.Exp)
                    yield
                    tt(p, "dve", E, E[:], E, E[:], mstrict, mstrict[:], ALU.mult)
                    tt(p, "dve", ET, ET[:], ET, ET[:], minclT, minclT[:], ALU.mult)
                    pqk = pmx.next()
                    mm(p, pqk, pqk[:, 0:128], fT, khT, fT, qhT)
                    qkT = G["b"].next()
                    tt(p, "dve", qkT, qkT[:], pqk, pqk[:, 0:128], ET, ET[:], ALU.mult)
                    yield
                    pkk = pmx.next()
                    mm(p, pkk, pkk[:, 0:128], fT, khT, fT, khT)
                    N = G["P"].next()
                    stt(p, "dve", N, N[:], pkk, pkk[:, 0:128], ab[:, 28 + j:29 + j], E, E[:], ALU.mult, ALU.mult, extra=[ab])
                    yield
                    pnt = pmx.next()
                    mm(p, pnt, pnt[:, 0:128], N, N[:], identf, identf[:])
                    NTt = G["P"].next()
                    cp(p, "act", NTt, NTt[:], pnt, pnt[:, 0:128])
                    yield
                    Q = G["P"].next()
                    tt(p, "dve", Q, Q[:], NTt, NTt[:], identf, identf[:], ALU.add)
                    P, PT = N, NTt
                    pend = None
                    for it in range(5):
                        pp = pmx.next()
                        mm(p, pp, pp[:, 0:128], PT, PT[:], P, P[:])
                        mm(p, pp, pp[:, 128:256], P, P[:], PT, PT[:])
                        P2, P2T = G["P"].next(), G["P"].next()
                        cp(p, "act", P2, P2[:], pp, pp[:, 0:128])
                        cp(p, "dve", P2T, P2T[:], pp, pp[:, 128:256])
                        if pend is not None:
                            pqq = pmx.next()
                            mm(p, pqq, pqq[:, 0:128], pend, pend[:], Q, Q[:])
                            Q2 = G["P"].next()
                            tt(p, "dve", Q2, Q2[:], pqq, pqq[:, 0:128], Q, Q[:], ALU.add)
                            Q = Q2
                        yield
                        pend = P2
                        P, PT = P2, P2T
                    pqq = pmx.next()
                    mm(p, pqq, pqq[:, 0:128], pend, pend[:], Q, Q[:])
                    Q2 = G["P"].next()
                    tt(p, "dve", Q2, Q2[:], pqq, pqq[:, 0:128], Q, Q[:], ALU.add)
                    Q = Q2
                    yield
                    TiT = G["b"].next()
                    cp(p, "act", TiT, TiT[:], Q, Q[:])
                    yield
                    pw = pmx.next()
                    mm(p, pw, pw[:, 0:128], kbg, kbg[:], TiT, TiT[:])
                    nwT = G["b"].next()
                    ts(p, "dve", nwT, nwT[:], pw, pw[:, 0:128], -1.0, None, ALU.mult)
                    yield
                    vn = G["b"].next()
                    oc = G["E"].next()
                    for cx in range(2):
                        r0, r1 = cx * 64, cx * 64 + 64
                        pv = pmx.next()
                        mm(p, pv, pv[:, 0:128], TiT, TiT[:], vbt, vbt[:], start=True, stop=False)
                        mm(p, pv, pv[:, 0:128], nwT, nwT[:], gstateb[j], gstateb[j][:], start=False, stop=True)
                        cp(p, "act", vn, vn[r0:r1, :], pv, pv[r0:r1, 0:128])
                        yield
                        po = pmx.next()
                        mm(p, po, po[:, 0:128], fT, qdT, gstateb[j], gstateb[j][:], start=True, stop=False)
                        mm(p, po, po[:, 0:128], qkT, qkT[r0:r1, :], vn, vn[r0:r1, :], start=False, stop=True)
                        cp(p, "act", oc, oc[r0:r1, :], po, po[r0:r1, 0:128])
                        pst = pmx.next()
                        mm(p, pst, pst[:, 0:128], kdc, kdc[r0:r1, :], vn, vn[r0:r1, :])
                        stt(p, "dve", gstate[j], gstate[j][:], gstate[j], gstate[j][:], egl[:, cx:cx + 1], pst, pst[:, 0:128],
                            ALU.mult, ALU.add, extra=[egl])
                        yield
                        cp(p, "act", gstateb[j], gstateb[j][:], gstate[j], gstate[j][:])
                        yield
                    junk2 = G["junk"].next()
                    act(p, junk2, junk2[:, 0:128], oc, oc[:], AF.Square, accum=osum[:, j:j + 1], accum_t=osum)
                    sgz = G["E"].next()
                    act(p, sgz, sgz[:], pt, pt[:, 1024 + j * 128: 1024 + (j + 1) * 128], AF.Silu)
                    yield
                    act(p, osum, osum[:, 4 + j:5 + j], osum, osum[:, j:j + 1], AF.Ln, bias=float(EPS), scale=1.0 / 128)
                    act(p, osum, osum[:, 4 + j:5 + j], osum, osum[:, 4 + j:5 + j], AF.Exp, scale=-0.5)
                    tt(p, "dve", sgz, sgz[:], sgz, sgz[:], ggain, ggain[:], ALU.mult)
                    yield
                    stt(p, "dve", mo, mo[:, 512 + j * 128: 512 + (j + 1) * 128], oc, oc[:], osum[:, 4 + j:5 + j], sgz, sgz[:],
                        ALU.mult, ALU.mult, extra=[osum])

                chains = [gdn_head(0), ret_head(0), gdn_head(1), gdn_head(2), ret_head(1), gdn_head(3)]
                if ti + 1 < NT:
                    chains.append(norm_chain(ti + 1, s))
                while chains:
                    for ch in list(chains):
                        try:
                            next(ch)
                        except StopIteration:
                            chains.remove(ch)
                if "mixed_tiles" in io:
                    p.dma("sp", mixed_rows(tok0 + c0), mo[:], reads=[mo], writes=[io["mixed_tiles"][ti]], semtile=mo)
                else:
                    p.dma("sp", mixed_rows(tok0 + c0), mo[:], reads=[mo])
            if "exchange" in io:
                io["exchange"](ti)
        p.finish()
        p.run()


OFF = dict(rq=0, rk=2048, rv=4096, rg=6144, gq=8192, gk=10240, gv=12288, gz=14336, ga=16384, gb=16400)


def _granules_ws(w, cols):
    sub = w[:, cols]
    return np.ascontiguousarray(sub.reshape(NK, 128, len(cols)).transpose(1, 0, 2)).reshape(128, NK * len(cols))


def p1_host_inputs(inp, b, g, S, TT):
    w = inp["w_in"][0]
    fm_cols = []
    for h in range(2):
        hr = 2 * g + h
        fm_cols += list(range(OFF["rq"] + hr * 256, OFF["rq"] + hr * 256 + 256))
        fm_cols += list(range(OFF["rk"] + hr * 256, OFF["rk"] + hr * 256 + 256))
    for j in range(4):
        hg = 4 * g + j
        for nm in ("gq", "gk", "gv"):
            fm_cols += list(range(OFF[nm] + hg * 128, OFF[nm] + hg * 128 + 128))
    tm_cols = []
    for nm, width, nh in (("rv", 256, 2), ("rg", 256, 2), ("gz", 128, 4)):
        for h in range(nh):
            hh = (2 * g + h) if nh == 2 else (4 * g + h)
            tm_cols += list(range(OFF[nm] + hh * width, OFF[nm] + hh * width + width))
    ab_cols = [OFF["ga"] + 4 * g + j for j in range(4)] + [OFF["gb"] + 4 * g + j for j in range(4)]
    w_fm = np.stack([_granules_ws(w, fm_cols[i * 128:(i + 1) * 128]) for i in range(20)])
    wt_ = w[:, tm_cols].reshape(4, 8, 128, 3, 512)
    w_tm = np.ascontiguousarray(wt_.transpose(3, 0, 2, 1, 4)).reshape(12, 128, 4096)
    w_ab = np.ascontiguousarray(w[:, ab_cols].reshape(NK, 128, 8).transpose(1, 0, 2))
    conv = inp["gdn_conv"][0]
    convw = np.zeros((128, 12, 4), np.float32)
    for j in range(4):
        hg = 4 * g + j
        for c in range(3):
            convw[:, j * 3 + c, :] = conv[:, c * 2048 + hg * 128: c * 2048 + hg * 128 + 128].T
    f32 = np.float32
    hs = np.array([2 * g, 2 * g + 1], f32)
    lg = np.log1p(-np.exp2(-5.0 - hs)).astype(f32)
    idx = np.arange(128, dtype=f32)
    rel = idx[None, :] - idx[:, None]
    maskT = np.where(rel >= 0, np.exp(np.maximum(rel, 0)[None] * lg[:, None, None]), 0.0).astype(f32)
    ret_maskT = np.ascontiguousarray((maskT * f32(1 / 16)).transpose(1, 0, 2))
    dq = np.exp((idx + 1.0)[None, :] * lg[:, None]).astype(f32)
    ret_dq = np.ascontiguousarray(np.broadcast_to(dq[None], (128, 2, 128))).astype(f32)
    dk = np.exp((127.0 - idx)[None, :] * lg[:, None]).astype(f32) * f32(1 / 16)
    ret_dk = np.ascontiguousarray(dk.T)
    ret_dc = np.ascontiguousarray(np.broadcast_to(np.exp(128.0 * lg)[None], (128, 2))).astype(f32)
    inv_freq = (10000.0 ** (-np.arange(128, dtype=f32) / f32(128))).astype(f32).reshape(128, 1)
    blk = (np.arange(128)[:, None] // 64) == (np.arange(128)[None, :] // 64)
    ii, jj = np.arange(128)[:, None], np.arange(128)[None, :]
    tri = (blk & (ii <= jj)).astype(f32)
    bones = blk.astype(f32)
    mstrict = (blk & (ii > jj)).astype(f32)
    minclT = (blk & (jj >= ii)).astype(f32)
    d = {
        "x": np.ascontiguousarray(inp["x"][b]),
        "pos": np.ascontiguousarray(inp["positions"][b][None, :]).astype(np.int32),
        "mix_gT": np.ascontiguousarray(inp["mix_norm"][0].reshape(NK, 128).T),
        "w_fm": w_fm, "w_tm": w_tm, "w_ab": w_ab, "convw": convw,
        "ret_gain": np.ascontiguousarray(inp["ret_norm"][0][2 * g:2 * g + 2].reshape(1, 512)),
        "gdn_gain": np.ascontiguousarray(inp["gdn_norm"][0].reshape(1, 128)),
        "a_log": np.ascontiguousarray(inp["gdn_a_log"][0][4 * g:4 * g + 4].reshape(1, 4)),
        "dt_bias": np.ascontiguousarray(inp["gdn_dt_bias"][0][4 * g:4 * g + 4].reshape(1, 4)),
        "inv_freq": inv_freq, "ret_maskT": ret_maskT, "ret_dq": ret_dq, "ret_dk": ret_dk, "ret_dc": ret_dc,
        "tri": tri, "bones": bones, "ones": np.ones((128, 128), f32), "mstrict": mstrict, "minclT": minclT,
    }
    return d


P1_SPECS = lambda S: {
    "x": ([S, D], F32), "pos": ([1, S], I32), "mix_gT": ([128, NK], F32),
    "w_fm": ([20, 128, 4096], F32), "w_tm": ([12, 128, 4096], F32), "w_ab": ([128, NK, 8], F32),
    "convw": ([128, 12, 4], F32), "ret_gain": ([1, 512], F32), "gdn_gain": ([1, 128], F32),
    "a_log": ([1, 4], F32), "dt_bias": ([1, 4], F32), "inv_freq": ([128, 1], F32),
    "ret_maskT": ([128, 2, 128], F32), "ret_dq": ([128, 2, 128], F32), "ret_dk": ([128, 2], F32),
    "ret_dc": ([128, 2], F32), "tri": ([128, 128], F32), "bones": ([128, 128], F32), "ones": ([128, 128], F32),
    "mstrict": ([128, 128], F32), "minclT": ([128, 128], F32),
}


def transpose_rows(p, tm, dstT, col0, ident, trps, gT=None, eng_alt=("dve", "act")):
    for kb in range(4):
        pt = trps.next()
        for kk in range(8):
            k = kb * 8 + kk
            tr(p, pt, pt[:, kk * 128:(kk + 1) * 128], tm, tm[:, k * 128:(k + 1) * 128], ident)
        if gT is not None:
            p.op("dve", lambda e, pt=pt, kb=kb: e.tensor_tensor(
                out=dstT[:, kb * 8:(kb + 1) * 8, col0:col0 + 128],
                in0=pt[:].rearrange("p (k t) -> p k t", k=8),
                in1=gT[:, kb * 8:(kb + 1) * 8].unsqueeze(2).to_broadcast([128, 8, 128]), op=ALU.mult),
                reads=[pt, gT], writes=[dstT])
        else:
            cp(p, eng_alt[kb % 2], dstT, dstT[:, kb * 8:(kb + 1) * 8, col0:col0 + 128], pt, pt[:].rearrange("p (k t) -> p k t", k=8))


def phase2(p, Tc, TT, io, gathered=None, gathered_rows=None):
    NT = Tc // TT
    NS = TT // 128
    with ExitStack() as st:
        p.stack = st
        identb = make_ident(p, BF16, "identb")

        def cload(name, shape, dt, src):
            t = p.sb(name, shape, dt)
            p.dma("sp", t[:], src, writes=[t])
            return t
        gx = cload("gx", [128, NK], F32, io["xattn_gT"])
        gm = cload("gm", [128, NK], F32, io["mem_gT"])
        gl = cload("gl", [128, NK], F32, io["mlp_gT"])

        ws = WStream(p, 3)
        h = p.sb("h", [128, NS, D], F32)
        hs = [p.view(f"h{s}", h.t[:, s, :]) for s in range(NS)]
        TB = max(TT, MEM)
        bufA = p.sb("bufA", [128, NK, TB], BF16)
        bufB = p.sb("bufB", [128, NK, TB], BF16)
        aTh = [p.view(f"aT{i}", bufB.t[:, i * 16:(i + 1) * 16, :]) for i in range(2)]
        tm = p.sb("tm", [128, D], BF16)
        ss = p.sb("ss", [128, 1], F32)
        rs = p.sb("rs", [128, 1], F32)
        gfin = p.sb("gfin", [128, 2048], F32)
        kTr = Ring([p.sb(f"kTh{i}", [128, 8, 256], BF16) for i in range(1)])
        vr = Ring([p.sb(f"vh{i}", [128, 2, 1024], BF16) for i in range(1)])
        if gathered is not None:
            sel = cload("sel", [128, 4], F32, io["sel"].partition_broadcast(128))
            cand = p.sb("cand", [128, 4, 1024], BF16)
            candv = [p.view(f"cand{q}", cand.t[:, q, :]) for q in range(4)]
        er = Ring([p.sb(f"e{i}", [128, 256], F32) for i in range(4)])
        pbr = Ring([p.sb(f"pb{i}", [128, 256], BF16) for i in range(4)])
        pTr = Ring([p.sb(f"pT{i}", [128, 2, TT], BF16) for i in range(2)])
        smr = Ring([p.sb(f"sm{i}", [128, 4], F32) for i in range(8)])
        rlr = Ring([p.sb(f"rl{i}", [128, 512], F32) for i in range(1)])
        vst = Ring([p.sb(f"vst{i}", [128, 512], BF16) for i in range(2)])
        pacc = Ring([p.ps(f"pacc{i}", [128, 512], F32) for i in range(6)])
        ptr = Ring([p.ps(f"ptr{i}", [128, 1024], BF16) for i in range(2)])

        x, mixed_in, mem, out = io["x"], io.get("mixed_in"), io["mem"], io["out"]
        kT_scr = p.view("kT_scr", io["kT_scr"])
        v_scr = p.view("v_scr", io["v_scr"])
        evi = [0]

        def evac_copy(ot, oap, ps, pap, scale=None):
            evi[0] += 1
            if evi[0] % 2:
                if scale is None:
                    cp(p, "act", ot, oap, ps, pap)
                else:
                    p.op("act", lambda e: e.mul(out=oap, in_=pap, mul=float(scale)), reads=[ps], writes=[ot])
            else:
                if scale is None:
                    cp(p, "dve", ot, oap, ps, pap)
                else:
                    ts(p, "dve", ot, oap, ps, pap, float(scale), None, ALU.mult)

        def rms_rows(src_t, src_ap):
            act(p, tm, tm[:], src_t, src_ap, AF.Square, accum=ss[:, 0:1], accum_t=ss)
            rstd_from_ss(p, rs, rs[:, 0:1], ss, ss[:, 0:1], 1.0 / D)
            ts(p, "dve", tm, tm[:], src_t, src_ap, rs[:, 0:1], None, ALU.mult, extra=[rs])

        def linear_as(actT, wsrc, n_kg, kpg, nblocks, sink):
            for n in range(nblocks):
                pss = [pacc.next() for _ in range(NS)]
                for kg in range(n_kg):
                    wt = ws.load(wsrc(n, kg))
                    wv = wt[:].rearrange("p (k c) -> p k c", k=kpg)
                    for s in range(NS):
                        for k2 in range(kpg):
                            k = kg * kpg + k2
                            mm(p, pss[s], pss[s][:], actT[0], actT[1](k, s), wt, wv[:, k2, :],
                               start=(k == 0), stop=(k == n_kg * kpg - 1))
                for s in range(NS):
                    sink(n, s, pss[s])

        def add_to_h(n, s, ps):
            tt(p, "dve", hs[s], hs[s][:, n * 512:(n + 1) * 512], ps, ps[:], hs[s], hs[s][:, n * 512:(n + 1) * 512], ALU.add)

        for mt in range(2):
            p.dma("sp", hs[0][:], mem[mt * 128:(mt + 1) * 128, :], writes=[hs[0]])
            rms_rows(hs[0], hs[0][:])
            transpose_rows(p, tm, bufA, mt * 128, identb, ptr, gT=gm)
        kst = bufB[:].rearrange("p k t -> p (k t)")[:, 0:NK * MEM].rearrange("p (k m) -> p k m", k=NK)
        for cb in range(NK):
            wt = ws.load(io["w_xk"][cb])
            wv = wt[:].rearrange("p (k c) -> p k c", k=NK)
            ps = pacc.next()
            for k in range(NK):
                mm(p, ps, ps[:, 0:256], wt, wv[:, k, :], bufA, bufA[:, k, 0:256], start=(k == 0), stop=(k == NK - 1))
            evac_copy(bufB, kst[:, cb, :], ps, ps[:, 0:256])
        p.dma("sp", kT_scr[:], kst, reads=[bufB], writes=[kT_scr])
        for n in range(8):
            pss = [pacc.next() for _ in range(2)]
            for kg in range(4):
                wt = ws.load(io["w_xv"][n * 4 + kg])
                wv = wt[:].rearrange("p (k c) -> p k c", k=8)
                for mt in range(2):
                    for k2 in range(8):
                        k = kg * 8 + k2
                        mm(p, pss[mt], pss[mt][:], bufA, bufA[:, k, mt * 128:(mt + 1) * 128], wt, wv[:, k2, :],
                           start=(k == 0), stop=(k == NK - 1))
            for mt in range(2):
                vs = vst.next()
                evac_copy(vs, vs[:], pss[mt], pss[mt][:])
                p.dma("sp", v_scr[:, mt, n * 512:(n + 1) * 512], vs[:], reads=[vs], writes=[v_scr])

        for ti in range(NT):
            tok0 = ti * TT
            for s in range(NS):
                r0 = tok0 + s * 128
                p.dma("sp", hs[s][:], x[r0:r0 + 128, :], writes=[hs[s]])
                for g in range(4):
                    if gathered is None:
                        p.dma("sp", tm[:, g * 1024:(g + 1) * 1024], mixed_in[g, r0:r0 + 128, :], writes=[tm])
                    else:
                        for q in range(4):
                            p.dma("sp", candv[q][:], gathered_rows(g, q * Tc + r0), reads=[gathered], writes=[candv[q]])
                        dst = tm[:, g * 1024:(g + 1) * 1024]
                        ts(p, "dve", tm, dst, candv[0], candv[0][:], sel[:, 0:1], None, ALU.mult, extra=[sel])
                        for q in range(1, 4):
                            stt(p, "dve", tm, dst, candv[q], candv[q][:], sel[:, q:q + 1], tm, dst, ALU.mult, ALU.add, extra=[sel])
                transpose_rows(p, tm, bufA, s * 128, identb, ptr)
            linear_as((bufA, lambda k, s: bufA[:, k, s * 128:(s + 1) * 128]), lambda n, kg: io["w_mo"][n * 4 + kg], 4, 8, 8, add_to_h)
            for s in range(NS):
                rms_rows(hs[s], hs[s][:])
                transpose_rows(p, tm, bufA, s * 128, identb, ptr, gT=gx)
            for cb in range(NK):
                wt = ws.load(io["w_xq"][cb])
                wv = wt[:].rearrange("p (k c) -> p k c", k=NK)
                ps = pacc.next()
                for k in range(NK):
                    mm(p, ps, ps[:, 0:TT], wt, wv[:, k, :], bufA, bufA[:, k, 0:TT], start=(k == 0), stop=(k == NK - 1))
                evac_copy(bufB, bufB[:, cb, 0:TT], ps, ps[:, 0:TT], scale=1.0 / 32.0)
            for hd in range(4):
                kTh, vh = kTr.next(), vr.next()
                p.dma("sp", kTh[:], kT_scr[:, hd * 8:(hd + 1) * 8, :], reads=[kT_scr], writes=[kTh])
                p.dma("sp", vh[:], v_scr[:, :, hd * 1024:(hd + 1) * 1024], reads=[v_scr], writes=[vh])
                pT = pTr.next()
                def soft_chain(s, hd=hd, kTh=kTh, pT=pT):
                    psc = pacc.next()
                    for c in range(8):
                        mm(p, psc, psc[:, 0:256], bufB, bufB[:, hd * 8 + c, s * 128:(s + 1) * 128], kTh, kTh[:, c, :],
                           start=(c == 0), stop=(c == 7))
                    sm = smr.next()
                    p.op("dve", lambda e, psc=psc, sm=sm: e.reduce_max(out=sm[:, 0:1], in_=psc[:, 0:256], axis=AX.X), reads=[psc], writes=[sm])
                    yield
                    ts(p, "dve", sm, sm[:, 1:2], sm, sm[:, 0:1], -1.0, None, ALU.mult)
                    ee = er.next()
                    act(p, ee, ee[:], psc, psc[:, 0:256], AF.Exp, bias=sm[:, 1:2], accum=sm[:, 2:3], accum_t=sm, extra=[sm])
                    yield
                    p.op("dve", lambda e, sm=sm: e.reciprocal(out=sm[:, 3:4], in_=sm[:, 2:3]), reads=[sm], writes=[sm])
                    pb = pbr.next()
                    ts(p, "dve", pb, pb[:], ee, ee[:], sm[:, 3:4], None, ALU.mult, extra=[sm])
                    yield
                    ptp = ptr.next()
                    for mt in range(2):
                        tr(p, ptp, ptp[:, mt * 128:(mt + 1) * 128], pb, pb[:, mt * 128:(mt + 1) * 128], identb)
                    cp(p, "act", pT, pT[:, :, s * 128:(s + 1) * 128], ptp, ptp[:, 0:256].rearrange("p (m t) -> p m t", m=2))
                chains = [soft_chain(s) for s in range(NS)]
                while chains:
                    for ch in list(chains):
                        try:
                            next(ch)
                        except StopIteration:
                            chains.remove(ch)
                for cb in range(8):
                    ps = pacc.next()
                    for mt in range(2):
                        mm(p, ps, ps[:, 0:TT], vh, vh[:, mt, cb * 128:(cb + 1) * 128], pT, pT[:, mt, :], start=(mt == 0), stop=(mt == 1))
                    evac_copy(bufA, bufA[:, hd * 8 + cb, 0:TT], ps, ps[:, 0:TT])
            linear_as((bufA, lambda k, s: bufA[:, k, s * 128:(s + 1) * 128]), lambda n, kg: io["w_xo"][n * 4 + kg], 4, 8, 8, add_to_h)
            for s in range(NS):
                rms_rows(hs[s], hs[s][:])
                transpose_rows(p, tm, bufA, s * 128, identb, ptr, gT=gl)

            def up_group(G):
                aT = aTh[G % 2]
                for cb in range(16):
                    wt = ws.load(io["w_up"][G * 16 + cb])
                    wv = wt[:].rearrange("p (k c) -> p k c", k=NK)
                    ps = pacc.next()
                    for k in range(NK):
                        mm(p, ps, ps[:, 0:TT], wt, wv[:, k, :], bufA, bufA[:, k, 0:TT], start=(k == 0), stop=(k == NK - 1))
                    rl = rlr.next()
                    act(p, rl, rl[:, 0:TT], ps, ps[:, 0:TT], AF.Relu)
                    tt(p, "dve", aT, aT[:, cb, 0:TT], rl, rl[:, 0:TT], rl, rl[:, 0:TT], ALU.mult)

            def down_group(G):
                aT = aTh[G % 2]
                linear_as((aT, lambda k, s: aT[:, k, s * 128:(s + 1) * 128]),
                          lambda n, kg: io["w_down"][(G * 8 + n) * 2 + kg], 2, 8, 8, add_to_h)
            NG = DFF // 2048
            up_group(0)
            for G in range(NG):
                if G + 1 < NG:
                    up_group(G + 1)
                down_group(G)
            for s in range(NS):
                r0 = tok0 + s * 128
                act(p, tm, tm[:], hs[s], hs[s][:], AF.Square, accum=ss[:, 0:1], accum_t=ss)
                rstd_from_ss(p, rs, rs[:, 0:1], ss, ss[:, 0:1], 1.0 / D)
                for hf in range(2):
                    p.dma("sp", gfin[:], io["final_gain"][0:1, hf * 2048:(hf + 1) * 2048].partition_broadcast(128), writes=[gfin])
                    stt(p, "dve", hs[s], hs[s][:, hf * 2048:(hf + 1) * 2048], hs[s], hs[s][:, hf * 2048:(hf + 1) * 2048], rs[:, 0:1],
                        gfin, gfin[:], ALU.mult, ALU.mult, extra=[rs])
                p.dma("sp", out[r0:r0 + 128, :], hs[s][:], reads=[hs[s]])
        p.finish()
        p.run()


def _gran_ws(w, nblk):
    K_, N_ = w.shape
    return np.ascontiguousarray(w.reshape(NK, 128, nblk, 128).transpose(2, 1, 0, 3)).reshape(nblk, 128, NK * 128)


def _gran_as(w):
    K_ = w.shape[0]
    nkg = K_ // 1024
    a = w.reshape(nkg, 8, 128, 8, 512)
    return np.ascontiguousarray(a.transpose(3, 0, 2, 1, 4)).reshape(8 * nkg, 128, 8 * 512)


def _gran_down(w):
    a = w.reshape(8, 2, 8, 128, 8, 512)
    return np.ascontiguousarray(a.transpose(0, 4, 1, 3, 2, 5)).reshape(128, 128, 8 * 512)


MIX_PERM = np.concatenate([np.concatenate([np.arange(g * 512, (g + 1) * 512), np.arange(2048 + g * 512, 2048 + (g + 1) * 512)])
                           for g in range(4)])


def p2_shared_inputs(inp):
    return {
        "xattn_gT": np.ascontiguousarray(inp["xattn_norm"][0].reshape(NK, 128).T),
        "mem_gT": np.ascontiguousarray(inp["mem_norm"][0].reshape(NK, 128).T),
        "mlp_gT": np.ascontiguousarray(inp["mlp_norm"][0].reshape(NK, 128).T),
        "final_gain": np.ascontiguousarray(inp["final_norm"].reshape(1, D)),
        "w_mo": _gran_as(inp["w_mix_out"][0][MIX_PERM]), "w_xq": _gran_ws(inp["w_xq"][0], 32), "w_xk": _gran_ws(inp["w_xk"][0], 32),
        "w_xv": _gran_as(inp["w_xv"][0]), "w_xo": _gran_as(inp["w_xo"][0]),
        "w_up": _gran_ws(inp["w_up"][0], 128), "w_down": _gran_down(inp["w_down"][0]),
    }


P2_SPECS = lambda Tc: {
    "x": ([Tc, D], F32), "mem": ([MEM, D], F32),
    "xattn_gT": ([128, NK], F32), "mem_gT": ([128, NK], F32), "mlp_gT": ([128, NK], F32), "final_gain": ([1, D], F32),
    "w_mo": ([32, 128, 4096], F32), "w_xq": ([32, 128, 4096], F32), "w_xk": ([32, 128, 4096], F32),
    "w_xv": ([32, 128, 4096], F32), "w_xo": ([32, 128, 4096], F32), "w_up": ([128, 128, 4096], F32),
    "w_down": ([128, 128, 4096], F32),
}


TT1 = 512
TT2 = 512
CH = 512


def build_p1(S, TT):
    nc = bass.Bass("TRN2", target_bir_lowering=False)
    io = {k: nc.dram_tensor(k, sh, dt, kind="ExternalInput").ap() for k, (sh, dt) in P1_SPECS(S).items()}
    io["mixed"] = nc.dram_tensor("mixed", [S, 1024], BF16, kind="ExternalOutput").ap()
    with ExitStack() as outer:
        phase1(Prog(nc, outer), S, TT, io)
    return nc


def build_p2(Tc, TT):
    nc = bass.Bass("TRN2", target_bir_lowering=False)
    io = {k: nc.dram_tensor(k, sh, dt, kind="ExternalInput").ap() for k, (sh, dt) in P2_SPECS(Tc).items()}
    io["mixed_in"] = nc.dram_tensor("mixed_in", [4, Tc, 1024], BF16, kind="ExternalInput").ap()
    io["out"] = nc.dram_tensor("out", [Tc, D], F32, kind="ExternalOutput").ap()
    io["kT_scr"] = nc.dram_tensor("kT_scr", [128, NK, 256], BF16).ap()
    io["v_scr"] = nc.dram_tensor("v_scr", [128, 2, D], BF16).ap()
    with ExitStack() as outer:
        phase2(Prog(nc, outer), Tc, TT, io)
    return nc


def build_fused(S, tt1, tt2):
    Tc = S // 4
    nc = bass.Bass("TRN2", target_bir_lowering=False)
    io1 = {k: nc.dram_tensor(k, sh, dt, kind="ExternalInput").ap() for k, (sh, dt) in P1_SPECS(S).items()}
    NCH = S // CH
    mixed_loc = [nc.dram_tensor(f"mixed_loc{k}", [CH, 1024], BF16) for k in range(NCH)]
    mixed_all = [nc.dram_tensor(f"mixed_all{k}", [4 * CH, 1024], BF16) for k in range(NCH)]
    io1["mixed"] = lambda r0: mixed_loc[r0 // CH][r0 % CH:r0 % CH + 128, :]
    io2 = {}
    for k, (sh, dt) in P2_SPECS(Tc).items():
        io2[k] = nc.dram_tensor("x2" if k == "x" else k, sh, dt, kind="ExternalInput").ap()
    io2["sel"] = nc.dram_tensor("sel", [1, 4], F32, kind="ExternalInput").ap()
    io2["out"] = nc.dram_tensor("out", [Tc, D], F32, kind="ExternalOutput").ap()
    io2["kT_scr"] = nc.dram_tensor("kT_scr", [128, NK, 256], BF16).ap()
    io2["v_scr"] = nc.dram_tensor("v_scr", [128, 2, D], BF16).ap()
    with ExitStack() as outer:
        p = Prog(nc, outer)
        ccsem = outer.enter_context(nc.semaphore("cc_sem"))
        p.sems["cc"] = ccsem
        assert tt1 == CH
        mlv = [p.view(f"mlv{k}", mixed_loc[k].ap()) for k in range(NCH)]
        io1["mixed_tiles"] = mlv

        def exchange(k):
            p._waits("pool", [mlv[k]], [])
            p.q["pool"].append(lambda eng, k=k: eng.collective_compute(
                "AllGather", ALU.bypass, replica_groups=[[0, 1, 2, 3], [4, 5, 6, 7]],
                ins=[mixed_loc[k].ap().opt()], outs=[mixed_all[k].ap().opt()]).then_inc(ccsem))
        io1["exchange"] = exchange
        phase1(p, S, tt1, io1)
        p.barrier()
        gat = p.view("mixed_all", mixed_all[0].ap())
        gat.lw = ("cc", NCH)
        phase2(p, Tc, tt2, io2, gathered=gat,
               gathered_rows=lambda g, row: mixed_all[row // CH][g * CH + row % CH: g * CH + row % CH + 128, :])
    return nc


_DBG = {}


def kernel(**inputs):
    inp = {k: np.asarray(v) for k, v in inputs.items()}
    B, S, _ = inp["x"].shape
    assert B == 2
    Tc = S // 4
    tt1, tt2 = min(TT1, S), min(TT2, Tc)
    nc = build_fused(S, tt1, tt2)
    shared = p2_shared_inputs(inp)
    maps = []
    for c in range(8):
        b, r = c // 4, c % 4
        m = p1_host_inputs(inp, b, r, S, tt1)
        m.update(shared)
        m["x2"] = np.ascontiguousarray(inp["x"][b, r * Tc:(r + 1) * Tc])
        m["mem"] = np.ascontiguousarray(inp["mem"][b])
        sel = np.zeros((1, 4), np.float32)
        sel[0, r] = 1.0
        m["sel"] = sel
        maps.append(m)
    res = run_bass_kernel_spmd(nc, maps, core_ids=list(range(8)))
    out = np.empty((B, S, D), np.float32)
    for c in range(8):
        b, r = c // 4, c % 4
        out[b, r * Tc:(r + 1) * Tc] = np.asarray(res.results[c]["out"])
    return out
```

```python
import numpy as np
from contextlib import ExitStack
import concourse.bass as bass
import concourse.mybir as mybir
from concourse.bass_utils import run_bass_kernel_spmd

F32 = mybir.dt.float32
BF16 = mybir.dt.bfloat16
I32 = mybir.dt.int32
AF = mybir.ActivationFunctionType
ALU = mybir.AluOpType
AX = mybir.AxisListType

D = 4096
NK = D // 128
MEM = 256
DFF = 4 * D
EPS = 1e-6
SEM_LIMIT = 30000

ENGS = ("pe", "act", "dve", "pool", "sp")


class T:
    __slots__ = ("name", "t", "lw", "rd", "dsem", "dcnt", "excl")

    def __init__(self, name, t, excl=False):
        self.name, self.t, self.lw, self.rd, self.dsem, self.dcnt, self.excl = name, t, None, {}, None, 0, excl

    def __getitem__(self, idx):
        return self.t[idx]


class Prog:
    def __init__(self, nc, semstack):
        self.nc, self.semstack, self.stack = nc, semstack, semstack
        self.q = {e: [] for e in ENGS}
        self.seen = {e: {} for e in ENGS}
        self.sems, self.cnt, self.cur, self.epoch = {}, {}, {}, {}
        for e in ENGS:
            self.epoch[e] = 0
            self._newkey(e)
        self.ndsem = 0
        self.ntile = 0
        self.dtiles = []

    def _newkey(self, e):
        key = f"{e}#{self.epoch[e]}"
        self.epoch[e] += 1
        self.sems[key] = self.semstack.enter_context(self.nc.semaphore("s_" + key.replace("#", "_")))
        self.cnt[key] = 0
        self.cur[e] = key

    def sb(self, name, shape, dt):
        self.ntile += 1
        return T(name, self.stack.enter_context(self.nc.sbuf_tensor(f"{name}_{self.ntile}", list(shape), dt)))

    def ps(self, name, shape, dt=F32):
        self.ntile += 1
        return T(name, self.stack.enter_context(self.nc.psum_tensor(f"{name}_{self.ntile}", list(shape), dt)), excl=True)

    def view(self, name, ap):
        return T(name, ap)

    def _dsem(self, tile):
        if tile.dsem is None:
            self.ndsem += 1
            key = f"d{self.ndsem}"
            self.sems[key] = self.semstack.enter_context(self.nc.semaphore("s_" + key))
            tile.dsem = key
            self.dtiles.append(tile)
        return tile.dsem

    def _waits(self, e, reads, writes):
        need = {}
        for t in reads:
            if t.lw is not None:
                k, v = t.lw
                if need.get(k, 0) < v:
                    need[k] = v
        for t in writes:
            if t.lw is not None:
                k, v = t.lw
                if need.get(k, 0) < v:
                    need[k] = v
            for k, v in t.rd.items():
                if need.get(k, 0) < v:
                    need[k] = v
        seen = self.seen[e]
        for k, v in need.items():
            if e == "pe" and k.startswith("pe#"):
                continue
            if seen.get(k, 0) >= v:
                continue
            seen[k] = v
            sem = self.sems[k]
            self.q[e].append(lambda eng, sem=sem, v=v: eng.wait_ge(sem, v))

    def op(self, e, fn, reads=(), writes=()):
        xr = [t for t in reads if t.excl and t not in writes]
        if xr:
            writes = list(writes) + xr
        self._waits(e, reads, writes)
        if self.cnt[self.cur[e]] >= SEM_LIMIT:
            self._newkey(e)
        key = self.cur[e]
        self.cnt[key] += 1
        n = self.cnt[key]
        sem = self.sems[key]
        self.q[e].append(lambda eng, fn=fn, sem=sem: fn(eng).then_inc(sem, 1))
        for t in writes:
            t.lw = (key, n)
            t.rd = {}
        for t in reads:
            if t not in writes:
                if t.rd.get(key, 0) < n:
                    t.rd[key] = n

    def dma(self, e, out, in_, reads=(), writes=(), semtile=None):
        self._waits(e, reads, writes)
        st = semtile if semtile is not None else (writes[0] if writes else reads[0])
        key = self._dsem(st)
        st.dcnt += 16
        v = st.dcnt
        sem = self.sems[key]
        self.q[e].append(lambda eng, out=out, in_=in_, sem=sem: eng.dma_start(out=out, in_=in_).then_inc(sem, 16))
        for t in writes:
            t.lw = (key, v)
            t.rd = {}
        for t in reads:
            if t not in writes:
                if t.rd.get(key, 0) < v:
                    t.rd[key] = v

    def finish(self):
        for t in self.dtiles:
            sem, v = self.sems[t.dsem], t.dcnt
            self.q["sp"].append(lambda eng, sem=sem, v=v: eng.wait_ge(sem, v))

    def barrier(self):
        final = {k: v for k, v in self.cnt.items() if v > 0}
        for t in self.dtiles:
            if t.dcnt > 0:
                final[t.dsem] = t.dcnt
        for e in ENGS:
            seen = self.seen[e]
            for k, v in final.items():
                if seen.get(k, 0) >= v:
                    continue
                seen[k] = v
                sem = self.sems[k]
                self.q[e].append(lambda eng, sem=sem, v=v: eng.wait_ge(sem, v))

    def run(self):
        q = self.q
        self.q = {e: [] for e in ENGS}
        self._emit(q)

    def _emit(self, q):
        self = type("Q", (), {"q": q, "nc": self.nc})()
        with self.nc.Block() as block:
            @block.tensor
            def _(eng):
                for f in self.q["pe"]:
                    f(eng)

            @block.scalar
            def _(eng):
                for f in self.q["act"]:
                    f(eng)

            @block.vector
            def _(eng):
                for f in self.q["dve"]:
                    f(eng)

            @block.gpsimd
            def _(eng):
                for f in self.q["pool"]:
                    f(eng)

            @block.sync
            def _(eng):
                for f in self.q["sp"]:
                    f(eng)


def mm(p, ot, oap, lt, lap, rt, rap, start=True, stop=True):
    p.op("pe", lambda e: e.matmul(oap, lhsT=lap, rhs=rap, start=start, stop=stop), reads=[lt, rt], writes=[ot])


def tr(p, ot, oap, it, iap, ident):
    p.op("pe", lambda e: e.transpose(oap, iap, ident[:]), reads=[it, ident], writes=[ot])


def ts(p, eng, ot, oap, it, iap, s1, s2, op0, op1=None, extra=()):
    if op1 is None:
        p.op(eng, lambda e: e.tensor_scalar(out=oap, in0=iap, scalar1=s1, scalar2=None, op0=op0), reads=[it, *extra], writes=[ot])
    else:
        p.op(eng, lambda e: e.tensor_scalar(out=oap, in0=iap, scalar1=s1, scalar2=s2, op0=op0, op1=op1), reads=[it, *extra], writes=[ot])


def tt(p, eng, ot, oap, at, aap, bt, bap, op):
    p.op(eng, lambda e: e.tensor_tensor(out=oap, in0=aap, in1=bap, op=op), reads=[at, bt], writes=[ot])


def stt(p, eng, ot, oap, at, aap, scalar, bt, bap, op0, op1, extra=()):
    p.op(eng, lambda e: e.scalar_tensor_tensor(out=oap, in0=aap, scalar=scalar, in1=bap, op0=op0, op1=op1),
         reads=[at, bt, *extra], writes=[ot])


def act(p, ot, oap, it, iap, func, bias=None, scale=None, accum=None, accum_t=None, extra=()):
    kw = {}
    if bias is not None:
        kw["bias"] = bias
    if scale is not None:
        kw["scale"] = scale
    if accum is not None:
        kw["accum_out"] = accum
    w = [ot] + ([accum_t] if accum_t is not None else [])
    p.op("act", lambda e: e.activation(out=oap, in_=iap, func=func, **kw), reads=[it, *extra], writes=w)


def cp(p, eng, ot, oap, it, iap):
    if eng == "act":
        p.op("act", lambda e: e.copy(out=oap, in_=iap), reads=[it], writes=[ot])
    else:
        p.op(eng, lambda e: e.tensor_copy(out=oap, in_=iap), reads=[it], writes=[ot])


def make_ident(p, dt, name="ident"):
    ident = p.sb(name, [128, 128], dt)
    p.op("pool", lambda e: e.memset(ident[:], 1.0), writes=[ident])
    p.op("pool", lambda e: e.affine_select(out=ident[:], in_=ident[:], pattern=[[-1, 128]], compare_op=ALU.is_equal,
                                           fill=0.0, base=0, channel_multiplier=1), reads=[ident], writes=[ident])
    return ident


class WStream:
    def __init__(self, p, nbuf, name="wg"):
        self.p = p
        self.slots = [p.sb(f"{name}{i}", [128, 4096], BF16) for i in range(nbuf)]
        self.i = 0

    def load(self, src_ap):
        t = self.slots[self.i % len(self.slots)]
        self.i += 1
        self.p.dma("pool", t[:], src_ap, writes=[t])
        return t


class Ring:
    def __init__(self, tiles):
        self.tiles, self.i = tiles, 0

    def next(self):
        t = self.tiles[self.i % len(self.tiles)]
        self.i += 1
        return t


def rstd_from_ss(p, ot, oap, it, iap, inv_n):
    ts(p, "dve", ot, oap, it, iap, float(inv_n), float(EPS), ALU.mult, ALU.add)
    act(p, ot, oap, ot, oap, AF.Sqrt)
    p.op("dve", lambda e: e.reciprocal(out=oap, in_=oap), reads=[ot], writes=[ot])


def norm_transpose(p, src_t, src_ap, gT, dstT, dst_col0, tm, junk_ss, ident, trps):
    ss, rs = junk_ss
    act(p, tm, tm[:], src_t, src_ap, AF.Square, accum=ss[:, 0:1], accum_t=ss)
    rstd_from_ss(p, rs, rs[:, 0:1], ss, ss[:, 0:1], 1.0 / D)
    ts(p, "dve", tm, tm[:], src_t, src_ap, rs[:, 0:1], None, ALU.mult, extra=[rs])
    for kb in range(4):
        pt = trps.next()
        for kk in range(8):
            k = kb * 8 + kk
            tr(p, pt, pt[:, kk * 128:(kk + 1) * 128], tm, tm[:, k * 128:(k + 1) * 128], ident)
        p.op("dve", lambda e, pt=pt, kb=kb: e.tensor_tensor(
            out=dstT[:, kb * 8:(kb + 1) * 8, dst_col0:dst_col0 + 128],
            in0=pt[:].rearrange("p (k t) -> p k t", k=8),
            in1=gT[:, kb * 8:(kb + 1) * 8].unsqueeze(2).to_broadcast([128, 8, 128]), op=ALU.mult),
            reads=[pt, gT], writes=[dstT])


def phase1(p, S, TT, io):
    NT = S // TT
    NS = TT // 128
    with ExitStack() as st:
        p.stack = st
        identb = make_ident(p, BF16, "identb")
        identf = make_ident(p, F32, "identf")

        def cload(name, shape, dt, src):
            t = p.sb(name, shape, dt)
            p.dma("sp", t[:], src, writes=[t])
            return t
        gT = cload("gT", [128, NK], F32, io["mix_gT"])
        invf = cload("invf", [128, 1], F32, io["inv_freq"])
        rmaskT = cload("rmaskT", [128, 2, 128], F32, io["ret_maskT"])
        rdq = cload("rdq", [128, 2, 128], F32, io["ret_dq"])
        rdk = cload("rdk", [128, 2], F32, io["ret_dk"])
        rdc = cload("rdc", [128, 2], F32, io["ret_dc"])
        rgain = cload("rgain", [128, 512], F32, io["ret_gain"].partition_broadcast(128))
        ggain = cload("ggain", [128, 128], F32, io["gdn_gain"].partition_broadcast(128))
        convw = cload("convw", [128, 12, 4], F32, io["convw"])
        alog = cload("alog", [128, 4], F32, io["a_log"].partition_broadcast(128))
        dtb = cload("dtb", [128, 4], F32, io["dt_bias"].partition_broadcast(128))
        tri = cload("tri", [128, 128], F32, io["tri"])
        bones = cload("bones", [128, 128], F32, io["bones"])
        ones = cload("ones", [128, 128], F32, io["ones"])
        mstrict = cload("mstrict", [128, 128], F32, io["mstrict"])
        minclT = cload("minclT", [128, 128], F32, io["minclT"])
        wab = cload("wabf", [128, NK, 8], F32, io["w_ab"])
        wabb = p.sb("wabb", [128, NK, 8], BF16)
        cp(p, "dve", wabb, wabb[:], wab, wab[:])
        aneg = p.sb("aneg", [128, 4], F32)
        act(p, aneg, aneg[:], alog, alog[:], AF.Exp)
        ts(p, "dve", aneg, aneg[:], aneg, aneg[:], -1.0, None, ALU.mult)

        ws = WStream(p, 3)
        xt = p.sb("xt", [128, D], F32)
        tm = p.sb("tm", [128, D], BF16)
        ss = p.sb("ss", [128, 1], F32)
        rs = p.sb("rs", [128, 1], F32)
        hnT = p.sb("hnT", [128, NK, TT], BF16)
        pfr = p.sb("pfr", [128, 8, TT], BF16)
        pfg = p.sb("pfg", [128, 12, 3 + TT], BF16)
        p.op("dve", lambda e: e.memset(pfg[:, :, 0:3], 0.0), writes=[pfg])
        ptm = [p.sb(f"ptm{s}", [128, 1536], BF16) for s in range(NS)]
        pab = [p.sb(f"pab{s}", [128, 8], F32) for s in range(NS)]

        def ring(name, shape, dt, n):
            return Ring([p.sb(f"{name}{i}", shape, dt) for i in range(n)])
        posi = p.sb("posi", [128, TT], I32)
        r_f = ring("f128w", [128, 128], F32, 6)
        r_ki = ring("ki", [128, 128], I32, 1)
        r_sc = ring("sincos", [128, 2, 128], F32, 2)
        r_rq = ring("rq", [128, 2, 2, 128], BF16, 2)
        r_rqd = ring("rqd", [128, 2, 2, 128], BF16, 2)
        r_rk = ring("rk", [128, 2, 2, 128], BF16, 2)
        r_gq = ring("gq", [128, 4, 3, 128], BF16, 2)
        rstate = [[p.sb(f"rstate{h}{d}", [128, 256], F32) for d in range(2)] for h in range(2)]
        rstateb = [[p.sb(f"rstateb{h}{d}", [128, 256], BF16) for d in range(2)] for h in range(2)]
        gstate = [p.sb(f"gstate{j}", [128, 128], F32) for j in range(4)]
        gstateb = [p.sb(f"gstateb{j}", [128, 128], BF16) for j in range(4)]
        for h in range(2):
            for d in range(2):
                p.op("dve", lambda e, h=h, d=d: e.memset(rstate[h][d][:], 0.0), writes=[rstate[h][d]])
                p.op("dve", lambda e, h=h, d=d: e.memset(rstateb[h][d][:], 0.0), writes=[rstateb[h][d]])
        for j in range(4):
            p.op("dve", lambda e, j=j: e.memset(gstate[j][:], 0.0), writes=[gstate[j]])
            p.op("dve", lambda e, j=j: e.memset(gstateb[j][:], 0.0), writes=[gstateb[j]])
        mixo = ring("mixo", [128, 1024], BF16, 2)
        pacc = Ring([p.ps(f"pacc{i}", [128, 512], F32) for i in range(2)])
        ptr = Ring([p.ps(f"ptr{i}", [128, 1024], BF16) for i in range(2)])
        pmx = Ring([p.ps(f"pmx{i}", [128, 512], F32) for i in range(4)])
        r_osum = ring("osum", [128, 8], F32, 2)
        g_ab = ring("gab", [128, 40], F32, 2)
        rr = [dict(sT=ring(f"sT{h}", [128, 128], BF16, 2), ktm=ring(f"ktm{h}", [128, 256], BF16, 1),
                   of=ring(f"of{h}", [128, 256], F32, 1), sg=ring(f"sg{h}", [128, 256], F32, 1),
                   junk=ring(f"rjunk{h}", [128, 256], F32, 1), sm=ring(f"rsm{h}", [128, 8], F32, 2)) for h in range(2)]
        gr = [dict(raw=ring(f"raw{j}", [128, 384], BF16, 1), junk=ring(f"gjunk{j}", [128, 256], F32, 1),
                   sm=ring(f"gsm{j}", [128, 8], F32, 4), b=ring(f"gb{j}", [128, 128], BF16, 10),
                   fT=ring(f"fT{j}", [128, 384], BF16, 1), E=ring(f"gE{j}", [128, 128], F32, 5),
                   P=ring(f"gP{j}", [128, 128], F32, 7)) for j in range(4)]

        w_fm, w_tm = io["w_fm"], io["w_tm"]
        x, pos, mixed = io["x"], io["pos"], io["mixed"]
        mixed_rows = mixed if callable(mixed) else (lambda r0: mixed[r0:r0 + 128, :])

        def reduce_sin(dst_t, dst_ap, ang, shift):
            src = ang
            if shift != 0.0:
                sh = r_f.next()
                ts(p, "dve", sh, sh[:], ang, ang[:], float(shift), None, ALU.add)
                src = sh
            ki = r_ki.next()
            kf = r_f.next()
            t2 = r_f.next()
            ts(p, "dve", ki, ki[:], src, src[:], float(1 / (2 * np.pi)), None, ALU.mult)
            cp(p, "dve", kf, kf[:], ki, ki[:])
            stt(p, "dve", t2, t2[:], kf, kf[:], float(-2 * np.pi), src, src[:], ALU.mult, ALU.add)
            ts(p, "dve", kf, kf[:], t2, t2[:], float(np.pi), float(-2 * np.pi), ALU.is_gt, ALU.mult)
            tt(p, "dve", t2, t2[:], t2, t2[:], kf, kf[:], ALU.add)
            ts(p, "dve", kf, kf[:], t2, t2[:], float(-np.pi), float(2 * np.pi), ALU.is_lt, ALU.mult)
            tt(p, "dve", t2, t2[:], t2, t2[:], kf, kf[:], ALU.add)
            ts(p, "dve", t2, t2[:], t2, t2[:], float(-np.pi), float(np.pi), ALU.max, ALU.min)
            act(p, dst_t, dst_ap, t2, t2[:], AF.Sin)

        def norm_chain(tile, s):
            r0 = tile * TT + s * 128
            p.dma("sp", xt[:], x[r0:r0 + 128, :], writes=[xt])
            yield
            act(p, tm, tm[:], xt, xt[:], AF.Square, accum=ss[:, 0:1], accum_t=ss)
            yield
            act(p, rs, rs[:, 0:1], ss, ss[:, 0:1], AF.Ln, bias=float(EPS), scale=1.0 / D)
            act(p, rs, rs[:, 0:1], rs, rs[:, 0:1], AF.Exp, scale=-0.5)
            yield
            ts(p, "dve", tm, tm[:], xt, xt[:], rs[:, 0:1], None, ALU.mult, extra=[rs])
            yield
            for kb in range(4):
                pt_ = ptr.next()
                for kk in range(8):
                    k = kb * 8 + kk
                    tr(p, pt_, pt_[:, kk * 128:(kk + 1) * 128], tm, tm[:, k * 128:(k + 1) * 128], identb)
                p.op("dve", lambda e, pt_=pt_, kb=kb: e.tensor_tensor(
                    out=hnT[:, kb * 8:(kb + 1) * 8, s * 128:(s + 1) * 128],
                    in0=pt_[:].rearrange("p (k t) -> p k t", k=8),
                    in1=gT[:, kb * 8:(kb + 1) * 8].unsqueeze(2).to_broadcast([128, 8, 128]), op=ALU.mult),
                    reads=[pt_, gT], writes=[hnT])
                yield

        for ti in range(NT):
            tok0 = ti * TT
            if ti == 0:
                for s in range(NS):
                    for _ in norm_chain(0, s):
                        pass
            if ti > 0:
                p.op("dve", lambda e: e.tensor_copy(out=pfg[:, :, 0:3], in_=pfg[:, :, TT:TT + 3]), reads=[pfg], writes=[pfg])
            p.dma("sp", posi[:], pos[0:1, tok0:tok0 + TT].partition_broadcast(128), writes=[posi])
            for blk in range(20):
                wt = ws.load(w_fm[blk])
                wv = wt[:].rearrange("p (k c) -> p k c", k=NK)
                for c0 in range(0, TT, 512):
                    cw = min(512, TT - c0)
                    ps = pacc.next()
                    for k in range(NK):
                        mm(p, ps, ps[:, 0:cw], wt, wv[:, k, :], hnT, hnT[:, k, c0:c0 + cw], start=(k == 0), stop=(k == NK - 1))
                    if blk < 8:
                        cp(p, "act", pfr, pfr[:, blk, c0:c0 + cw], ps, ps[:, 0:cw])
                    else:
                        cp(p, "act", pfg, pfg[:, blk - 8, 3 + c0:3 + c0 + cw], ps, ps[:, 0:cw])
            for n in range(3):
                pss = [pmx.next() for _ in range(NS)]
                for kg in range(4):
                    wt = ws.load(w_tm[n * 4 + kg])
                    wv = wt[:].rearrange("p (k c) -> p k c", k=8)
                    for s in range(NS):
                        for k2 in range(8):
                            k = kg * 8 + k2
                            mm(p, pss[s], pss[s][:], hnT, hnT[:, k, s * 128:(s + 1) * 128], wt, wv[:, k2, :],
                               start=(k == 0), stop=(k == NK - 1))
                for s in range(NS):
                    cp(p, "act" if s % 2 == 0 else "dve", ptm[s], ptm[s][:, n * 512:(n + 1) * 512], pss[s], pss[s][:])
            for s in range(NS):
                ps = pacc.next()
                for k in range(NK):
                    mm(p, ps, ps[:, 0:8], hnT, hnT[:, k, s * 128:(s + 1) * 128], wabb, wabb[:, k, :],
                       start=(k == 0), stop=(k == NK - 1))
                cp(p, "act", pab[s], pab[s][:], ps, ps[:, 0:8])

            def prep_chain(s, out):
                c0 = s * 128
                ang = r_f.next()
                sc = r_sc.next()
                rq, rqd, rk = r_rq.next(), r_rqd.next(), r_rk.next()
                gq = r_gq.next()
                out.update(sc=sc, rq=rq, rqd=rqd, rk=rk, gq=gq)
                cp(p, "dve", ang, ang[:], posi, posi[:, c0:c0 + 128])
                ts(p, "dve", ang, ang[:], ang, ang[:], invf[:, 0:1], None, ALU.mult, extra=[invf])
                reduce_sin(sc, sc[:, 0, :], ang, 0.0)
                yield
                reduce_sin(sc, sc[:, 1, :], ang, np.pi / 2)
                yield
                sinT, cosT = sc[:, 0, :], sc[:, 1, :]
                for h in range(2):
                    for qk, dst in ((0, rq), (1, rk)):
                        x1 = pfr[:, h * 4 + qk * 2 + 0, c0:c0 + 128]
                        x2 = pfr[:, h * 4 + qk * 2 + 1, c0:c0 + 128]
                        t1, t2 = r_f.next(), r_f.next()
                        tt(p, "dve", t1, t1[:], pfr, x1, sc, cosT, ALU.mult)
                        tt(p, "dve", t2, t2[:], pfr, x2, sc, sinT, ALU.mult)
                        tt(p, "dve", dst, dst[:, h, 0, :], t1, t1[:], t2, t2[:], ALU.subtract)
                        t1, t2 = r_f.next(), r_f.next()
                        tt(p, "dve", t1, t1[:], pfr, x2, sc, cosT, ALU.mult)
                        tt(p, "dve", t2, t2[:], pfr, x1, sc, sinT, ALU.mult)
                        tt(p, "dve", dst, dst[:, h, 1, :], t1, t1[:], t2, t2[:], ALU.add)
                        yield
                    for half in range(2):
                        tt(p, "dve", rqd, rqd[:, h, half, :], rq, rq[:, h, half, :], rdq, rdq[:, h, :], ALU.mult)
                for j in range(4):
                    for c in range(3):
                        b = j * 3 + c
                        t1 = r_f.next()
                        ts(p, "dve", t1, t1[:], pfg, pfg[:, b, c0 + 3:c0 + 3 + 128], convw[:, b, 3:4], None, ALU.mult, extra=[convw])
                        for w in (2, 1, 0):
                            stt(p, "dve", t1, t1[:], pfg, pfg[:, b, c0 + w:c0 + w + 128], convw[:, b, w:w + 1], t1, t1[:],
                                ALU.mult, ALU.add, extra=[convw])
                        act(p, gq, gq[:, j, c, :], t1, t1[:], AF.Silu)
                        yield


            nxt = None
            for s in range(NS):
                c0 = s * 128
                mo = mixo.next()
                pt = ptm[s]
                if s == 0:
                    cur = {}
                    for _ in prep_chain(0, cur):
                        pass
                else:
                    cur = nxt
                sc, rq, rqd, rk, gq = cur["sc"], cur["rq"], cur["rqd"], cur["rk"], cur["gq"]
                def ret_head(h, mo=mo, pt=pt, rq=rq, rqd=rqd, rk=rk):
                    R = rr[h]
                    ptk = ptr.next()
                    for half in range(2):
                        tr(p, ptk, ptk[:, half * 128:(half + 1) * 128], rk, rk[:, h, half, :], identb)
                    ktm = R["ktm"].next()
                    ts(p, "dve", ktm, ktm[:], ptk, ptk[:, 0:256], rdk[:, h:h + 1], None, ALU.mult, extra=[rdk])
                    psc = pmx.next()
                    for half in range(2):
                        mm(p, psc, psc[:, 0:128], rk, rk[:, h, half, :], rq, rq[:, h, half, :], start=(half == 0), stop=(half == 1))
                    sT = R["sT"].next()
                    tt(p, "dve", sT, sT[:], psc, psc[:, 0:128], rmaskT, rmaskT[:, h, :], ALU.mult)
                    yield
                    vap = pt[:, h * 256:(h + 1) * 256]
                    po = pmx.next()
                    mm(p, po, po[:, 0:256], sT, sT[:], pt, vap, start=True, stop=False)
                    for half in range(2):
                        mm(p, po, po[:, 0:256], rqd, rqd[:, h, half, :], rstateb[h][half], rstateb[h][half][:],
                           start=False, stop=(half == 1))
                    of = R["of"].next()
                    cp(p, "act", of, of[:], po, po[:, 0:256])
                    yield
                    for half in range(2):
                        pst = pmx.next()
                        mm(p, pst, pst[:, 0:256], ktm, ktm[:, half * 128:(half + 1) * 128], pt, vap)
                        stt(p, "dve", rstate[h][half], rstate[h][half][:], rstate[h][half], rstate[h][half][:], rdc[:, h:h + 1],
                            pst, pst[:, 0:256], ALU.mult, ALU.add, extra=[rdc])
                        cp(p, "act", rstateb[h][half], rstateb[h][half][:], rstate[h][half], rstate[h][half][:])
                        yield
                    junk = R["junk"].next()
                    sq = R["sm"].next()
                    act(p, junk, junk[:], of, of[:], AF.Square, accum=sq[:, 0:1], accum_t=sq)
                    sg = R["sg"].next()
                    act(p, sg, sg[:], pt, pt[:, 512 + h * 256: 512 + (h + 1) * 256], AF.Silu)
                    yield
                    act(p, sq, sq[:, 1:2], sq, sq[:, 0:1], AF.Ln, bias=float(EPS), scale=1.0 / 256)
                    act(p, sq, sq[:, 1:2], sq, sq[:, 1:2], AF.Exp, scale=-0.5)
                    tt(p, "dve", sg, sg[:], sg, sg[:], rgain, rgain[:, h * 256:(h + 1) * 256], ALU.mult)
                    yield
                    stt(p, "dve", mo, mo[:, h * 256:(h + 1) * 256], of, of[:], sq[:, 1:2], sg, sg[:], ALU.mult, ALU.mult, extra=[sq])

                ab = g_ab.next()
                pb8 = pab[s]
                tt(p, "dve", ab, ab[:, 24:28], pb8, pb8[:, 0:4], dtb, dtb[:], ALU.add)
                stt(p, "dve", ab, ab[:, 0:4], ab, ab[:, 24:28], -1.0, ab, ab[:, 24:28], ALU.mult, ALU.max)
                act(p, ab, ab[:, 0:4], ab, ab[:, 0:4], AF.Exp, scale=-1.0)
                act(p, ab, ab[:, 0:4], ab, ab[:, 0:4], AF.Ln, bias=1.0)
                stt(p, "dve", ab, ab[:, 0:4], ab, ab[:, 24:28], 0.0, ab, ab[:, 0:4], ALU.max, ALU.add)
                tt(p, "dve", ab, ab[:, 0:4], ab, ab[:, 0:4], aneg, aneg[:], ALU.mult)
                act(p, ab, ab[:, 4:8], pb8, pb8[:, 4:8], AF.Sigmoid)
                ts(p, "dve", ab, ab[:, 28:32], ab, ab[:, 4:8], -1.0, None, ALU.mult)
                pg = pmx.next()
                mm(p, pg, pg[:, 0:4], tri, tri[:], ab, ab[:, 0:4])
                mm(p, pg, pg[:, 4:8], bones, bones[:], ab, ab[:, 0:4])
                cp(p, "dve", ab, ab[:, 8:16], pg, pg[:, 0:8])
                act(p, ab, ab[:, 16:20], ab, ab[:, 8:12], AF.Exp)
                tt(p, "dve", ab, ab[:, 24:28], ab, ab[:, 12:16], ab, ab[:, 8:12], ALU.subtract)
                act(p, ab, ab[:, 20:24], ab, ab[:, 24:28], AF.Exp)
                tt(p, "dve", ab, ab[:, 32:36], ab, ab[:, 4:8], ab, ab[:, 16:20], ALU.mult)
                osum = r_osum.next()

                def gdn_head(j, mo=mo, pt=pt, gq=gq, ab=ab, osum=osum):
                    G = gr[j]
                    pq = ptr.next()
                    for c in range(3):
                        tr(p, pq, pq[:, c * 128:(c + 1) * 128], gq, gq[:, j, c, :], identb)
                    raw = G["raw"].next()
                    cp(p, "act", raw, raw[:], pq, pq[:, 0:384])
                    yield
                    nq = G["sm"].next()
                    junk = G["junk"].next()
                    act(p, junk, junk[:, 0:128], raw, raw[:, 0:128], AF.Square, accum=nq[:, 0:1], accum_t=nq)
                    act(p, junk, junk[:, 128:256], raw, raw[:, 128:256], AF.Square, accum=nq[:, 1:2], accum_t=nq)
                    yield
                    act(p, nq, nq[:, 2:4], nq, nq[:, 0:2], AF.Ln, bias=float(EPS))
                    act(p, nq, nq[:, 2:4], nq, nq[:, 2:4], AF.Exp, scale=-0.5)
                    yield
                    ts(p, "dve", nq, nq[:, 4:5], nq, nq[:, 2:3], float(128 ** -0.5), None, ALU.mult)
                    tt(p, "dve", nq, nq[:, 5:6], nq, nq[:, 4:5], ab, ab[:, 16 + j:17 + j], ALU.mult)
                    tt(p, "dve", nq, nq[:, 6:7], nq, nq[:, 3:4], ab, ab[:, 32 + j:33 + j], ALU.mult)
                    tt(p, "dve", nq, nq[:, 7:8], nq, nq[:, 3:4], ab, ab[:, 20 + j:21 + j], ALU.mult)
                    qh, qd, kh, kbg, kdc, vbt = (G["b"].next() for _ in range(6))
                    ts(p, "dve", qh, qh[:], raw, raw[:, 0:128], nq[:, 4:5], None, ALU.mult, extra=[nq])
                    ts(p, "dve", qd, qd[:], raw, raw[:, 0:128], nq[:, 5:6], None, ALU.mult, extra=[nq])
                    ts(p, "dve", kh, kh[:], raw, raw[:, 128:256], nq[:, 3:4], None, ALU.mult, extra=[nq])
                    yield
                    ts(p, "dve", kbg, kbg[:], raw, raw[:, 128:256], nq[:, 6:7], None, ALU.mult, extra=[nq])
                    ts(p, "dve", kdc, kdc[:], raw, raw[:, 128:256], nq[:, 7:8], None, ALU.mult, extra=[nq])
                    ts(p, "dve", vbt, vbt[:], raw, raw[:, 256:384], ab[:, 4 + j:5 + j], None, ALU.mult, extra=[ab])
                    pb = ptr.next()
                    tr(p, pb, pb[:, 0:128], qh, qh[:], identb)
                    tr(p, pb, pb[:, 128:256], qd, qd[:], identb)
                    tr(p, pb, pb[:, 256:384], kh, kh[:], identb)
                    fT = G["fT"].next()
                    cp(p, "act", fT, fT[:], pb, pb[:, 0:384])
                    qhT, qdT, khT = fT[:, 0:128], fT[:, 128:256], fT[:, 256:384]
                    yield
                    trl = G["E"].next()
                    ts(p, "dve", trl, trl[:], tri, tri[:], ab[:, j:j + 1], None, ALU.mult, extra=[ab])
                    yield
                    pgb = pmx.next()
                    mm(p, pgb, pgb[:, 0:128], ones, ones[:], trl, trl[:])
                    E = G["E"].next()
                    ts(p, "dve", E, E[:], pgb, pgb[:, 0:128], ab[:, 8 + j:9 + j], 0.0, ALU.subtract, ALU.max, extra=[ab])
                    ET = G["E"].next()
                    ts(p, "dve", ET, ET[:], pgb, pgb[:, 0:128], ab[:, 8 + j:9 + j], 0.0, ALU.subtract, ALU.min, extra=[ab])
                    egl = G["sm"].next()
                    act(p, egl, egl[:, 0:1], pgb, pgb[:, 63:64], AF.Exp)
                    act(p, egl, egl[:, 1:2], pgb, pgb[:, 127:128], AF.Exp)
                    yield
                    act(p, E, E[:], E, E[:], AF.Exp, scale=-1.0)
                    act(p, ET, ET[:], ET, ET[:], AF.Exp)
                    yield
                    tt(p, "dve", E, E[:], E, E[:], mstrict, mstrict[:], ALU.mult)
                    tt(p, "dve", ET, ET[:], ET, ET[:], minclT, minclT[:], ALU.mult)
                    pqk = pmx.next()
                    mm(p, pqk, pqk[:, 0:128], fT, khT, fT, qhT)
                    qkT = G["b"].next()
                    tt(p, "dve", qkT, qkT[:], pqk, pqk[:, 0:128], ET, ET[:], ALU.mult)
                    yield
                    pkk = pmx.next()
                    mm(p, pkk, pkk[:, 0:128], fT, khT, fT, khT)
                    N = G["P"].next()
                    stt(p, "dve", N, N[:], pkk, pkk[:, 0:128], ab[:, 28 + j:29 + j], E, E[:], ALU.mult, ALU.mult, extra=[ab])
                    yield
                    pnt = pmx.next()
                    mm(p, pnt, pnt[:, 0:128], N, N[:], identf, identf[:])
                    NTt = G["P"].next()
                    cp(p, "act", NTt, NTt[:], pnt, pnt[:, 0:128])
                    yield
                    Q = G["P"].next()
                    tt(p, "dve", Q, Q[:], NTt, NTt[:], identf, identf[:], ALU.add)
                    P, PT = N, NTt
                    pend = None
                    for it in range(5):
                        pp = pmx.next()
                        mm(p, pp, pp[:, 0:128], PT, PT[:], P, P[:])
                        mm(p, pp, pp[:, 128:256], P, P[:], PT, PT[:])
                        P2, P2T = G["P"].next(), G["P"].next()
                        cp(p, "act", P2, P2[:], pp, pp[:, 0:128])
                        cp(p, "dve", P2T, P2T[:], pp, pp[:, 128:256])
                        if pend is not None:
                            pqq = pmx.next()
                            mm(p, pqq, pqq[:, 0:128], pend, pend[:], Q, Q[:])
                            Q2 = G["P"].next()
                            tt(p, "dve", Q2, Q2[:], pqq, pqq[:, 0:128], Q, Q[:], ALU.add)
                            Q = Q2
                        yield
                        pend = P2
                        P, PT = P2, P2T
                    pqq = pmx.next()
                    mm(p, pqq, pqq[:, 0:128], pend, pend[:], Q, Q[:])
                    Q2 = G["P"].next()
                    tt(p, "dve", Q2, Q2[:], pqq, pqq[:, 0:128], Q, Q[:], ALU.add)
                    Q = Q2
                    yield
                    TiT = G["b"].next()
                    cp(p, "act", TiT, TiT[:], Q, Q[:])
                    yield
                    pw = pmx.next()
                    mm(p, pw, pw[:, 0:128], kbg, kbg[:], TiT, TiT[:])
                    nwT = G["b"].next()
                    ts(p, "dve", nwT, nwT[:], pw, pw[:, 0:128], -1.0, None, ALU.mult)
                    yield
                    vn = G["b"].next()
                    oc = G["E"].next()
                    for cx in range(2):
                        r0, r1 = cx * 64, cx * 64 + 64
                        pv = pmx.next()
                        mm(p, pv, pv[:, 0:128], TiT, TiT[:], vbt, vbt[:], start=True, stop=False)
                        mm(p, pv, pv[:, 0:128], nwT, nwT[:], gstateb[j], gstateb[j][:], start=False, stop=True)
                        cp(p, "act", vn, vn[r0:r1, :], pv, pv[r0:r1, 0:128])
                        yield
                        po = pmx.next()
                        mm(p, po, po[:, 0:128], fT, qdT, gstateb[j], gstateb[j][:], start=True, stop=False)
                        mm(p, po, po[:, 0:128], qkT, qkT[r0:r1, :], vn, vn[r0:r1, :], start=False, stop=True)
                        cp(p, "act", oc, oc[r0:r1, :], po, po[r0:r1, 0:128])
                        pst = pmx.next()
                        mm(p, pst, pst[:, 0:128], kdc, kdc[r0:r1, :], vn, vn[r0:r1, :])
                        stt(p, "dve", gstate[j], gstate[j][:], gstate[j], gstate[j][:], egl[:, cx:cx + 1], pst, pst[:, 0:128],
                            ALU.mult, ALU.add, extra=[egl])
                        yield
                        cp(p, "act", gstateb[j], gstateb[j][:], gstate[j], gstate[j][:])
                        yield
                    junk2 = G["junk"].next()
                    act(p, junk2, junk2[:, 0:128], oc, oc[:], AF.Square, accum=osum[:, j:j + 1], accum_t=osum)
                    sgz = G["E"].next()
                    act(p, sgz, sgz[:], pt, pt[:, 1024 + j * 128: 1024 + (j + 1) * 128], AF.Silu)
                    yield
                    act(p, osum, osum[:, 4 + j:5 + j], osum, osum[:, j:j + 1], AF.Ln, bias=float(EPS), scale=1.0 / 128)
                    act(p, osum, osum[:, 4 + j:5 + j], osum, osum[:, 4 + j:5 + j], AF.Exp, scale=-0.5)
                    tt(p, "dve", sgz, sgz[:], sgz, sgz[:], ggain, ggain[:], ALU.mult)
                    yield
                    stt(p, "dve", mo, mo[:, 512 + j * 128: 512 + (j + 1) * 128], oc, oc[:], osum[:, 4 + j:5 + j], sgz, sgz[:],
                        ALU.mult, ALU.mult, extra=[osum])

                chains = [gdn_head(0), ret_head(0), gdn_head(1), gdn_head(2), ret_head(1), gdn_head(3)]
                if ti + 1 < NT:
                    chains.append(norm_chain(ti + 1, s))
                if s + 1 < NS:
                    nxt = {}
                    chains.append(prep_chain(s + 1, nxt))
                while chains:
                    for ch in list(chains):
                        try:
                            next(ch)
                        except StopIteration:
                            chains.remove(ch)
                if "mixed_tiles" in io:
                    p.dma("sp", mixed_rows(tok0 + c0), mo[:], reads=[mo], writes=[io["mixed_tiles"][ti]], semtile=mo)
                else:
                    p.dma("sp", mixed_rows(tok0 + c0), mo[:], reads=[mo])
            if "exchange" in io:
                io["exchange"](ti)
        p.finish()
        p.run()


OFF = dict(rq=0, rk=2048, rv=4096, rg=6144, gq=8192, gk=10240, gv=12288, gz=14336, ga=16384, gb=16400)


def _granules_ws(w, cols):
    sub = w[:, cols]
    return np.ascontiguousarray(sub.reshape(NK, 128, len(cols)).transpose(1, 0, 2)).reshape(128, NK * len(cols))


def p1_host_inputs(inp, b, g, S, TT):
    w = inp["w_in"][0]
    fm_cols = []
    for h in range(2):
        hr = 2 * g + h
        fm_cols += list(range(OFF["rq"] + hr * 256, OFF["rq"] + hr * 256 + 256))
        fm_cols += list(range(OFF["rk"] + hr * 256, OFF["rk"] + hr * 256 + 256))
    for j in range(4):
        hg = 4 * g + j
        for nm in ("gq", "gk", "gv"):
            fm_cols += list(range(OFF[nm] + hg * 128, OFF[nm] + hg * 128 + 128))
    tm_cols = []
    for nm, width, nh in (("rv", 256, 2), ("rg", 256, 2), ("gz", 128, 4)):
        for h in range(nh):
            hh = (2 * g + h) if nh == 2 else (4 * g + h)
            tm_cols += list(range(OFF[nm] + hh * width, OFF[nm] + hh * width + width))
    ab_cols = [OFF["ga"] + 4 * g + j for j in range(4)] + [OFF["gb"] + 4 * g + j for j in range(4)]
    w_fm = np.stack([_granules_ws(w, fm_cols[i * 128:(i + 1) * 128]) for i in range(20)])
    wt_ = w[:, tm_cols].reshape(4, 8, 128, 3, 512)
    w_tm = np.ascontiguousarray(wt_.transpose(3, 0, 2, 1, 4)).reshape(12, 128, 4096)
    w_ab = np.ascontiguousarray(w[:, ab_cols].reshape(NK, 128, 8).transpose(1, 0, 2))
    conv = inp["gdn_conv"][0]
    convw = np.zeros((128, 12, 4), np.float32)
    for j in range(4):
        hg = 4 * g + j
        for c in range(3):
            convw[:, j * 3 + c, :] = conv[:, c * 2048 + hg * 128: c * 2048 + hg * 128 + 128].T
    f32 = np.float32
    hs = np.array([2 * g, 2 * g + 1], f32)
    lg = np.log1p(-np.exp2(-5.0 - hs)).astype(f32)
    idx = np.arange(128, dtype=f32)
    rel = idx[None, :] - idx[:, None]
    maskT = np.where(rel >= 0, np.exp(np.maximum(rel, 0)[None] * lg[:, None, None]), 0.0).astype(f32)
    ret_maskT = np.ascontiguousarray((maskT * f32(1 / 16)).transpose(1, 0, 2))
    dq = np.exp((idx + 1.0)[None, :] * lg[:, None]).astype(f32)
    ret_dq = np.ascontiguousarray(np.broadcast_to(dq[None], (128, 2, 128))).astype(f32)
    dk = np.exp((127.0 - idx)[None, :] * lg[:, None]).astype(f32) * f32(1 / 16)
    ret_dk = np.ascontiguousarray(dk.T)
    ret_dc = np.ascontiguousarray(np.broadcast_to(np.exp(128.0 * lg)[None], (128, 2))).astype(f32)
    inv_freq = (10000.0 ** (-np.arange(128, dtype=f32) / f32(128))).astype(f32).reshape(128, 1)
    blk = (np.arange(128)[:, None] // 64) == (np.arange(128)[None, :] // 64)
    ii, jj = np.arange(128)[:, None], np.arange(128)[None, :]
    tri = (blk & (ii <= jj)).astype(f32)
    bones = blk.astype(f32)
    mstrict = (blk & (ii > jj)).astype(f32)
    minclT = (blk & (jj >= ii)).astype(f32)
    d = {
        "x": np.ascontiguousarray(inp["x"][b]),
        "pos": np.ascontiguousarray(inp["positions"][b][None, :]).astype(np.int32),
        "mix_gT": np.ascontiguousarray(inp["mix_norm"][0].reshape(NK, 128).T),
        "w_fm": w_fm, "w_tm": w_tm, "w_ab": w_ab, "convw": convw,
        "ret_gain": np.ascontiguousarray(inp["ret_norm"][0][2 * g:2 * g + 2].reshape(1, 512)),
        "gdn_gain": np.ascontiguousarray(inp["gdn_norm"][0].reshape(1, 128)),
        "a_log": np.ascontiguousarray(inp["gdn_a_log"][0][4 * g:4 * g + 4].reshape(1, 4)),
        "dt_bias": np.ascontiguousarray(inp["gdn_dt_bias"][0][4 * g:4 * g + 4].reshape(1, 4)),
        "inv_freq": inv_freq, "ret_maskT": ret_maskT, "ret_dq": ret_dq, "ret_dk": ret_dk, "ret_dc": ret_dc,
        "tri": tri, "bones": bones, "ones": np.ones((128, 128), f32), "mstrict": mstrict, "minclT": minclT,
    }
    return d


P1_SPECS = lambda S: {
    "x": ([S, D], F32), "pos": ([1, S], I32), "mix_gT": ([128, NK], F32),
    "w_fm": ([20, 128, 4096], F32), "w_tm": ([12, 128, 4096], F32), "w_ab": ([128, NK, 8], F32),
    "convw": ([128, 12, 4], F32), "ret_gain": ([1, 512], F32), "gdn_gain": ([1, 128], F32),
    "a_log": ([1, 4], F32), "dt_bias": ([1, 4], F32), "inv_freq": ([128, 1], F32),
    "ret_maskT": ([128, 2, 128], F32), "ret_dq": ([128, 2, 128], F32), "ret_dk": ([128, 2], F32),
    "ret_dc": ([128, 2], F32), "tri": ([128, 128], F32), "bones": ([128, 128], F32), "ones": ([128, 128], F32),
    "mstrict": ([128, 128], F32), "minclT": ([128, 128], F32),
}


def transpose_rows(p, tm, dstT, col0, ident, trps, gT=None, eng_alt=("dve", "act")):
    for kb in range(4):
        pt = trps.next()
        for kk in range(8):
            k = kb * 8 + kk
            tr(p, pt, pt[:, kk * 128:(kk + 1) * 128], tm, tm[:, k * 128:(k + 1) * 128], ident)
        if gT is not None:
            p.op("dve", lambda e, pt=pt, kb=kb: e.tensor_tensor(
                out=dstT[:, kb * 8:(kb + 1) * 8, col0:col0 + 128],
                in0=pt[:].rearrange("p (k t) -> p k t", k=8),
                in1=gT[:, kb * 8:(kb + 1) * 8].unsqueeze(2).to_broadcast([128, 8, 128]), op=ALU.mult),
                reads=[pt, gT], writes=[dstT])
        else:
            cp(p, eng_alt[kb % 2], dstT, dstT[:, kb * 8:(kb + 1) * 8, col0:col0 + 128], pt, pt[:].rearrange("p (k t) -> p k t", k=8))


def phase2(p, Tc, TT, io, gathered=None, gathered_rows=None):
    NT = Tc // TT
    NS = TT // 128
    with ExitStack() as st:
        p.stack = st
        identb = make_ident(p, BF16, "identb")

        def cload(name, shape, dt, src):
            t = p.sb(name, shape, dt)
            p.dma("sp", t[:], src, writes=[t])
            return t
        gx = cload("gx", [128, NK], F32, io["xattn_gT"])
        gm = cload("gm", [128, NK], F32, io["mem_gT"])
        gl = cload("gl", [128, NK], F32, io["mlp_gT"])

        ws = WStream(p, 3)
        h = p.sb("h", [128, NS, D], F32)
        hs = [p.view(f"h{s}", h.t[:, s, :]) for s in range(NS)]
        TB = max(TT, MEM)
        bufA = p.sb("bufA", [128, NK, TB], BF16)
        bufB = p.sb("bufB", [128, NK, TB], BF16)
        aTh = [p.view(f"aT{i}", bufB.t[:, i * 16:(i + 1) * 16, :]) for i in range(2)]
        tm = p.sb("tm", [128, D], BF16)
        ss = p.sb("ss", [128, 1], F32)
        rs = p.sb("rs", [128, 1], F32)
        gfin = p.sb("gfin", [128, 2048], F32)
        kTr = Ring([p.sb(f"kTh{i}", [128, 8, 256], BF16) for i in range(1)])
        vr = Ring([p.sb(f"vh{i}", [128, 2, 1024], BF16) for i in range(1)])
        if gathered is not None:
            sel = cload("sel", [128, 4], F32, io["sel"].partition_broadcast(128))
            cand = p.sb("cand", [128, 4, 1024], BF16)
            candv = [p.view(f"cand{q}", cand.t[:, q, :]) for q in range(4)]
        er = Ring([p.sb(f"e{i}", [128, 256], F32) for i in range(4)])
        pbr = Ring([p.sb(f"pb{i}", [128, 256], BF16) for i in range(4)])
        pTr = Ring([p.sb(f"pT{i}", [128, 2, TT], BF16) for i in range(2)])
        smr = Ring([p.sb(f"sm{i}", [128, 4], F32) for i in range(8)])
        rlr = Ring([p.sb(f"rl{i}", [128, 512], F32) for i in range(1)])
        vst = Ring([p.sb(f"vst{i}", [128, 512], BF16) for i in range(2)])
        pacc = Ring([p.ps(f"pacc{i}", [128, 512], F32) for i in range(6)])
        ptr = Ring([p.ps(f"ptr{i}", [128, 1024], BF16) for i in range(2)])

        x, mixed_in, mem, out = io["x"], io.get("mixed_in"), io["mem"], io["out"]
        kT_scr = p.view("kT_scr", io["kT_scr"])
        v_scr = p.view("v_scr", io["v_scr"])
        evi = [0]

        def evac_copy(ot, oap, ps, pap, scale=None):
            evi[0] += 1
            if evi[0] % 2:
                if scale is None:
                    cp(p, "act", ot, oap, ps, pap)
                else:
                    p.op("act", lambda e: e.mul(out=oap, in_=pap, mul=float(scale)), reads=[ps], writes=[ot])
            else:
                if scale is None:
                    cp(p, "dve", ot, oap, ps, pap)
                else:
                    ts(p, "dve", ot, oap, ps, pap, float(scale), None, ALU.mult)

        def rms_rows(src_t, src_ap):
            act(p, tm, tm[:], src_t, src_ap, AF.Square, accum=ss[:, 0:1], accum_t=ss)
            rstd_from_ss(p, rs, rs[:, 0:1], ss, ss[:, 0:1], 1.0 / D)
            ts(p, "dve", tm, tm[:], src_t, src_ap, rs[:, 0:1], None, ALU.mult, extra=[rs])

        def linear_as(actT, wsrc, n_kg, kpg, nblocks, sink):
            for n in range(nblocks):
                pss = [pacc.next() for _ in range(NS)]
                for kg in range(n_kg):
                    wt = ws.load(wsrc(n, kg))
                    wv = wt[:].rearrange("p (k c) -> p k c", k=kpg)
                    for s in range(NS):
                        for k2 in range(kpg):
                            k = kg * kpg + k2
                            mm(p, pss[s], pss[s][:], actT[0], actT[1](k, s), wt, wv[:, k2, :],
                               start=(k == 0), stop=(k == n_kg * kpg - 1))
                for s in range(NS):
                    sink(n, s, pss[s])

        def add_to_h(n, s, ps):
            tt(p, "dve", hs[s], hs[s][:, n * 512:(n + 1) * 512], ps, ps[:], hs[s], hs[s][:, n * 512:(n + 1) * 512], ALU.add)

        for mt in range(2):
            p.dma("sp", hs[0][:], mem[mt * 128:(mt + 1) * 128, :], writes=[hs[0]])
            rms_rows(hs[0], hs[0][:])
            transpose_rows(p, tm, bufA, mt * 128, identb, ptr, gT=gm)
        kst = bufB[:].rearrange("p k t -> p (k t)")[:, 0:NK * MEM].rearrange("p (k m) -> p k m", k=NK)
        for cb in range(NK):
            wt = ws.load(io["w_xk"][cb])
            wv = wt[:].rearrange("p (k c) -> p k c", k=NK)
            ps = pacc.next()
            for k in range(NK):
                mm(p, ps, ps[:, 0:256], wt, wv[:, k, :], bufA, bufA[:, k, 0:256], start=(k == 0), stop=(k == NK - 1))
            evac_copy(bufB, kst[:, cb, :], ps, ps[:, 0:256])
        p.dma("sp", kT_scr[:], kst, reads=[bufB], writes=[kT_scr])
        for n in range(8):
            pss = [pacc.next() for _ in range(2)]
            for kg in range(4):
                wt = ws.load(io["w_xv"][n * 4 + kg])
                wv = wt[:].rearrange("p (k c) -> p k c", k=8)
                for mt in range(2):
                    for k2 in range(8):
                        k = kg * 8 + k2
                        mm(p, pss[mt], pss[mt][:], bufA, bufA[:, k, mt * 128:(mt + 1) * 128], wt, wv[:, k2, :],
                           start=(k == 0), stop=(k == NK - 1))
            for mt in range(2):
                vs = vst.next()
                evac_copy(vs, vs[:], pss[mt], pss[mt][:])
                p.dma("sp", v_scr[:, mt, n * 512:(n + 1) * 512], vs[:], reads=[vs], writes=[v_scr])

        for ti in range(NT):
            tok0 = ti * TT
            for s in range(NS):
                r0 = tok0 + s * 128
                p.dma("sp", hs[s][:], x[r0:r0 + 128, :], writes=[hs[s]])
                for g in range(4):
                    if gathered is None:
                        p.dma("sp", tm[:, g * 1024:(g + 1) * 1024], mixed_in[g, r0:r0 + 128, :], writes=[tm])
                    else:
                        for q in range(4):
                            p.dma("sp", candv[q][:], gathered_rows(g, q * Tc + r0), reads=[gathered], writes=[candv[q]])
                        dst = tm[:, g * 1024:(g + 1) * 1024]
                        ts(p, "dve", tm, dst, candv[0], candv[0][:], sel[:, 0:1], None, ALU.mult, extra=[sel])
                        for q in range(1, 4):
                            stt(p, "dve", tm, dst, candv[q], candv[q][:], sel[:, q:q + 1], tm, dst, ALU.mult, ALU.add, extra=[sel])
                transpose_rows(p, tm, bufA, s * 128, identb, ptr)
            linear_as((bufA, lambda k, s: bufA[:, k, s * 128:(s + 1) * 128]), lambda n, kg: io["w_mo"][n * 4 + kg], 4, 8, 8, add_to_h)
            for s in range(NS):
                rms_rows(hs[s], hs[s][:])
                transpose_rows(p, tm, bufA, s * 128, identb, ptr, gT=gx)
            for cb in range(NK):
                wt = ws.load(io["w_xq"][cb])
                wv = wt[:].rearrange("p (k c) -> p k c", k=NK)
                ps = pacc.next()
                for k in range(NK):
                    mm(p, ps, ps[:, 0:TT], wt, wv[:, k, :], bufA, bufA[:, k, 0:TT], start=(k == 0), stop=(k == NK - 1))
                evac_copy(bufB, bufB[:, cb, 0:TT], ps, ps[:, 0:TT], scale=1.0 / 32.0)
            for hd in range(4):
                kTh, vh = kTr.next(), vr.next()
                p.dma("sp", kTh[:], kT_scr[:, hd * 8:(hd + 1) * 8, :], reads=[kT_scr], writes=[kTh])
                p.dma("sp", vh[:], v_scr[:, :, hd * 1024:(hd + 1) * 1024], reads=[v_scr], writes=[vh])
                pT = pTr.next()
                def soft_chain(s, hd=hd, kTh=kTh, pT=pT):
                    psc = pacc.next()
                    for c in range(8):
                        mm(p, psc, psc[:, 0:256], bufB, bufB[:, hd * 8 + c, s * 128:(s + 1) * 128], kTh, kTh[:, c, :],
                           start=(c == 0), stop=(c == 7))
                    sm = smr.next()
                    p.op("dve", lambda e, psc=psc, sm=sm: e.reduce_max(out=sm[:, 0:1], in_=psc[:, 0:256], axis=AX.X), reads=[psc], writes=[sm])
                    yield
                    ts(p, "dve", sm, sm[:, 1:2], sm, sm[:, 0:1], -1.0, None, ALU.mult)
                    ee = er.next()
                    act(p, ee, ee[:], psc, psc[:, 0:256], AF.Exp, bias=sm[:, 1:2], accum=sm[:, 2:3], accum_t=sm, extra=[sm])
                    yield
                    p.op("dve", lambda e, sm=sm: e.reciprocal(out=sm[:, 3:4], in_=sm[:, 2:3]), reads=[sm], writes=[sm])
                    pb = pbr.next()
                    ts(p, "dve", pb, pb[:], ee, ee[:], sm[:, 3:4], None, ALU.mult, extra=[sm])
                    yield
                    ptp = ptr.next()
                    for mt in range(2):
                        tr(p, ptp, ptp[:, mt * 128:(mt + 1) * 128], pb, pb[:, mt * 128:(mt + 1) * 128], identb)
                    cp(p, "act", pT, pT[:, :, s * 128:(s + 1) * 128], ptp, ptp[:, 0:256].rearrange("p (m t) -> p m t", m=2))
                chains = [soft_chain(s) for s in range(NS)]
                while chains:
                    for ch in list(chains):
                        try:
                            next(ch)
                        except StopIteration:
                            chains.remove(ch)
                for cb in range(8):
                    ps = pacc.next()
                    for mt in range(2):
                        mm(p, ps, ps[:, 0:TT], vh, vh[:, mt, cb * 128:(cb + 1) * 128], pT, pT[:, mt, :], start=(mt == 0), stop=(mt == 1))
                    evac_copy(bufA, bufA[:, hd * 8 + cb, 0:TT], ps, ps[:, 0:TT])
            linear_as((bufA, lambda k, s: bufA[:, k, s * 128:(s + 1) * 128]), lambda n, kg: io["w_xo"][n * 4 + kg], 4, 8, 8, add_to_h)
            for s in range(NS):
                rms_rows(hs[s], hs[s][:])
                transpose_rows(p, tm, bufA, s * 128, identb, ptr, gT=gl)

            def up_group(G):
                aT = aTh[G % 2]
                for cb in range(16):
                    wt = ws.load(io["w_up"][G * 16 + cb])
                    wv = wt[:].rearrange("p (k c) -> p k c", k=NK)
                    ps = pacc.next()
                    for k in range(NK):
                        mm(p, ps, ps[:, 0:TT], wt, wv[:, k, :], bufA, bufA[:, k, 0:TT], start=(k == 0), stop=(k == NK - 1))
                    rl = rlr.next()
                    act(p, rl, rl[:, 0:TT], ps, ps[:, 0:TT], AF.Relu)
                    tt(p, "dve", aT, aT[:, cb, 0:TT], rl, rl[:, 0:TT], rl, rl[:, 0:TT], ALU.mult)

            def down_group(G):
                aT = aTh[G % 2]
                linear_as((aT, lambda k, s: aT[:, k, s * 128:(s + 1) * 128]),
                          lambda n, kg: io["w_down"][(G * 8 + n) * 2 + kg], 2, 8, 8, add_to_h)
            NG = DFF // 2048
            up_group(0)
            for G in range(NG):
                if G + 1 < NG:
                    up_group(G + 1)
                down_group(G)
            for s in range(NS):
                r0 = tok0 + s * 128
                act(p, tm, tm[:], hs[s], hs[s][:], AF.Square, accum=ss[:, 0:1], accum_t=ss)
                rstd_from_ss(p, rs, rs[:, 0:1], ss, ss[:, 0:1], 1.0 / D)
                for hf in range(2):
                    p.dma("sp", gfin[:], io["final_gain"][0:1, hf * 2048:(hf + 1) * 2048].partition_broadcast(128), writes=[gfin])
                    stt(p, "dve", hs[s], hs[s][:, hf * 2048:(hf + 1) * 2048], hs[s], hs[s][:, hf * 2048:(hf + 1) * 2048], rs[:, 0:1],
                        gfin, gfin[:], ALU.mult, ALU.mult, extra=[rs])
                p.dma("sp", out[r0:r0 + 128, :], hs[s][:], reads=[hs[s]])
        p.finish()
        p.run()


def _gran_ws(w, nblk):
    K_, N_ = w.shape
    return np.ascontiguousarray(w.reshape(NK, 128, nblk, 128).transpose(2, 1, 0, 3)).reshape(nblk, 128, NK * 128)


def _gran_as(w):
    K_ = w.shape[0]
    nkg = K_ // 1024
    a = w.reshape(nkg, 8, 128, 8, 512)
    return np.ascontiguousarray(a.transpose(3, 0, 2, 1, 4)).reshape(8 * nkg, 128, 8 * 512)


def _gran_down(w):
    a = w.reshape(8, 2, 8, 128, 8, 512)
    return np.ascontiguousarray(a.transpose(0, 4, 1, 3, 2, 5)).reshape(128, 128, 8 * 512)


MIX_PERM = np.concatenate([np.concatenate([np.arange(g * 512, (g + 1) * 512), np.arange(2048 + g * 512, 2048 + (g + 1) * 512)])
                           for g in range(4)])


def p2_shared_inputs(inp):
    return {
        "xattn_gT": np.ascontiguousarray(inp["xattn_norm"][0].reshape(NK, 128).T),
        "mem_gT": np.ascontiguousarray(inp["mem_norm"][0].reshape(NK, 128).T),
        "mlp_gT": np.ascontiguousarray(inp["mlp_norm"][0].reshape(NK, 128).T),
        "final_gain": np.ascontiguousarray(inp["final_norm"].reshape(1, D)),
        "w_mo": _gran_as(inp["w_mix_out"][0][MIX_PERM]), "w_xq": _gran_ws(inp["w_xq"][0], 32), "w_xk": _gran_ws(inp["w_xk"][0], 32),
        "w_xv": _gran_as(inp["w_xv"][0]), "w_xo": _gran_as(inp["w_xo"][0]),
        "w_up": _gran_ws(inp["w_up"][0], 128), "w_down": _gran_down(inp["w_down"][0]),
    }


P2_SPECS = lambda Tc: {
    "x": ([Tc, D], F32), "mem": ([MEM, D], F32),
    "xattn_gT": ([128, NK], F32), "mem_gT": ([128, NK], F32), "mlp_gT": ([128, NK], F32), "final_gain": ([1, D], F32),
    "w_mo": ([32, 128, 4096], F32), "w_xq": ([32, 128, 4096], F32), "w_xk": ([32, 128, 4096], F32),
    "w_xv": ([32, 128, 4096], F32), "w_xo": ([32, 128, 4096], F32), "w_up": ([128, 128, 4096], F32),
    "w_down": ([128, 128, 4096], F32),
}


TT1 = 512
TT2 = 512
CH = 512


def build_p1(S, TT):
    nc = bass.Bass("TRN2", target_bir_lowering=False)
    io = {k: nc.dram_tensor(k, sh, dt, kind="ExternalInput").ap() for k, (sh, dt) in P1_SPECS(S).items()}
    io["mixed"] = nc.dram_tensor("mixed", [S, 1024], BF16, kind="ExternalOutput").ap()
    with ExitStack() as outer:
        phase1(Prog(nc, outer), S, TT, io)
    return nc


def build_p2(Tc, TT):
    nc = bass.Bass("TRN2", target_bir_lowering=False)
    io = {k: nc.dram_tensor(k, sh, dt, kind="ExternalInput").ap() for k, (sh, dt) in P2_SPECS(Tc).items()}
    io["mixed_in"] = nc.dram_tensor("mixed_in", [4, Tc, 1024], BF16, kind="ExternalInput").ap()
    io["out"] = nc.dram_tensor("out", [Tc, D], F32, kind="ExternalOutput").ap()
    io["kT_scr"] = nc.dram_tensor("kT_scr", [128, NK, 256], BF16).ap()
    io["v_scr"] = nc.dram_tensor("v_scr", [128, 2, D], BF16).ap()
    with ExitStack() as outer:
        phase2(Prog(nc, outer), Tc, TT, io)
    return nc


def build_fused(S, tt1, tt2):
    Tc = S // 4
    nc = bass.Bass("TRN2", target_bir_lowering=False)
    io1 = {k: nc.dram_tensor(k, sh, dt, kind="ExternalInput").ap() for k, (sh, dt) in P1_SPECS(S).items()}
    NCH = S // CH
    mixed_loc = [nc.dram_tensor(f"mixed_loc{k}", [CH, 1024], BF16) for k in range(NCH)]
    mixed_all = [nc.dram_tensor(f"mixed_all{k}", [4 * CH, 1024], BF16) for k in range(NCH)]
    io1["mixed"] = lambda r0: mixed_loc[r0 // CH][r0 % CH:r0 % CH + 128, :]
    io2 = {}
    for k, (sh, dt) in P2_SPECS(Tc).items():
        io2[k] = nc.dram_tensor("x2" if k == "x" else k, sh, dt, kind="ExternalInput").ap()
    io2["sel"] = nc.dram_tensor("sel", [1, 4], F32, kind="ExternalInput").ap()
    io2["out"] = nc.dram_tensor("out", [Tc, D], F32, kind="ExternalOutput").ap()
    io2["kT_scr"] = nc.dram_tensor("kT_scr", [128, NK, 256], BF16).ap()
    io2["v_scr"] = nc.dram_tensor("v_scr", [128, 2, D], BF16).ap()
    with ExitStack() as outer:
        p = Prog(nc, outer)
        ccsem = outer.enter_context(nc.semaphore("cc_sem"))
        p.sems["cc"] = ccsem
        assert tt1 == CH
        mlv = [p.view(f"mlv{k}", mixed_loc[k].ap()) for k in range(NCH)]
        io1["mixed_tiles"] = mlv

        def exchange(k):
            p._waits("pool", [mlv[k]], [])
            p.q["pool"].append(lambda eng, k=k: eng.collective_compute(
                "AllGather", ALU.bypass, replica_groups=[[0, 1, 2, 3], [4, 5, 6, 7]],
                ins=[mixed_loc[k].ap().opt()], outs=[mixed_all[k].ap().opt()]).then_inc(ccsem))
        io1["exchange"] = exchange
        phase1(p, S, tt1, io1)
        p.barrier()
        gat = p.view("mixed_all", mixed_all[0].ap())
        gat.lw = ("cc", NCH)
        phase2(p, Tc, tt2, io2, gathered=gat,
               gathered_rows=lambda g, row: mixed_all[row // CH][g * CH + row % CH: g * CH + row % CH + 128, :])
    return nc


_DBG = {}


def kernel(**inputs):
    inp = {k: np.asarray(v) for k, v in inputs.items()}
    B, S, _ = inp["x"].shape
    assert B == 2
    Tc = S // 4
    tt1, tt2 = min(TT1, S), min(TT2, Tc)
    nc = build_fused(S, tt1, tt2)
    shared = p2_shared_inputs(inp)
    maps = []
    for c in range(8):
        b, r = c // 4, c % 4
        m = p1_host_inputs(inp, b, r, S, tt1)
        m.update(shared)
        m["x2"] = np.ascontiguousarray(inp["x"][b, r * Tc:(r + 1) * Tc])
        m["mem"] = np.ascontiguousarray(inp["mem"][b])
        sel = np.zeros((1, 4), np.float32)
        sel[0, r] = 1.0
        m["sel"] = sel
        maps.append(m)
    res = run_bass_kernel_spmd(nc, maps, core_ids=list(range(8)))
    out = np.empty((B, S, D), np.float32)
    for c in range(8):
        b, r = c // 4, c % 4
        out[b, r * Tc:(r + 1) * Tc] = np.asarray(res.results[c]["out"])
    return out
```
